# Optimizing a Trainium2 kernel written in Bass

```python
import jax, jax.numpy as jnp
from jax import lax
import numpy as np

D_MODEL = 2048
BATCH = 8
SEQ = 4096
DEPTH = 4

GRID_W = 64
CTX_LEN = 256
N_MIXERS = 3
N_SSD_LAYERS = (DEPTH + 2) // 3
N_FOURIER_LAYERS = (DEPTH + 1) // 3
N_POOL_LAYERS = DEPTH // 3
FFN_HIDDEN = ((8 * D_MODEL // 3 + 255) // 256) * 256
SSD_EXPAND = 2
SSD_D_INNER = SSD_EXPAND * D_MODEL
SSD_HEAD_DIM = 64
SSD_HEADS = SSD_D_INNER // SSD_HEAD_DIM
SSD_GROUPS = 8
SSD_HEADS_PER_GROUP = SSD_HEADS // SSD_GROUPS
SSD_STATE = 128
SSD_CONV_W = 5
SSD_CHUNK = 128
SSD_CONV_DIM = SSD_D_INNER + 2 * SSD_GROUPS * SSD_STATE
SSD_IN_PROJ = SSD_D_INNER + SSD_CONV_DIM + 2 * SSD_HEADS
N_FOURIER_GROUPS = 4
POOL_WINDOWS = (2, 4, 8, 16)
POOL_GROUP_DIM = D_MODEL // len(POOL_WINDOWS)
POS_BASE = 10000.0
NORM_EPS = 1e-6
SSD_NORM_EPS = 1e-5

kernel_name = 'hybrid_ssd_fourier_pool_dit_trunk'


def rmsnorm(x, w, eps=NORM_EPS):
    x32 = x.astype(jnp.float32)
    y = x32 * lax.rsqrt(jnp.mean(x32 * x32, axis=-1, keepdims=True) + eps)
    return y.astype(x.dtype) * w


def ada_norm(x, w, shift, scale):
    return rmsnorm(x, w) * (1 + scale) + shift


def grid_pos_embed(n_tokens, dtype):
    rows = n_tokens // GRID_W
    t = jnp.arange(rows * GRID_W)
    r = (t // GRID_W).astype(jnp.float32)[:, None]
    col = (t % GRID_W).astype(jnp.float32)[:, None]
    quarter = D_MODEL // 4
    omega = 1.0 / (POS_BASE ** (jnp.arange(quarter, dtype=jnp.float32) / quarter))
    emb = jnp.concatenate([jnp.sin(r * omega), jnp.cos(r * omega), jnp.sin(col * omega), jnp.cos(col * omega)], axis=-1)
    return emb.astype(dtype)


def swiglu(h, w_gate, w_up, w_down):
    return (jax.nn.silu(h @ w_gate) * (h @ w_up)) @ w_down


def dwconv_centred(u, w, b):
    pad = SSD_CONV_W // 2
    out = lax.conv_general_dilated(u, w[:, None, :], window_strides=(1,), padding=[(pad, pad)],
                                   dimension_numbers=('NWC', 'WIO', 'NWC'), feature_group_count=u.shape[-1])
    return out + b


def ssd_scan(xs, dt, a, bm, cm, h0):
    b, l = xs.shape[:2]
    nc = l // SSD_CHUNK

    def chunks(t):
        return jnp.swapaxes(t.reshape(b, nc, SSD_CHUNK, *t.shape[2:]), 0, 1)

    log_a = dt.astype(jnp.float32) * a
    xdt = xs * dt[..., None]
    tri = jnp.tril(jnp.ones((SSD_CHUNK, SSD_CHUNK), dtype=bool))[None, :, :, None, None]

    def step(state, inp):
        xc, lac, bc, cc = inp
        cum = jnp.cumsum(lac, axis=1)
        decay = jnp.exp(jnp.where(tri, cum[:, :, None] - cum[:, None, :], -jnp.inf))
        scores = jnp.einsum('blgn,bsgn->blsg', cc, bc)
        y = jnp.einsum('blsg,blsge,bsgep->blgep', scores, decay, xc)
        y = y + jnp.einsum('blgn,bgepn->blgep', cc, state) * jnp.exp(cum)[..., None]
        last = cum[:, -1]
        w_in = jnp.exp(last[:, None] - cum)
        state = state * jnp.exp(last)[..., None, None] + jnp.einsum('bsgn,bsge,bsgep->bgepn', bc, w_in, xc)
        return state, y

    state, ys = lax.scan(step, h0, (chunks(xdt), chunks(log_a), chunks(bm), chunks(cm)))
    y = jnp.swapaxes(ys, 0, 1).reshape(xs.shape)
    return y.astype(xs.dtype), state


def ssd_mixer(h, hc, in_proj, conv_w, conv_b, dt_bias, a_log, d_skip, norm_w, out_proj, with_ctx):
    G, E, P, N = SSD_GROUPS, SSD_HEADS_PER_GROUP, SSD_HEAD_DIM, SSD_STATE

    def project(u):
        b, l = u.shape[:2]
        z, xbc, dt = jnp.split(u @ in_proj, [SSD_D_INNER, SSD_D_INNER + SSD_CONV_DIM], axis=-1)
        xbc = jax.nn.silu(dwconv_centred(xbc, conv_w, conv_b))
        xs, bm, cm = jnp.split(xbc, [SSD_D_INNER, SSD_D_INNER + G * N], axis=-1)
        return (z, xs.reshape(b, l, G, E, P), bm.reshape(b, l, G, N), cm.reshape(b, l, G, N),
                dt.reshape(b, l, 2, G, E))

    def gated_out(y, z):
        b, l = y.shape[:2]
        g = (y.reshape(b, l, G, -1) * jax.nn.silu(z).reshape(b, l, G, -1)).astype(jnp.float32)
        g = g * lax.rsqrt(jnp.mean(g * g, axis=-1, keepdims=True) + SSD_NORM_EPS)
        return (g.reshape(b, l, SSD_D_INNER).astype(y.dtype) * norm_w) @ out_proj

    def rev(t, d):
        return t[:, ::-1] if d == 1 else t

    z, xs, bm, cm, dt = project(h)
    zc, xsc, bmc, cmc, dtc = project(hc)
    h0 = jnp.zeros((h.shape[0], G, E, P, N), jnp.float32)
    y_lat, y_ctx = 0, 0
    for d in range(2):
        a = -jnp.exp(a_log[d].astype(jnp.float32)).reshape(G, E)
        bias = dt_bias[d].reshape(G, E)
        skip = d_skip[d].reshape(G, E)[..., None]
        dt_l = jax.nn.softplus(dt[:, :, d] + bias)
        dt_c = jax.nn.softplus(dtc[:, :, d] + bias)
        yc_d, ctx_state = ssd_scan(rev(xsc, d), rev(dt_c, d), a, rev(bmc, d), rev(cmc, d), h0)
        yl_d, _ = ssd_scan(rev(xs, d), rev(dt_l, d), a, rev(bm, d), rev(cm, d), ctx_state)
        y_lat = y_lat + rev(yl_d, d) + skip * xs
        if with_ctx:
            y_ctx = y_ctx + rev(yc_d, d) + skip * xsc
    out_ctx = gated_out(y_ctx, zc) if with_ctx else None
    return gated_out(y_lat, z), out_ctx


def fourier_mixer(h, w_out):
    b, l, d = h.shape
    hg = h.astype(jnp.float32).reshape(b, l, N_FOURIER_GROUPS, d // N_FOURIER_GROUPS)
    f = jnp.fft.fftn(hg, axes=(1, 3), norm='ortho').real
    return f.reshape(b, l, d).astype(h.dtype) @ w_out


def pool_mixer(h, w_grp, scale):
    b, l, d = h.shape
    csum = jnp.cumsum(h.astype(jnp.float32), axis=1)
    csum = jnp.concatenate([jnp.zeros((b, 1, d), jnp.float32), csum], axis=1)
    pos = jnp.arange(l)
    groups = []
    for g, win in enumerate(POOL_WINDOWS):
        lo = jnp.clip(pos - win // 2, 0, l)
        hi = jnp.clip(pos + (win - win // 2), 0, l)
        sl = slice(g * POOL_GROUP_DIM, (g + 1) * POOL_GROUP_DIM)
        cg = csum[..., sl]
        mean = (cg[:, hi] - cg[:, lo]) / (hi - lo).astype(jnp.float32)[:, None]
        groups.append(mean.astype(h.dtype) - h[..., sl])
    pooled = jnp.stack(groups, axis=2)
    out = jnp.einsum('blgi,gio->blgo', pooled, w_grp)
    return out.reshape(b, l, d) * scale


def setup_inputs(seed: int = 0) -> dict:
    key = jax.random.key(seed)
    ks = jax.random.split(key, 24)
    f32 = jnp.float32
    D = D_MODEL

    def nrm(k, shape, scale):
        return jax.random.normal(k, shape, f32) * scale

    dt0 = jnp.exp(jax.random.uniform(ks[12], (N_SSD_LAYERS, 2, SSD_HEADS), f32, np.log(1e-3), np.log(1e-1)))
    return {
        'x': nrm(ks[0], (BATCH, SEQ, D), 1.0),
        'c': nrm(ks[1], (BATCH, D), 1.0),
        'ctx': nrm(ks[2], (BATCH, CTX_LEN, D), 1.0),
        'c_ctx': nrm(ks[3], (D,), 1.0),
        'w_mod': nrm(ks[4], (DEPTH, D, 6 * D), 0.5 * D ** -0.5),
        'b_mod': nrm(ks[5], (DEPTH, 6 * D), 0.02),
        'norm_w': 1.0 + nrm(ks[6], (DEPTH, 2, D), 0.1),
        'ffn_w_gate': nrm(ks[7], (DEPTH, D, FFN_HIDDEN), D ** -0.5),
        'ffn_w_up': nrm(ks[8], (DEPTH, D, FFN_HIDDEN), D ** -0.5),
        'ffn_w_down': nrm(ks[9], (DEPTH, FFN_HIDDEN, D), FFN_HIDDEN ** -0.5),
        'ssd_in_proj': nrm(ks[10], (N_SSD_LAYERS, D, SSD_IN_PROJ), D ** -0.5),
        'ssd_conv_w': nrm(ks[11], (N_SSD_LAYERS, SSD_CONV_W, SSD_CONV_DIM), SSD_CONV_W ** -0.5),
        'ssd_conv_b': nrm(ks[13], (N_SSD_LAYERS, SSD_CONV_DIM), 0.02),
        'ssd_dt_bias': dt0 + jnp.log(-jnp.expm1(-dt0)),
        'ssd_a_log': jnp.log(jax.random.uniform(ks[14], (N_SSD_LAYERS, 2, SSD_HEADS), f32, 1.0, 16.0)),
        'ssd_d': 1.0 + nrm(ks[15], (N_SSD_LAYERS, 2, SSD_HEADS), 0.1),
        'ssd_norm_w': 1.0 + nrm(ks[16], (N_SSD_LAYERS, SSD_D_INNER), 0.1),
        'ssd_out_proj': nrm(ks[17], (N_SSD_LAYERS, SSD_D_INNER, D), SSD_D_INNER ** -0.5),
        'fourier_w_out': nrm(ks[18], (N_FOURIER_LAYERS, D, D), D ** -0.5),
        'pool_w': nrm(ks[19], (N_POOL_LAYERS, len(POOL_WINDOWS), POOL_GROUP_DIM, POOL_GROUP_DIM), POOL_GROUP_DIM ** -0.5),
        'pool_scale': 1.0 + nrm(ks[20], (N_POOL_LAYERS, D), 0.1),
        'final_norm_w': 1.0 + nrm(ks[21], (D,), 0.1),
    }


def reference(x, c, ctx, c_ctx, w_mod, b_mod, norm_w, ffn_w_gate, ffn_w_up, ffn_w_down,
              ssd_in_proj, ssd_conv_w, ssd_conv_b, ssd_dt_bias, ssd_a_log, ssd_d, ssd_norm_w, ssd_out_proj,
              fourier_w_out, pool_w, pool_scale, final_norm_w):
    x = x + grid_pos_embed(x.shape[1], x.dtype)[None]
    for i in range(DEPTH):
        kind, j = i % N_MIXERS, i // N_MIXERS
        last = i == DEPTH - 1
        need_ctx = (not last) or kind == 0
        mod = jax.nn.silu(c) @ w_mod[i] + b_mod[i]
        sh1, sc1, g1, sh2, sc2, g2 = jnp.split(mod[:, None, :], 6, axis=-1)
        h = ada_norm(x, norm_w[i, 0], sh1, sc1)
        if need_ctx:
            mod_c = jax.nn.silu(c_ctx) @ w_mod[i] + b_mod[i]
            ch1, cs1, cg1, ch2, cs2, cg2 = jnp.split(mod_c, 6, axis=-1)
            hc = ada_norm(ctx, norm_w[i, 0], ch1, cs1)
        if kind == 0:
            y, yc = ssd_mixer(h, hc, ssd_in_proj[j], ssd_conv_w[j], ssd_conv_b[j], ssd_dt_bias[j], ssd_a_log[j],
                              ssd_d[j], ssd_norm_w[j], ssd_out_proj[j], not last)
        elif kind == 1:
            y = fourier_mixer(h, fourier_w_out[j])
            yc = None if last else fourier_mixer(hc, fourier_w_out[j])
        else:
            y = pool_mixer(h, pool_w[j], pool_scale[j])
            yc = None if last else pool_mixer(hc, pool_w[j], pool_scale[j])
        x = x + g1 * y
        x = x + g2 * swiglu(ada_norm(x, norm_w[i, 1], sh2, sc2), ffn_w_gate[i], ffn_w_up[i], ffn_w_down[i])
        if not last:
            ctx = ctx + cg1 * yc
            ctx = ctx + cg2 * swiglu(ada_norm(ctx, norm_w[i, 1], ch2, cs2), ffn_w_gate[i], ffn_w_up[i], ffn_w_down[i])
    return rmsnorm(x, final_norm_w)
```

```python
import contextlib
import numpy as np
import ml_dtypes
import concourse.bass as bass
import concourse.mybir as mybir
from concourse.bass_utils import run_bass_kernel_spmd

F32 = mybir.dt.float32
BF16 = mybir.dt.bfloat16
ALU = mybir.AluOpType
AF = mybir.ActivationFunctionType
AX = mybir.AxisListType
NCORES = 8
POOL_WINDOWS = (2, 4, 8, 16)


class Cfg:
    def __init__(s, D=2048, L=4096, Lc=256, depth=4):
        s.D, s.L, s.Lc, s.T, s.depth = D, L, Lc, L + Lc, depth
        s.KD = D // 128
        s.DI = 2 * D
        s.NH = s.DI // 64
        s.G = 8
        s.E = s.NH // 8
        s.EP = s.E * 64
        s.CONV = s.DI + 2 * s.G * 128
        s.INP = s.DI + s.CONV + 2 * s.NH
        s.FH = ((8 * D // 3 + 255) // 256) * 256
        s.KH = s.FH // 128
        s.GD = D // 4
        s.GK = s.GD // 128
        s.MC = 6 * D // 8
        s.CB = s.MC // 3
        s.NSSD = (depth + 2) // 3
        s.NFOU = (depth + 1) // 3
        s.NPOOL = depth // 3
        s.NCH = s.T // 128
        s.blocks = [(0, Lc)] + [(Lc + 512 * i, 512) for i in range(L // 512)]


class Res:
    __slots__ = ("w", "readers", "wprev")

    def __init__(self):
        self.w = None
        self.readers = []
        self.wprev = []


class FW:
    NDMA = 24

    def __init__(self, nc):
        self.nc = nc
        self.es = contextlib.ExitStack()
        self.eng = {"pe": nc.tensor, "act": nc.scalar, "dve": nc.vector, "pool": nc.gpsimd, "sp": nc.sync}
        self.sem, self.cnt = {}, {}
        for e in ("pe", "act", "dve", "pool"):
            self.sem[e] = self.es.enter_context(nc.semaphore("s_" + e))
            self.cnt[e] = 0
        self.dsem, self.dcnt, self.dnext = {}, {}, {}
        for q in ("sp", "pool"):
            self.dsem[q] = [self.es.enter_context(nc.semaphore(f"d_{q}{i}")) for i in range(self.NDMA)]
            self.dcnt[q] = [0] * self.NDMA
            self.dnext[q] = 0
        self.known = {}
        self.uid = 0
        self.cc_toks = []

    def sb(self, stack, shape, dtype, name="t"):
        self.uid += 1
        return stack.enter_context(self.nc.sbuf_tensor(f"{name}_{self.uid}", list(shape), dtype))

    def ps(self, stack, shape, dtype, name="p"):
        self.uid += 1
        return stack.enter_context(self.nc.psum_tensor(f"{name}_{self.uid}", list(shape), dtype))

    def _wait(self, e, tok):
        if tok is None:
            return
        sem, val = tok
        key = (e, id(sem))
        if self.known.get(key, 0) >= val:
            return
        self.known[key] = val
        self.eng[e].wait_ge(sem, val)

    def _deps(self, e, reads, writes, extra=()):
        for t in extra:
            self._wait(e, t)
        for r in reads:
            self._wait(e, r.w)
            for t in r.wprev:
                self._wait(e, t)
        for w in writes:
            self._wait(e, w.w)
            for t in w.wprev:
                self._wait(e, t)
            for t in w.readers:
                self._wait(e, t)

    def _post(self, tok, reads, writes):
        for r in reads:
            r.readers.append(tok)
        for w in writes:
            if w.w is not None and w.w[0] is not tok[0]:
                w.wprev.append(w.w)
                if len(w.wprev) > 64:
                    w.wprev = w.wprev[-64:]
            w.w = tok
            w.readers = []

    def _sig(self, e, ins, reads, writes):
        self.cnt[e] += 1
        ins.then_inc(self.sem[e], 1)
        tok = (self.sem[e], self.cnt[e])
        self._post(tok, reads, writes)
        return tok

    def op(self, e, fn, reads=(), writes=()):
        self._deps(e, reads, writes)
        return self._sig(e, fn(self.eng[e]), reads, writes)

    def mm(self, out_ap, pairs, reads=(), writes=(), start=True, stop=True):
        self._deps("pe", reads, writes)
        n = len(pairs)
        ins = None
        for i, (l, r) in enumerate(pairs):
            ins = self.nc.tensor.matmul(out_ap, l, r, start=(start and i == 0), stop=(stop and i == n - 1))
        return self._sig("pe", ins, reads, writes)

    def mms(self, groups, reads=(), writes=()):
        self._deps("pe", reads, writes)
        ins = None
        for out_ap, pairs in groups:
            n = len(pairs)
            for i, (l, r) in enumerate(pairs):
                ins = self.nc.tensor.matmul(out_ap, l, r, start=(i == 0), stop=(i == n - 1))
        return self._sig("pe", ins, reads, writes)

    def trs(self, items, ident, reads=(), writes=()):
        self._deps("pe", reads, writes)
        ins = None
        for o, i in items:
            ins = self.nc.tensor.transpose(o, i, ident)
        return self._sig("pe", ins, reads, writes)

    def dma(self, q, out, in_, reads=(), writes=(), extra=()):
        try:
            so, si = tuple(out.shape), tuple(in_.shape)
        except Exception:
            so, si = None, None
        if so is not None and so == si and len(so) >= 3:
            ndesc = so[0] * int(np.prod(so[1:-1]))
            if ndesc > 1024 and so[1] > 1:
                per = max(1, so[1] * 1024 // ndesc)
                tok = None
                for a in range(0, so[1], per):
                    b = min(so[1], a + per)
                    tok = self._dma1(q, out[:, a:b], in_[:, a:b], reads, writes, extra)
                return tok
        return self._dma1(q, out, in_, reads, writes, extra)

    def _dma1(self, q, out, in_, reads=(), writes=(), extra=()):
        i = self.dnext[q]
        self.dnext[q] = (i + 1) % self.NDMA
        sem = self.dsem[q][i]
        if self.dcnt[q][i] > 0:
            self._wait(q, (sem, self.dcnt[q][i]))
        self._deps(q, reads, writes, extra)
        ins = self.eng[q].dma_start(out=out, in_=in_)
        self.dcnt[q][i] += 16
        ins.then_inc(sem, 16)
        tok = (sem, self.dcnt[q][i])
        self._post(tok, reads, writes)
        return tok

    def barrier(self):
        toks = [(self.sem[e], self.cnt[e]) for e in self.sem if self.cnt[e] > 0]
        for q in self.dsem:
            for i, s in enumerate(self.dsem[q]):
                if self.dcnt[q][i] > 0:
                    toks.append((s, self.dcnt[q][i]))
        for e in ("pe", "act", "dve", "pool", "sp"):
            for t in toks:
                self._wait(e, t)


def bc(ap, steps):
    return bass.AP(ap.tensor, ap.offset, [list(ap.ap[0])] + [list(x) for x in steps])


def dap(t, offset, dims):
    a = t if isinstance(t, bass.AP) else t.ap()
    return bass.AP(a.tensor, a.offset + offset, [list(x) for x in dims])


def build(cfg):
    c = cfg
    D, L, Lc, T, KD, DI, NH, G, E, EP = c.D, c.L, c.Lc, c.T, c.KD, c.DI, c.NH, c.G, c.E, c.EP
    CONV, INP, FH, KH, GD, GK, MC, CB, depth = c.CONV, c.INP, c.FH, c.KH, c.GD, c.GK, c.MC, c.CB, c.depth
    NCH = c.NCH
    nc = bass.Bass("TRN2", target_bir_lowering=False)
    f = FW(nc)

    def din(name, shape, dt=F32):
        return nc.dram_tensor(name, list(shape), dt, kind="ExternalInput").ap()

    x_in = din("x", [L, D])
    ctx_in = din("ctx", [Lc, D])
    c2_in = din("c2", [2, D])
    ident_in = din("ident", [128, 128])
    pos_in = din("posT", [D // 2, 64])
    tri_in = din("tri", [128, 4, 128])
    wmod_in = din("wmod", [depth, D, 6 * D])
    bmod_in = din("bmod", [depth, 6 * D])
    normw_in = din("normw", [depth * 2, D])
    fnw_in = din("fnw", [1, D])
    ffg_in = din("ffg", [depth, D, FH])
    ffu_in = din("ffu", [depth, D, FH])
    ffd_in = din("ffd", [depth, FH, D])
    inp_in = din("inproj", [c.NSSD, D, INP])
    convw_in = din("convw", [c.NSSD, 5, CONV])
    convb_in = din("convb", [c.NSSD, 1, CONV])
    dtb_in = din("dtb", [c.NSSD, 2 * NH, 1])
    alog_in = din("alog", [c.NSSD, 2 * NH, 1])
    dsk_in = din("dskip", [c.NSSD, 2, NH])
    ssdnw_in = din("ssdnw", [c.NSSD, 1, DI])
    outp_in = din("outproj", [c.NSSD, DI, D])
    fou_in = din("fouw", [max(c.NFOU, 1), D, D])
    poolw_in = din("poolw", [max(c.NPOOL, 1), 4 * GD, GD])
    pscale_in = din("pscale", [max(c.NPOOL, 1), 1, D])
    dftc_in = din("dftc", [2, GD, GD], BF16)
    dftL_in = din("dftL", [2, L, L], BF16)
    dftLc_in = din("dftLc", [2, Lc, Lc], BF16)
    out_ext = nc.dram_tensor("out", [L, D], F32, kind="ExternalOutput").ap()

    def dscr(name, shape, dt):
        return nc.dram_tensor(name, list(shape), dt)

    _shared = {}

    def dscr_shared(name, shape, dt):
        if name not in _shared:
            _shared[name] = nc.dram_tensor(name, list(shape), dt)
        return _shared[name]

    XT = dscr("XT", [D, T], F32).ap()
    S_big = dscr("S_big", [max(DI, D), T], BF16).ap()

    top = f.es
    ident_f = f.sb(top, [128, 128], F32, "identf")
    ident_b = f.sb(top, [128, 128], BF16, "identb")
    ones_b = f.sb(top, [128, 128], BF16, "onesb")
    tri_f = f.sb(top, [128, 4, 128], F32, "tri")
    modT = f.sb(top, [128, depth, 6 * KD, 2], F32, "modT")
    A12 = f.sb(top, [128, depth, 2, KD, 2], F32, "A12")
    normT = f.sb(top, [128, KD, depth * 2], F32, "normT")
    fnwT = f.sb(top, [128, KD, 1], F32, "fnwT")
    rconst = Res()
    banks = [f.ps(top, [128, 512], F32, f"bank{i}") for i in range(8)]
    rbank = [Res() for _ in range(8)]

    f.dma("sp", ident_f[:], ident_in[:, :], writes=[rconst])
    f.dma("sp", tri_f[:], tri_in[:, :, :], writes=[rconst])
    f.op("dve", lambda e: e.tensor_copy(out=ident_b[:], in_=ident_f[:]), reads=[rconst], writes=[rconst])
    f.op("dve", lambda e: e.memset(ones_b[:], 1.0), writes=[rconst])

    gathered = {}

    def prep_weight(key, src, rows, cols):
        full = dscr(f"wg_{key}", [rows, cols], BF16)
        toks = []
        step = 128
        for r0 in range(0, rows, step):
            r1 = min(rows, r0 + step)
            toks.append(f.dma("pool", full.ap()[r0:r1, :], src[r0:r1, :]))
        gathered[key] = (full.ap(), toks)

    def prep_layer_weights(i):
        kind, j = i % 3, i // 3
        if kind == 0:
            prep_weight(f"inp{j}", inp_in[j], D, INP)
            prep_weight(f"outp{j}", outp_in[j], DI, D)
        elif kind == 1:
            gathered["dftL0"] = (dftL_in[0], [])
            gathered["dftL1"] = (dftL_in[1], [])
            prep_weight(f"fou{j}", fou_in[j], D, D)
        else:
            prep_weight(f"pool{j}", poolw_in[j], 4 * GD, GD)
        prep_weight(f"ffg{i}", ffg_in[i], D, FH)
        prep_weight(f"ffu{i}", ffu_in[i], D, FH)
        prep_weight(f"ffd{i}", ffd_in[i], FH, D)

    prep_layer_weights(0)

    def load_T(rows_ap, n, W, out_tile_ap, rows_tile=None, rows_res=None):
        with contextlib.ExitStack() as st:
            if rows_tile is None:
                rows = f.sb(st, [n, W], F32, "rows")
                rr = Res()
                f.dma("sp", rows[:], rows_ap, writes=[rr])
            else:
                rows, rr = rows_tile, rows_res
            nk = W // 128
            per = 512 // n
            for k0 in range(0, nk, per):
                k1 = min(nk, k0 + per)
                b = 0
                groups = []
                for k in range(k0, k1):
                    groups.append((banks[b][:, (k - k0) * n:(k - k0 + 1) * n],
                                   [(rows[:, k * 128:(k + 1) * 128], ident_f[0:n, 0:n])]))
                f.mms(groups, reads=[rr, rconst], writes=[rbank[b]])
                src = banks[b][:, 0:(k1 - k0) * n].rearrange("p (k n) -> p k n", n=n)
                f.op("dve", lambda e: e.tensor_copy(out=out_tile_ap[:, k0:k1, :], in_=src), reads=[rbank[b]], writes=[rconst])
            f.barrier()

    with contextlib.ExitStack() as st:
        scT = f.sb(st, [128, KD, 2], F32, "scT")
        rsc = Res()
        load_T(c2_in[:, :], 2, D, scT)
        f.op("act", lambda e: e.activation(out=scT[:], in_=scT[:], func=AF.Silu), reads=[rconst], writes=[rsc])
        rowsM = f.sb(st, [2, 6 * D], F32, "rowsM")
        bm = f.sb(st, [2, 6 * D], F32, "bm")
        rrows = Res()
        rbm = Res()
        wm = [f.sb(st, [128, KD, 512], F32, "wm") for _ in range(3)]
        rwm = [Res(), Res(), Res()]
        MCB = min(512, 6 * D)
        it = 0
        for i in range(depth):
            f.dma("sp", bm[:], dap(bmod_in, i * 6 * D, [(0, 2), (1, 6 * D)]), writes=[rbm])
            for cb in range(6 * D // MCB):
                s = it % 3
                it += 1
                f.dma("sp", wm[s][:, :, 0:MCB], dap(wmod_in, i * D * 6 * D + cb * MCB, [(6 * D, 128), (128 * 6 * D, KD), (1, MCB)]), writes=[rwm[s]])
                b = 1 + s
                f.mm(banks[b][0:2, 0:MCB], [(scT[:, k, :], wm[s][:, k, 0:MCB]) for k in range(KD)],
                     reads=[rsc, rwm[s]], writes=[rbank[b]])
                f.op("dve", lambda e: e.tensor_tensor(out=rowsM[:, cb * MCB:(cb + 1) * MCB], in0=banks[b][0:2, 0:MCB],
                                                      in1=bm[:, cb * MCB:(cb + 1) * MCB], op=ALU.add),
                     reads=[rbank[b], rbm], writes=[rrows])
            load_T(None, 2, 6 * D, modT[:, i, :, :], rows_tile=rowsM, rows_res=rrows)
        f.barrier()
    load_T(normw_in[:, :], depth * 2, D, normT)
    load_T(fnw_in[:, :], 1, D, fnwT)
    for i in range(depth):
        for j in range(2):
            sc = modT[:, i, (3 * j + 1) * KD:(3 * j + 2) * KD, :]
            nw = bc(normT[:, :, 2 * i + j], [(depth * 2, KD), (0, 2)])
            f.op("dve", lambda e: e.scalar_tensor_tensor(out=A12[:, i, j, :, :], in0=sc, scalar=1.0, in1=nw,
                                                         op0=ALU.add, op1=ALU.mult), reads=[rconst], writes=[rconst])
    f.barrier()

    def MOD(i, s, k, v):
        return modT[:, i, s * KD + k, v:v + 1]

    def vsplit(t0, n):
        return [(0, n, 1 if t0 < Lc else 0)]

    with contextlib.ExitStack() as st:
        posT = f.sb(st, [128, KD // 2, 64], F32, "posT")
        rpos = Res()
        f.dma("sp", posT[:], dap(pos_in, 0, [(64, 128), (128 * 64, KD // 2), (1, 64)]), writes=[rpos])
        xin = [f.sb(st, [128, 4, D], F32, "xin") for _ in range(2)]
        rxin = [Res(), Res()]
        xo = [f.sb(st, [128, KD, 512], F32, "xo") for _ in range(2)]
        rxo = [Res(), Res()]
        for bi, (t0, nb) in enumerate(c.blocks):
            s = bi % 2
            nt = nb // 128
            if t0 < Lc:
                src = ctx_in[t0:t0 + nb, :]
            else:
                src = x_in[t0 - Lc:t0 - Lc + nb, :]
            f.dma("sp", xin[s][:, 0:nt, :], src.rearrange("(a p) d -> p a d", p=128), writes=[rxin[s]])
            for k in range(KD):
                b = k % 8
                f.trs([(banks[b][:, a * 128:(a + 1) * 128], xin[s][:, a, k * 128:(k + 1) * 128]) for a in range(nt)],
                      ident_f[:], reads=[rxin[s], rconst], writes=[rbank[b]])
                if t0 < Lc:
                    f.op("act", lambda e: e.copy(out=xo[s][:, k, 0:nb], in_=banks[b][:, 0:nb]), reads=[rbank[b]], writes=[rxo[s]])
                else:
                    r0 = (t0 - Lc) // 64
                    nr = nb // 64
                    if k < KD // 2:
                        in1 = bc(posT[:, k, r0:r0 + nr], [(1, nr), (0, 64)])
                    else:
                        in1 = bc(posT[:, k - KD // 2, 0:64], [(0, nr), (1, 64)])
                    f.op("dve", lambda e: e.tensor_tensor(out=xo[s][:, k, 0:nb].rearrange("p (r c) -> p r c", c=64),
                                                          in0=banks[b][:, 0:nb].rearrange("p (r c) -> p r c", c=64),
                                                          in1=in1, op=ALU.add), reads=[rbank[b], rpos], writes=[rxo[s]])
            f.dma("pool", dap(XT, t0, [(T, 128), (128 * T, KD), (1, nb)]), xo[s][:, :, 0:nb], reads=[rxo[s]])
        f.barrier()

    def load_norm_block(st_tiles, i, j, t0, nb, want_x=True):
        xt, rxt, hn, rhn, sq, rsq, rstd, rrstd = st_tiles[:8]
        v = 1 if t0 < Lc else 0
        f.dma("sp", xt[:, :, 0:nb], dap(XT, t0, [(T, 128), (128 * T, KD), (1, nb)]), writes=[rxt])
        for k in range(KD):
            f.op("act", lambda e: e.activation(out=sq[:, k % 2, 0:nb], in_=xt[:, k, 0:nb], func=AF.Square),
                 reads=[rxt], writes=[rsq[k % 2]])
            f.mm(banks[0][:, 0:nb], [(ones_b[:], sq[:, k % 2, 0:nb])], reads=[rsq[k % 2], rconst],
                 writes=[rbank[0]], start=(k == 0), stop=(k == KD - 1))
        eps = 1e-6
        f.op("act", lambda e: e.activation(out=rstd[:, 0:nb], in_=banks[0][:, 0:nb], func=AF.Sqrt, bias=eps, scale=1.0 / D),
             reads=[rbank[0]], writes=[rrstd])
        f.op("dve", lambda e: e.reciprocal(out=rstd[:, 0:nb], in_=rstd[:, 0:nb]), reads=[rrstd], writes=[rrstd])
        if hn is None:
            return
        tmp = st_tiles[8]
        rtmp = st_tiles[9]
        for k in range(KD):
            s = k % 2
            f.op("dve", lambda e: e.scalar_tensor_tensor(out=tmp[:, s, 0:nb], in0=xt[:, k, 0:nb], scalar=A12[:, i, j, k, v:v + 1],
                                                         in1=rstd[:, 0:nb], op0=ALU.mult, op1=ALU.mult),
                 reads=[rxt, rrstd, rconst], writes=[rtmp[s]])
            f.op("act", lambda e: e.activation(out=hn[:, k, 0:nb], in_=tmp[:, s, 0:nb], func=AF.Identity,
                                               bias=MOD(i, 3 * j, k, v), scale=1.0),
                 reads=[rtmp[s], rconst], writes=[rhn])

    def alloc_norm_tiles(st, with_hn=True, hn_dtype=BF16):
        xt = f.sb(st, [128, KD, 512], F32, "xt")
        hn = f.sb(st, [128, KD, 512], hn_dtype, "hn") if with_hn else None
        sq = f.sb(st, [128, 2, 512], BF16, "sq")
        rstd = f.sb(st, [128, 512], F32, "rstd")
        tmp = f.sb(st, [128, 2, 512], F32, "tmpn")
        return [xt, Res(), hn, Res(), sq, [Res(), Res()], rstd, Res(), tmp, [Res(), Res()]]

    WSLOT = 3

    def alloc_ring(st, nk_max):
        ring = [f.sb(st, [128, nk_max, 512], BF16, "wring") for _ in range(WSLOT)]
        return {"t": ring, "r": [Res() for _ in range(WSLOT)], "n": 0}

    def wload(ring, wkey, row0, nk, col0, ncol, src_ap=None, tok=None):
        s = ring["n"] % WSLOT
        ring["n"] += 1
        if src_ap is None:
            src_ap, tok = gathered[wkey]
        ncols_total = src_ap.shape[1]
        f.dma("sp", ring["t"][s][:, 0:nk, 0:ncol],
              dap(src_ap, row0 * ncols_total + col0, [(ncols_total, 128), (128 * ncols_total, nk), (1, ncol)]),
              writes=[ring["r"][s]], extra=list(tok))
        return ring["t"][s], ring["r"][s]

    def proj_residual(i, gslot, wkey, K_chunks, src, kmap=None, extra_scale=None, skip_ctx=False):
        with contextlib.ExitStack() as st:
            ring = alloc_ring(st, max(K_chunks, 1))
            sin = f.sb(st, [128, K_chunks, 512], BF16, "sin")
            rsin = Res()
            xt = f.sb(st, [128, KD, 512], F32, "xtp")
            rxt = Res()
            gs = f.sb(st, [128, KD, 2], F32, "gs")
            rgs = Res()
            if extra_scale is not None:
                f.op("dve", lambda e: e.tensor_tensor(out=gs[:], in0=modT[:, i, gslot * KD:(gslot + 1) * KD, :],
                                                      in1=bc(extra_scale[:, :, 0], [(1, KD), (0, 2)]), op=ALU.mult),
                     reads=[rconst], writes=[rgs])
            else:
                f.op("dve", lambda e: e.tensor_copy(out=gs[:], in_=modT[:, i, gslot * KD:(gslot + 1) * KD, :]),
                     reads=[rconst], writes=[rgs])
            for (t0, nb) in c.blocks:
                if skip_ctx and t0 < Lc:
                    continue
                v = 1 if t0 < Lc else 0
                f.dma("sp", sin[:, :, 0:nb], dap(src, t0, [(T, 128), (128 * T, K_chunks), (1, nb)]), writes=[rsin])
                f.dma("sp", xt[:, :, 0:nb], dap(XT, t0, [(T, 128), (128 * T, KD), (1, nb)]), writes=[rxt])
                mcols = min(512, D)
                for mb in range(D // mcols):
                    if kmap is None:
                        wt, rw = wload(ring, wkey, 0, K_chunks, mb * mcols, mcols)
                    for mm_ in range(mcols // 128):
                        m = mb * (mcols // 128) + mm_
                        b = m % 4
                        if kmap is None:
                            pairs = [(wt[:, k, mm_ * 128:(mm_ + 1) * 128], sin[:, k, 0:nb]) for k in range(K_chunks)]
                            f.mm(banks[b][:, 0:nb], pairs, reads=[rw, rsin], writes=[rbank[b]])
                        else:
                            ks, wrow0, wcol0 = kmap(m)
                            wt, rw = wload(ring, wkey, wrow0, len(ks), wcol0, 128)
                            pairs = [(wt[:, kk, 0:128], sin[:, k, 0:nb]) for kk, k in enumerate(ks)]
                            f.mm(banks[b][:, 0:nb], pairs, reads=[rw, rsin], writes=[rbank[b]])
                        f.op("dve", lambda e: e.scalar_tensor_tensor(out=xt[:, m, 0:nb], in0=banks[b][:, 0:nb], scalar=gs[:, m, v:v + 1],
                                                                     in1=xt[:, m, 0:nb], op0=ALU.mult, op1=ALU.add),
                             reads=[rbank[b], rgs], writes=[rxt])
                f.dma("pool", dap(XT, t0, [(T, 128), (128 * T, KD), (1, nb)]), xt[:, :, 0:nb], reads=[rxt])
            f.barrier()

    def ffn_phase(i, skip_ctx):
        HB = 512 if FH % 512 == 0 else 256
        with contextlib.ExitStack() as st:
            nt = alloc_norm_tiles(st)
            xt, rxt, hn, rhn = nt[0], nt[1], nt[2], nt[3]
            ring = alloc_ring(st, max(KD, (KH + 1) // 2))
            act = f.sb(st, [128, KH, 512], BF16, "act")
            ract = Res()
            sg = f.sb(st, [128, 2, 512], F32, "sg")
            rsg = [Res(), Res()]
            for (t0, nb) in c.blocks:
                if skip_ctx and t0 < Lc:
                    continue
                v = 1 if t0 < Lc else 0
                load_norm_block(nt, i, 1, t0, nb)
                it = 0
                for jb in range(FH // HB):
                    wg, rwg = wload(ring, f"ffg{i}", 0, KD, jb * HB, HB)
                    wu, rwu = wload(ring, f"ffu{i}", 0, KD, jb * HB, HB)
                    for cc in range(HB // 128):
                        j = jb * (HB // 128) + cc
                        s = it % 2
                        it += 1
                        bg, bu = 4 + 2 * s, 5 + 2 * s
                        f.mm(banks[bg][:, 0:nb], [(wg[:, k, cc * 128:(cc + 1) * 128], hn[:, k, 0:nb]) for k in range(KD)],
                             reads=[rwg, rhn], writes=[rbank[bg]])
                        f.mm(banks[bu][:, 0:nb], [(wu[:, k, cc * 128:(cc + 1) * 128], hn[:, k, 0:nb]) for k in range(KD)],
                             reads=[rwu, rhn], writes=[rbank[bu]])
                        f.op("act", lambda e: e.activation(out=sg[:, s, 0:nb], in_=banks[bg][:, 0:nb], func=AF.Silu),
                             reads=[rbank[bg]], writes=[rsg[s]])
                        f.op("dve", lambda e: e.tensor_tensor(out=act[:, j, 0:nb], in0=sg[:, s, 0:nb], in1=banks[bu][:, 0:nb], op=ALU.mult),
                             reads=[rsg[s], rbank[bu]], writes=[ract])
                mcols = min(512, D)
                halves = [(0, (KH + 1) // 2), ((KH + 1) // 2, KH)]
                for mb in range(D // mcols):
                    nm = mcols // 128
                    for hi, (k0, k1) in enumerate(halves):
                        wd, rwd = wload(ring, f"ffd{i}", k0 * 128, k1 - k0, mb * mcols, mcols)
                        for mm_ in range(nm):
                            b = mm_ + 4 * (mb % 2)
                            f.mm(banks[b][:, 0:nb], [(wd[:, k - k0, mm_ * 128:(mm_ + 1) * 128], act[:, k, 0:nb]) for k in range(k0, k1)],
                                 reads=[rwd, ract], writes=[rbank[b]], start=(hi == 0), stop=(hi == 1))
                    for mm_ in range(nm):
                        m = mb * nm + mm_
                        b = mm_ + 4 * (mb % 2)
                        f.op("dve", lambda e: e.scalar_tensor_tensor(out=xt[:, m, 0:nb], in0=banks[b][:, 0:nb], scalar=MOD(i, 5, m, v),
                                                                     in1=xt[:, m, 0:nb], op0=ALU.mult, op1=ALU.add),
                             reads=[rbank[b], rconst], writes=[rxt])
                f.dma("pool", dap(XT, t0, [(T, 128), (128 * T, KD), (1, nb)]), xt[:, :, 0:nb], reads=[rxt])
            f.barrier()

    def final_phase():
        with contextlib.ExitStack() as st:
            nt = alloc_norm_tiles(st, with_hn=False)
            xt, rxt, rstd, rrstd = nt[0], nt[1], nt[6], nt[7]
            yt = f.sb(st, [128, KD, 512], F32, "yt")
            ryt = Res()
            ot = f.sb(st, [128, 4, D], F32, "ot")
            rot = Res()
            for (t0, nb) in c.blocks:
                if t0 < Lc:
                    continue
                load_norm_block(nt, 0, 0, t0, nb)
                for k in range(KD):
                    f.op("dve", lambda e: e.scalar_tensor_tensor(out=yt[:, k, 0:nb], in0=xt[:, k, 0:nb], scalar=fnwT[:, k, 0:1],
                                                                 in1=rstd[:, 0:nb], op0=ALU.mult, op1=ALU.mult),
                         reads=[rxt, rrstd, rconst], writes=[ryt])
                na = nb // 128
                for a in range(na):
                    for k0 in range(0, KD, 4):
                        b = (k0 // 4) % 8
                        f.trs([(banks[b][:, (k - k0) * 128:(k - k0 + 1) * 128], yt[:, k, a * 128:(a + 1) * 128]) for k in range(k0, min(KD, k0 + 4))],
                              ident_f[:], reads=[ryt, rconst], writes=[rbank[b]])
                        w = min(KD, k0 + 4) - k0
                        if (k0 // 4) % 2 == 0:
                            f.op("act", lambda e: e.copy(out=ot[:, a, k0 * 128:(k0 + w) * 128], in_=banks[b][:, 0:w * 128]),
                                 reads=[rbank[b]], writes=[rot])
                        else:
                            f.op("dve", lambda e: e.tensor_copy(out=ot[:, a, k0 * 128:(k0 + w) * 128], in_=banks[b][:, 0:w * 128]),
                                 reads=[rbank[b]], writes=[rot])
                f.dma("pool", out_ext[t0 - Lc:t0 - Lc + nb, :].rearrange("(a p) d -> p a d", p=128), ot[:, 0:na, :], reads=[rot])
            f.barrier()

    def pool_phase(i, j, need_ctx):
        with contextlib.ExitStack() as st0:
          psT = f.sb(st0, [128, KD, 1], F32, "psT")
          load_T(pscale_in[j], 1, D, psT)
          with contextlib.ExitStack() as st:
            rstd_all = f.sb(st, [128, T], F32, "rstdall")
            rra = Res()
            with contextlib.ExitStack() as st2:
                nt = alloc_norm_tiles(st2, with_hn=False)
                for (t0, nb) in c.blocks:
                    if (not need_ctx) and t0 < Lc:
                        continue
                    load_norm_block(nt, i, 0, t0, nb)
                    f.op("act", lambda e: e.copy(out=rstd_all[:, t0:t0 + nb], in_=nt[6][:, 0:nb]), reads=[nt[7]], writes=[rra])
                f.barrier()
            PADW = 8
            seqs = ([(0, Lc, 1)] if need_ctx else []) + [(Lc, L, 0)]
            xc = f.sb(st, [128, L], F32, "xc")
            rxc = Res()
            hp = f.sb(st, [128, L + 2 * PADW], F32, "hp")
            rhp = Res()
            wa = f.sb(st, [128, L + 2 * PADW], F32, "wa")
            wb = f.sb(st, [128, L + 2 * PADW], F32, "wb")
            rwa, rwb = Res(), Res()
            inv = f.sb(st, [128, L], F32, "inv")
            rinv = Res()
            po = f.sb(st, [128, L], BF16, "po")
            rpo = Res()
            for (s0, S, v) in seqs:
                for g, win in enumerate(POOL_WINDOWS):
                    f.op("dve", lambda e: e.memset(inv[:, 0:S], 1.0 / win), writes=[rinv])
                    for pos in list(range(win // 2)) + list(range(S - win // 2 + 1, S)):
                        lo, hi = max(pos - win // 2, 0), min(pos + win - win // 2, S)
                        val = 1.0 / (hi - lo)
                        f.op("dve", lambda e: e.memset(inv[:, pos:pos + 1], val), writes=[rinv])
                    for kk in range(GK):
                        k = g * GK + kk
                        f.dma("sp", xc[:, 0:S], dap(XT, k * 128 * T + s0, [(T, 128), (1, S)]), writes=[rxc])
                        f.op("dve", lambda e: e.memset(hp[:, 0:PADW], 0.0), writes=[rhp])
                        f.op("dve", lambda e: e.memset(hp[:, PADW + S:PADW + S + PADW], 0.0), writes=[rhp])
                        f.op("dve", lambda e: e.scalar_tensor_tensor(out=xc[:, 0:S], in0=xc[:, 0:S], scalar=A12[:, i, 0, k, v:v + 1],
                                                                     in1=rstd_all[:, s0:s0 + S], op0=ALU.mult, op1=ALU.mult),
                             reads=[rra, rconst], writes=[rxc])
                        f.op("act", lambda e: e.activation(out=hp[:, PADW:PADW + S], in_=xc[:, 0:S], func=AF.Identity,
                                                           bias=MOD(i, 0, k, v), scale=1.0), reads=[rxc, rconst], writes=[rhp])
                        cur, rcur = hp, rhp
                        width = S + 2 * PADW
                        step = 1
                        bufs = [(wa, rwa), (wb, rwb)]
                        bi = 0
                        while step < win:
                            nxt, rn = bufs[bi % 2]
                            bi += 1
                            nw = width - step
                            f.op("dve", lambda e: e.tensor_tensor(out=nxt[:, 0:nw], in0=cur[:, 0:nw], in1=cur[:, step:step + nw], op=ALU.add),
                                 reads=[rcur], writes=[rn])
                            cur, rcur, width = nxt, rn, nw
                            step *= 2
                        off = PADW - win // 2
                        f.op("dve", lambda e: e.tensor_tensor(out=xc[:, 0:S], in0=cur[:, off:off + S], in1=inv[:, 0:S], op=ALU.mult),
                             reads=[rcur, rinv], writes=[rxc])
                        f.op("dve", lambda e: e.tensor_tensor(out=po[:, 0:S], in0=xc[:, 0:S], in1=hp[:, PADW:PADW + S], op=ALU.subtract),
                             reads=[rxc, rhp], writes=[rpo])
                        f.dma("pool", dap(S_big, k * 128 * T + s0, [(T, 128), (1, S)]), po[:, 0:S], reads=[rpo])
            f.barrier()

          def kmap(m):
              g = m // GK
              return [g * GK + kk for kk in range(GK)], g * GD, (m % GK) * 128
          proj_residual(i, 2, f"pool{j}", KD, S_big, kmap=kmap, extra_scale=psT, skip_ctx=not need_ctx)

    def fourier_phase(i, j, need_ctx):
        ABd = dscr(f"ABd{i}", [D // 128, T, 2, 128], BF16).ap()
        with contextlib.ExitStack() as st:
            nt = alloc_norm_tiles(st)
            hn, rhn = nt[2], nt[3]
            cs = f.sb(st, [128, 2, GK, GD], BF16, "cs")
            rcs = Res()
            for a in range(2):
                f.dma("sp", cs[:, a, :, :], dap(dftc_in, a * GD * GD, [(GD, 128), (128 * GD, GK), (1, GD)]), writes=[rcs])
            ab = f.sb(st, [128, KD, 2, 128], BF16, "ab")
            rab = Res()
            for (t0, nb) in c.blocks:
                if (not need_ctx) and t0 < Lc:
                    continue
                load_norm_block(nt, i, 0, t0, nb)
                for a in range(nb // 128):
                    it = 0
                    for g in range(4):
                        for part in range(2):
                            b = 4 + it % 4
                            it += 1
                            f.mm(banks[b][:, 0:GD], [(hn[:, g * GK + kk, a * 128:(a + 1) * 128], cs[:, part, kk, :]) for kk in range(GK)],
                                 reads=[rhn, rcs], writes=[rbank[b]])
                            dst = ab[:, g * GK:(g + 1) * GK, part, :]
                            srcp = banks[b][:, 0:GD].rearrange("p (k n) -> p k n", n=128)
                            if part == 0:
                                f.op("act", lambda e: e.copy(out=dst, in_=srcp), reads=[rbank[b]], writes=[rab])
                            else:
                                f.op("dve", lambda e: e.tensor_copy(out=dst, in_=srcp), reads=[rbank[b]], writes=[rab])
                    tt = t0 + a * 128
                    f.dma("pool", dap(ABd, tt * 256, [(256, 128), (T * 256, KD), (1, 256)]), ab[:].rearrange("p k a n -> p k (a n)"), reads=[rab])
            f.barrier()
        with contextlib.ExitStack() as st:
            seqs = ([(0, Lc, dftLc_in, None)] if need_ctx else []) + [(Lc, L, None, None)]
            yt = f.sb(st, [128, 2, 512], BF16, "ytf")
            ryt = [Res(), Res()]
            for (s0, S, small_src, _) in seqs:
                nl = S // 128
                kb = min(512, S)
                scale = 1.0 / float(np.sqrt(S * GD))
                with contextlib.ExitStack() as st2:
                    slab = f.sb(st2, [128, 2, nl, kb], BF16, "slab")
                    rslab = Res()
                    abl = [f.sb(st2, [128, nl, 256], BF16, "abl") for _ in range(2)]
                    rabl = [Res(), Res()]
                    for k1b in range(S // kb):
                        for part in range(2):
                            if small_src is not None:
                                f.dma("sp", slab[:, part, :, :], dap(small_src, part * S * S + k1b * kb, [(S, 128), (128 * S, nl), (1, kb)]), writes=[rslab])
                            else:
                                g_ap, g_tok = gathered[f"dftL{part}"]
                                f.dma("sp", slab[:, part, :, :], dap(g_ap, k1b * kb, [(S, 128), (128 * S, nl), (1, kb)]), writes=[rslab], extra=list(g_tok))
                        for q in range(KD):
                            s = q % 2
                            f.dma("sp", abl[s][:], dap(ABd, q * T * 256 + s0 * 256, [(256, 128), (128 * 256, nl), (1, 256)]), writes=[rabl[s]])
                            b = 4 + q % 4
                            pairs = []
                            for part in range(2):
                                for l in range(nl):
                                    pairs.append((abl[s][:, l, part * 128:(part + 1) * 128], slab[:, part, l, :]))
                            f.mm(banks[b][:, 0:kb], pairs, reads=[rabl[s], rslab], writes=[rbank[b]])
                            f.op("act", lambda e: e.activation(out=yt[:, s, 0:kb], in_=banks[b][:, 0:kb], func=AF.Copy, scale=scale),
                                 reads=[rbank[b]], writes=[ryt[s]])
                            f.dma("pool", dap(S_big, q * 128 * T + s0 + k1b * kb, [(T, 128), (1, kb)]), yt[:, s, 0:kb], reads=[ryt[s]])
                    f.barrier()
        proj_residual(i, 2, f"fou{j}", KD, S_big, skip_ctx=not need_ctx)

    def ssd_phase(i, j, with_ctx):
        KC = CONV // 128
        KI = DI // 128
        Ztok = dscr_shared("Ztok", [T, DI], BF16).ap()
        XBC = dscr_shared("XBC", [CONV, T], F32).ap()
        XStok = dscr_shared("XStok", [T, DI], BF16).ap()
        BCT = dscr_shared("BCT", [2 * G * 128, T], BF16).ap()
        Btok = dscr_shared("Btok", [T, G * 128], BF16).ap()
        CUMd = dscr_shared("CUMd", [2, NH, T], F32).ap()
        DTd = dscr_shared("DTd", [2, NH, T], F32).ap()
        Hd = dscr_shared("Hd", [2, NCH, 128, DI], BF16).ap()
        wkey = f"inp{j}"
        with contextlib.ExitStack() as stA:
            cwT = f.sb(stA, [128, KC, 5], F32, "cwT")
            cbT = f.sb(stA, [128, KC, 1], F32, "cbT")
            snT = f.sb(stA, [128, KI, 1], F32, "snT")
            load_T(convw_in[j], 5, CONV, cwT)
            load_T(convb_in[j], 1, CONV, cbT)
            load_T(ssdnw_in[j], 1, DI, snT)
            dtp = f.sb(stA, [NH, 2, 3], F32, "dtp")
            rdtp = Res()
            for d in range(2):
                f.dma("sp", dtp[:, d, 0:1], dtb_in[j, d * NH:(d + 1) * NH, :], writes=[rdtp])
                f.dma("sp", dtp[:, d, 1:2], alog_in[j, d * NH:(d + 1) * NH, :], writes=[rdtp])
            f.op("act", lambda e: e.activation(out=dtp[:, :, 1:2], in_=dtp[:, :, 1:2], func=AF.Exp), reads=[rdtp], writes=[rdtp])
            f.op("dve", lambda e: e.tensor_scalar(out=dtp[:, :, 1:2], in0=dtp[:, :, 1:2], scalar1=-1.0, scalar2=None, op0=ALU.mult), reads=[rdtp], writes=[rdtp])
            skb = f.sb(stA, [128, 3, NH], F32, "skb")
            rskb = Res()
            f.dma("sp", skb[:, 0:2, :], dap(dsk_in, j * 2 * NH, [(0, 128), (NH, 2), (1, NH)]), writes=[rskb])
            f.op("dve", lambda e: e.tensor_tensor(out=skb[:, 2, :], in0=skb[:, 0, :], in1=skb[:, 1, :], op=ALU.add), reads=[rskb], writes=[rskb])
            f.barrier()

            st_dt = contextlib.ExitStack()
            dtT = f.sb(st_dt, [NH, 2, T], F32, "dtT")
            rdt = Res()
            with contextlib.ExitStack() as st:
                nt = alloc_norm_tiles(st)
                hn, rhn = nt[2], nt[3]
                ring = alloc_ring(st, KD)
                zt = f.sb(st, [128, 2, 512], BF16, "zt")
                rzt = [Res(), Res()]
                xe = f.sb(st, [128, 2, 512], F32, "xe")
                rxe = [Res(), Res()]
                it = 0
                for (t0, nb) in c.blocks:
                    load_norm_block(nt, i, 0, t0, nb)
                    for cb0 in range(0, DI, 512):
                        ncb = min(512, DI - cb0)
                        wt, rw = wload(ring, wkey, 0, KD, cb0, ncb)
                        for a in range(nb // 128):
                            s = it % 2
                            it += 1
                            b = 4 + s
                            f.mm(banks[b][:, 0:ncb], [(hn[:, k, a * 128:(a + 1) * 128], wt[:, k, 0:ncb]) for k in range(KD)],
                                 reads=[rhn, rw], writes=[rbank[b]])
                            f.op("act", lambda e: e.copy(out=zt[:, s, 0:ncb], in_=banks[b][:, 0:ncb]), reads=[rbank[b]], writes=[rzt[s]])
                            f.dma("pool", Ztok[t0 + a * 128:t0 + (a + 1) * 128, cb0:cb0 + ncb], zt[:, s, 0:ncb], reads=[rzt[s]])
                    for cb0 in range(0, CONV, 512):
                        ncb = min(512, CONV - cb0)
                        wt, rw = wload(ring, wkey, 0, KD, DI + cb0, ncb)
                        for cc in range(ncb // 128):
                            s = it % 2
                            it += 1
                            b = 6 + s
                            f.mm(banks[b][:, 0:nb], [(wt[:, k, cc * 128:(cc + 1) * 128], hn[:, k, 0:nb]) for k in range(KD)],
                                 reads=[rhn, rw], writes=[rbank[b]])
                            f.op("dve", lambda e: e.tensor_copy(out=xe[:, s, 0:nb], in_=banks[b][:, 0:nb]), reads=[rbank[b]], writes=[rxe[s]])
                            f.dma("pool", XBC[cb0 + cc * 128:cb0 + (cc + 1) * 128, t0:t0 + nb], xe[:, s, 0:nb], reads=[rxe[s]])
                    wt, rw = wload(ring, wkey, 0, KD, DI + CONV, 2 * NH)
                    for d in range(2):
                        b = d
                        f.mm(banks[b][0:NH, 0:nb], [(wt[:, k, d * NH:(d + 1) * NH], hn[:, k, 0:nb]) for k in range(KD)],
                             reads=[rhn, rw], writes=[rbank[b]])
                        f.op("act", lambda e: e.activation(out=dtT[:, d, t0:t0 + nb], in_=banks[b][0:NH, 0:nb], func=AF.Exp,
                                                           bias=dtp[:, d, 0:1], scale=1.0), reads=[rbank[b], rdtp], writes=[rdt])
                f.op("act", lambda e: e.activation(out=dtT[:], in_=dtT[:], func=AF.Ln, bias=1.0, scale=1.0), reads=[rdt], writes=[rdt])
                f.dma("pool", dap(DTd, 0, [(T, NH), (NH * T, 2), (1, T)]), dtT[:], reads=[rdt])
                f.barrier()
            with contextlib.ExitStack() as st:
                ca = f.sb(st, [NH, 2, T], F32, "ca")
                cbuf = f.sb(st, [NH, 2, T], F32, "cbuf")
                rca, rcb = Res(), Res()
                for d in range(2):
                    f.op("dve", lambda e: e.tensor_scalar(out=ca[:, d, :], in0=dtT[:, d, :], scalar1=dtp[:, d, 1:2], scalar2=None, op0=ALU.mult),
                         reads=[rdt, rdtp], writes=[rca])
                cur, rcur, nxt, rnxt = ca, rca, cbuf, rcb
                step = 1
                while step < 128:
                    for d in range(2):
                        cv = cur[:, d, :].rearrange("p (c t) -> p c t", t=128)
                        nv = nxt[:, d, :].rearrange("p (c t) -> p c t", t=128)
                        if d == 0:
                            f.op("dve", lambda e: e.tensor_tensor(out=nv[:, :, step:128], in0=cv[:, :, step:128], in1=cv[:, :, 0:128 - step], op=ALU.add),
                                 reads=[rcur], writes=[rnxt])
                            f.op("dve", lambda e: e.tensor_copy(out=nv[:, :, 0:step], in_=cv[:, :, 0:step]), reads=[rcur], writes=[rnxt])
                        else:
                            f.op("dve", lambda e: e.tensor_tensor(out=nv[:, :, 0:128 - step], in0=cv[:, :, 0:128 - step], in1=cv[:, :, step:128], op=ALU.add),
                                 reads=[rcur], writes=[rnxt])
                            f.op("dve", lambda e: e.tensor_copy(out=nv[:, :, 128 - step:128], in_=cv[:, :, 128 - step:128]), reads=[rcur], writes=[rnxt])
                    cur, rcur, nxt, rnxt = nxt, rnxt, cur, rcur
                    step *= 2
                f.dma("pool", dap(CUMd, 0, [(T, NH), (NH * T, 2), (1, T)]), cur[:], reads=[rcur])
                f.barrier()
            st_dt.close()

            with contextlib.ExitStack() as st:
                GRP = 4
                raw = f.sb(st, [128, GRP, 516], F32, "raw")
                rraw = Res()
                acc = f.sb(st, [128, GRP, 512], F32, "cacc")
                racc = [Res() for _ in range(GRP)]
                so = f.sb(st, [128, GRP, 512], BF16, "so")
                rso = Res()
                tk = f.sb(st, [128, 4, GRP * 128], BF16, "tk")
                rtk = Res()
                for (t0, nb) in c.blocks:
                    s0, S = (0, Lc) if t0 < Lc else (Lc, L)
                    lo = 2 if t0 > s0 else 0
                    hi = 2 if t0 + nb < s0 + S else 0
                    for g0 in range(0, KC, GRP):
                        if lo == 0:
                            f.op("dve", lambda e: e.memset(raw[:, :, 0:2], 0.0), writes=[rraw])
                        if hi == 0:
                            f.op("dve", lambda e: e.memset(raw[:, :, 2 + nb:4 + nb], 0.0), writes=[rraw])
                        f.dma("sp", raw[:, :, 2 - lo:2 + nb + hi], dap(XBC, g0 * 128 * T + t0 - lo, [(T, 128), (128 * T, GRP), (1, nb + lo + hi)]), writes=[rraw])
                        for gg in range(GRP):
                            ch = g0 + gg
                            f.op("act", lambda e: e.activation(out=acc[:, gg, 0:nb], in_=raw[:, gg, 0:nb], func=AF.Identity,
                                                               bias=cbT[:, ch, 0:1], scale=cwT[:, ch, 0:1]), reads=[rraw, rconst], writes=[racc[gg]])
                        for w in range(1, 5):
                            for gg in range(GRP):
                                ch = g0 + gg
                                f.op("dve", lambda e: e.scalar_tensor_tensor(out=acc[:, gg, 0:nb], in0=raw[:, gg, w:w + nb], scalar=cwT[:, ch, w:w + 1],
                                                                             in1=acc[:, gg, 0:nb], op0=ALU.mult, op1=ALU.add),
                                     reads=[rraw, rconst], writes=[racc[gg]])
                        for gg in range(GRP):
                            f.op("act", lambda e: e.activation(out=so[:, gg, 0:nb], in_=acc[:, gg, 0:nb], func=AF.Silu), reads=[racc[gg]], writes=[rso])
                        is_x = g0 < KI
                        is_b = (g0 >= KI) and (g0 < KI + G)
                        if not is_x:
                            f.dma("pool", dap(BCT, (g0 - KI) * 128 * T + t0, [(T, 128), (128 * T, GRP), (1, nb)]), so[:, :, 0:nb], reads=[rso])
                        if is_x or is_b:
                            na = nb // 128
                            for a in range(na):
                                b = 4 + a % 4
                                pb = banks[b][:].bitcast(BF16)
                                f.trs([(pb[:, gg * 128:(gg + 1) * 128], so[:, gg, a * 128:(a + 1) * 128]) for gg in range(GRP)],
                                      ident_b[:], reads=[rso, rconst], writes=[rbank[b]])
                                if a % 2 == 0:
                                    f.op("act", lambda e: e.copy(out=tk[:, a, :], in_=pb[:, 0:GRP * 128]), reads=[rbank[b]], writes=[rtk])
                                else:
                                    f.op("dve", lambda e: e.tensor_copy(out=tk[:, a, :], in_=pb[:, 0:GRP * 128]), reads=[rbank[b]], writes=[rtk])
                            if is_x:
                                dst = dap(XStok, t0 * DI + g0 * 128, [(DI, 128), (128 * DI, na), (1, GRP * 128)])
                            else:
                                dst = dap(Btok, t0 * G * 128 + (g0 - KI) * 128, [(G * 128, 128), (128 * G * 128, na), (1, GRP * 128)])
                            f.dma("pool", dst, tk[:, 0:na, :], reads=[rtk])
                f.barrier()

            chunk_prep_cache = {}

            def chunk_small(st_t, cidx, d, want, b=0):
                sm, rsm = st_t["sm"], st_t["rsm"]
                cdt = st_t["cdt"]
                rcdt = st_t["rcdt"]
                t0 = cidx * 128
                f.dma("sp", cdt[:, 0, :], dap(CUMd, d * NH * T + t0, [(T, NH), (1, 128)]), writes=[rcdt])
                f.dma("sp", cdt[:, 1, :], dap(DTd, d * NH * T + t0, [(T, NH), (1, 128)]), writes=[rcdt])
                f.mms([(banks[b][:, 0:NH], [(cdt[:, 0, :], ident_f[0:NH, 0:NH])]),
                       (banks[b][:, NH:2 * NH], [(cdt[:, 1, :], ident_f[0:NH, 0:NH])])], reads=[rcdt, rconst], writes=[rbank[b]])
                f.op("dve", lambda e: e.tensor_copy(out=sm[:, 0:2, :], in_=banks[b][:, 0:2 * NH].rearrange("p (a h) -> p a h", h=NH)),
                     reads=[rbank[b]], writes=[rsm])
                f.op("dve", lambda e: e.tensor_scalar(out=sm[:, 2, :], in0=sm[:, 0, :], scalar1=-1.0, scalar2=None, op0=ALU.mult), reads=[rsm], writes=[rsm])
                f.op("act", lambda e: e.activation(out=sm[:, 3, :], in_=sm[:, 0, :], func=AF.Exp), reads=[rsm], writes=[rsm])
                if want == "state":
                    selm = tri_f[:, 2 + d, :]
                    f.mm(banks[b][:, 2 * NH:3 * NH], [(selm, sm[:, 0, :])], reads=[rsm, rconst], writes=[rbank[b]])
                    f.op("dve", lambda e: e.tensor_copy(out=sm[:, 4, :], in_=banks[b][:, 2 * NH:3 * NH]), reads=[rbank[b]], writes=[rsm])
                    f.op("dve", lambda e: e.tensor_tensor(out=sm[:, 5, :], in0=sm[:, 4, :], in1=sm[:, 0, :], op=ALU.subtract), reads=[rsm], writes=[rsm])
                    f.op("act", lambda e: e.activation(out=sm[:, 5, :], in_=sm[:, 5, :], func=AF.Exp), reads=[rsm], writes=[rsm])
                    f.op("act", lambda e: e.activation(out=sm[:, 6, :], in_=sm[:, 4, :], func=AF.Exp), reads=[rsm], writes=[rsm])
                    f.op("dve", lambda e: e.tensor_tensor(out=sm[:, 7, :], in0=sm[:, 5, :], in1=sm[:, 1, :], op=ALU.mult), reads=[rsm], writes=[rsm])

            with contextlib.ExitStack() as st:
                tls = [{"sm": f.sb(st, [128, 8, NH], F32, "sm"), "rsm": Res(), "cdt": f.sb(st, [NH, 2, 128], F32, "cdt"), "rcdt": Res()} for _ in range(2)]
                xsD = [f.sb(st, [128, DI], BF16, "xs") for _ in range(2)]
                rxsD = [Res(), Res()]
                btD = [f.sb(st, [128, G * 128], BF16, "bt") for _ in range(2)]
                rbtD = [Res(), Res()]
                wxD = [f.sb(st, [128, DI], BF16, "wx") for _ in range(2)]
                rwxD = [Res(), Res()]
                H = [f.sb(st, [128, DI], F32, "H") for _ in range(2)]
                rH = [Res(), Res()]
                HbD = [f.sb(st, [128, DI], BF16, "Hb") for _ in range(2)]
                rHbD = [Res(), Res()]
                orders = [list(range(NCH)), [1, 0] + list(range(NCH - 1, 1, -1))]
                for d in range(2):
                    f.op("dve", lambda e: e.memset(H[d][:], 0.0), writes=[rH[d]])
                for step_i in range(NCH):
                    for d in range(2):
                        tl = tls[d]
                        sm, rsm = tl["sm"], tl["rsm"]
                        xs, rxs, bt, rbt, wx, rwx, Hb, rHb = xsD[d], rxsD[d], btD[d], rbtD[d], wxD[d], rwxD[d], HbD[d], rHbD[d]
                        cidx = orders[d][step_i]
                        t0 = cidx * 128
                        f.op("act", lambda e: e.copy(out=Hb[:], in_=H[d][:]), reads=[rH[d]], writes=[rHb])
                        f.dma("pool", Hd[d, cidx, :, :], Hb[:], reads=[rHb])
                        if step_i == NCH - 1:
                            continue
                        chunk_small(tl, cidx, d, "state", b=d)
                        f.dma("sp", xs[:], XStok[t0:t0 + 128, :], writes=[rxs])
                        f.dma("sp", bt[:], Btok[t0:t0 + 128, :], writes=[rbt])
                        f.op("dve", lambda e: e.tensor_tensor(out=wx[:].rearrange("p (h q) -> p h q", q=64), in0=xs[:].rearrange("p (h q) -> p h q", q=64),
                                                              in1=bc(sm[:, 7, :], [(1, NH), (0, 64)]), op=ALU.mult), reads=[rxs, rsm], writes=[rwx])
                        ngb = max(1, 512 // EP)
                        for g0 in range(0, G, ngb):
                            b = 2 + ((g0 // ngb) + 3 * d) % 6
                            groups = [(banks[b][:, (g - g0) * EP:(g - g0 + 1) * EP], [(bt[:, g * 128:(g + 1) * 128], wx[:, g * EP:(g + 1) * EP])])
                                      for g in range(g0, g0 + ngb)]
                            f.mms(groups, reads=[rbt, rwx], writes=[rbank[b]])
                            wdt = ngb * EP
                            hv = H[d][:, g0 * EP:g0 * EP + wdt]
                            h0 = g0 * E
                            nh = ngb * E
                            f.op("dve", lambda e: e.tensor_tensor(out=hv.rearrange("p (h q) -> p h q", q=64), in0=hv.rearrange("p (h q) -> p h q", q=64),
                                                                  in1=bc(sm[:, 6, h0:h0 + nh], [(1, nh), (0, 64)]), op=ALU.mult), reads=[rsm], writes=[rH[d]])
                            f.op("dve", lambda e: e.tensor_tensor(out=hv, in0=hv, in1=banks[b][:, 0:wdt], op=ALU.add), reads=[rbank[b]], writes=[rH[d]])
                f.barrier()

            with contextlib.ExitStack() as st:
                tls = [{"sm": f.sb(st, [128, 8, NH], F32, "sm"), "rsm": Res(), "cdt": f.sb(st, [NH, 2, 128], F32, "cdt"), "rcdt": Res()} for _ in range(2)]
                xsP = [f.sb(st, [128, DI], BF16, "xs") for _ in range(2)]
                rxsP = [Res(), Res()]
                zzP = [f.sb(st, [128, DI], BF16, "zz") for _ in range(2)]
                rzzP = [Res(), Res()]
                bctP = [f.sb(st, [128, 2 * G, 128], BF16, "bct") for _ in range(2)]
                rbctP = [Res(), Res()]
                GmP = [f.sb(st, [128, 2, G, 128], F32, "Gm") for _ in range(2)]
                rGmP = [Res(), Res()]
                sz = f.sb(st, [128, DI], F32, "sz")
                rsz = Res()
                HsD = [f.sb(st, [128, DI], BF16, "Hs") for _ in range(2)]
                rHsD = [Res(), Res()]
                HPC = min(16, NH)
                cumb = [f.sb(st, [128, HPC, 128], F32, "cumb") for _ in range(2)]
                rcumb = [Res(), Res()]
                xdtD = [f.sb(st, [128, DI], BF16, "xdt") for _ in range(2)]
                rxdtD = [Res(), Res()]
                yacc = f.sb(st, [128, DI], F32, "yacc")
                ryacc = Res()
                Eb = [f.sb(st, [128, E, 128], F32, "Eb") for _ in range(2)]
                rEb = [Res(), Res()]
                Mb = [f.sb(st, [128, E, 128], BF16, "Mb") for _ in range(2)]
                rMb = [Res(), Res()]
                tmpyP = [f.sb(st, [128, EP], F32, "tmpy") for _ in range(2)]
                rtmpyP = [Res(), Res()]
                ssq = f.sb(st, [128, 2, G], F32, "ssq")
                rssq = Res()
                gn = f.sb(st, [128, DI], BF16, "gn")
                rgn = Res()
                gT = f.sb(st, [128, KI, 128], BF16, "gT")
                rgT = Res()
                junk = f.sb(st, [128, DI // G], F32, "junk")
                rjunk = Res()
                gi = 0
                ci = 0
                pc = 0
                for cidx in range(NCH):
                    t0 = cidx * 128
                    if t0 < Lc and not with_ctx:
                        continue
                    par = ci % 2
                    ci += 1
                    xs, rxs, zz, rzz, bct, rbct, Gm, rGm = xsP[par], rxsP[par], zzP[par], rzzP[par], bctP[par], rbctP[par], GmP[par], rGmP[par]
                    f.dma("sp", xs[:], XStok[t0:t0 + 128, :], writes=[rxs])
                    f.dma("sp", zz[:], Ztok[t0:t0 + 128, :], writes=[rzz])
                    f.dma("sp", bct[:], dap(BCT, t0, [(T, 128), (128 * T, 2 * G), (1, 128)]), writes=[rbct])
                    for d in range(2):
                        f.dma("sp", HsD[d][:], Hd[d, cidx, :, :], writes=[rHsD[d]])
                    groups = [(banks[0][:, g * 128:(g + 1) * 128] if g < 4 else banks[1][:, (g - 4) * 128:(g - 3) * 128],
                               [(bct[:, g, :], bct[:, G + g, :])]) for g in range(G)]
                    f.mms(groups, reads=[rbct], writes=[rbank[0], rbank[1]])
                    for d in range(2):
                        for hb in range(2):
                            f.op("dve", lambda e: e.tensor_tensor(out=Gm[:, d, hb * 4:(hb + 1) * 4, :], in0=banks[hb][:, :].rearrange("p (g l) -> p g l", l=128),
                                                                  in1=bc(tri_f[:, d, :], [(0, 4), (1, 128)]), op=ALU.mult),
                                 reads=[rbank[hb], rconst], writes=[rGm])
                    for d in range(2):
                        chunk_small(tls[d], cidx, d, "out", b=2 + d)
                        smd = tls[d]["sm"]
                        f.op("dve", lambda e: e.tensor_tensor(out=xdtD[d][:].rearrange("p (h q) -> p h q", q=64), in0=xs[:].rearrange("p (h q) -> p h q", q=64),
                                                              in1=bc(smd[:, 1, :], [(1, NH), (0, 64)]), op=ALU.mult), reads=[rxs, tls[d]["rsm"]], writes=[rxdtD[d]])
                    for d in range(2):
                        sm, rsm = tls[d]["sm"], tls[d]["rsm"]
                        xdt, rxdt, Hs, rHs = xdtD[d], rxdtD[d], HsD[d], rHsD[d]
                        for g in range(G):
                            h0 = g * E
                            if h0 % HPC == 0:
                                cs_ = pc % 2
                                pc += 1
                                f.dma("sp", cumb[cs_][:], dap(CUMd, (d * NH + h0) * T + t0, [(0, 128), (T, HPC), (1, 128)]), writes=[rcumb[cs_]])
                            s = gi % 2
                            gi += 1
                            tmpy, rtmpy = tmpyP[s], rtmpyP[s]
                            for e_ in range(E):
                                h = h0 + e_
                                f.op("act", lambda e: e.activation(out=Eb[s][:, e_, :], in_=cumb[cs_][:, h % HPC, :], func=AF.Exp,
                                                                   bias=sm[:, 2, h:h + 1], scale=1.0), reads=[rcumb[cs_], rsm], writes=[rEb[s]])
                            f.op("dve", lambda e: e.scalar_tensor_tensor(out=Mb[s][:], in0=Eb[s][:], scalar=1.0, in1=bc(Gm[:, d, g, :], [(0, E), (1, 128)]),
                                                                         op0=ALU.min, op1=ALU.mult), reads=[rEb[s], rGm], writes=[rMb[s]])
                            bi_, bx_ = 4 + 2 * s, 5 + 2 * s
                            groups = [(banks[bi_][:, e_ * 64:(e_ + 1) * 64], [(Mb[s][:, e_, :], xdt[:, (h0 + e_) * 64:(h0 + e_ + 1) * 64])]) for e_ in range(E)]
                            f.mms(groups, reads=[rMb[s], rxdt], writes=[rbank[bi_]])
                            f.mm(banks[bx_][:, 0:EP], [(bct[:, G + g, :], Hs[:, g * EP:(g + 1) * EP])], reads=[rbct, rHs], writes=[rbank[bx_]])
                            yv = yacc[:, g * EP:(g + 1) * EP]
                            f.op("dve", lambda e: e.tensor_tensor(out=tmpy[:].rearrange("p (h q) -> p h q", q=64), in0=banks[bx_][:, 0:EP].rearrange("p (h q) -> p h q", q=64),
                                                                  in1=bc(sm[:, 3, h0:h0 + E], [(1, E), (0, 64)]), op=ALU.mult), reads=[rbank[bx_], rsm], writes=[rtmpy])
                            if d == 0:
                                f.op("dve", lambda e: e.tensor_tensor(out=yv, in0=tmpy[:], in1=banks[bi_][:, 0:EP], op=ALU.add), reads=[rtmpy, rbank[bi_]], writes=[ryacc])
                            else:
                                f.op("dve", lambda e: e.tensor_tensor(out=tmpy[:], in0=tmpy[:], in1=banks[bi_][:, 0:EP], op=ALU.add), reads=[rtmpy, rbank[bi_]], writes=[rtmpy])
                                f.op("dve", lambda e: e.tensor_tensor(out=yv, in0=yv, in1=tmpy[:], op=ALU.add), reads=[rtmpy], writes=[ryacc])
                    f.op("dve", lambda e: e.tensor_tensor(out=sz[:].rearrange("p (h q) -> p h q", q=64), in0=xs[:].rearrange("p (h q) -> p h q", q=64),
                                                          in1=bc(skb[:, 2, :], [(1, NH), (0, 64)]), op=ALU.mult), reads=[rxs, rskb], writes=[rsz])
                    f.op("dve", lambda e: e.tensor_tensor(out=yacc[:], in0=yacc[:], in1=sz[:], op=ALU.add), reads=[rsz], writes=[ryacc])
                    f.op("act", lambda e: e.activation(out=sz[:], in_=zz[:], func=AF.Silu), reads=[rzz], writes=[rsz])
                    f.op("dve", lambda e: e.tensor_tensor(out=yacc[:], in0=yacc[:], in1=sz[:], op=ALU.mult), reads=[rsz], writes=[ryacc])
                    gw = DI // G
                    f.op("dve", lambda e: e.memset(ssq[:], 0.0), writes=[rssq])
                    for g in range(G):
                        f.op("act", lambda e: e.activation(out=junk[:], in_=yacc[:, g * gw:(g + 1) * gw], func=AF.Square, accum_out=ssq[:, 0, g:g + 1]),
                             reads=[ryacc], writes=[rjunk, rssq])
                    f.op("act", lambda e: e.activation(out=ssq[:, 1, :], in_=ssq[:, 0, :], func=AF.Sqrt, bias=1e-5, scale=1.0 / gw), reads=[rssq], writes=[rssq])
                    f.op("dve", lambda e: e.reciprocal(out=ssq[:, 1, :], in_=ssq[:, 1, :]), reads=[rssq], writes=[rssq])
                    f.op("dve", lambda e: e.tensor_tensor(out=gn[:].rearrange("p (g q) -> p g q", q=gw), in0=yacc[:].rearrange("p (g q) -> p g q", q=gw),
                                                          in1=bc(ssq[:, 1, :], [(1, G), (0, gw)]), op=ALU.mult), reads=[ryacc, rssq], writes=[rgn])
                    for k0 in range(0, KI, 4):
                        b = 2 + (k0 // 4) % 2
                        pb = banks[b][:].bitcast(BF16)
                        f.trs([(pb[:, (k - k0) * 128:(k - k0 + 1) * 128], gn[:, k * 128:(k + 1) * 128]) for k in range(k0, k0 + 4)],
                              ident_b[:], reads=[rgn, rconst], writes=[rbank[b]])
                        f.op("dve", lambda e: e.tensor_tensor(out=gT[:, k0:k0 + 4, :], in0=pb[:, 0:512].rearrange("p (k t) -> p k t", t=128),
                                                              in1=bc(snT[:, k0:k0 + 4, 0], [(1, 4), (0, 128)]), op=ALU.mult), reads=[rbank[b], rconst], writes=[rgT])
                    f.dma("pool", dap(S_big, t0, [(T, 128), (128 * T, KI), (1, 128)]), gT[:], reads=[rgT])
                f.barrier()
        proj_residual(i, 2, f"outp{j}", KI, S_big, skip_ctx=not with_ctx)

    for i in range(depth):
        kind, j = i % 3, i // 3
        last = (i == depth - 1)
        if i + 1 < depth:
            prep_layer_weights(i + 1)
        if kind == 0:
            ssd_phase(i, j, not last)
        elif kind == 1:
            fourier_phase(i, j, not last)
        else:
            pool_phase(i, j, not last)
        ffn_phase(i, skip_ctx=last)
    final_phase()
    f.barrier()
    f.es.close()
    return nc


def host_consts(cfg):
    c = cfg
    D, L, Lc, GD = c.D, c.L, c.Lc, c.GD
    quarter = D // 4
    omega = (1.0 / (np.float32(10000.0) ** (np.arange(quarter, dtype=np.float32) / np.float32(quarter)))).astype(np.float32)
    kk = np.arange(64, dtype=np.float32)[:, None]
    Tt = np.concatenate([np.sin(kk * omega), np.cos(kk * omega)], axis=-1).astype(np.float32)
    posT = np.ascontiguousarray(Tt.T)
    s = np.arange(128)[:, None]
    l = np.arange(128)[None, :]
    tri = np.zeros((128, 4, 128), np.float32)
    tri[:, 0, :] = (s <= l)
    tri[:, 1, :] = (s >= l)
    tri[127, 2, :] = 1.0
    tri[0, 3, :] = 1.0

    def dft(n):
        idx = np.arange(n, dtype=np.int64)
        ang = 2.0 * np.pi * ((idx[:, None] * idx[None, :]) % n).astype(np.float64) / n
        return np.cos(ang), np.sin(ang)
    cc, sc = dft(GD)
    cL, sL = dft(L)
    cLc, sLc = dft(Lc)
    bf = ml_dtypes.bfloat16
    return {
        "posT": posT, "tri": tri, "ident": np.eye(128, dtype=np.float32),
        "dftc": np.stack([cc, sc]).astype(np.float32).astype(bf),
        "dftL": np.stack([cL, -sL]).astype(np.float32).astype(bf),
        "dftLc": np.stack([cLc, -sLc]).astype(np.float32).astype(bf),
    }


def make_in_maps(cfg, inp):
    c = cfg
    D, NH = c.D, c.NH
    hc = host_consts(c)
    f32 = np.float32
    A = {k: np.asarray(v) for k, v in inp.items()}

    pw = A["pool_w"]
    pw2 = pw.reshape(pw.shape[0], 4 * c.GD, c.GD) if pw.shape[0] > 0 else np.zeros((1, 4 * c.GD, c.GD), f32)
    fw = A["fourier_w_out"] if A["fourier_w_out"].shape[0] > 0 else np.zeros((1, D, D), f32)
    ps = A["pool_scale"] if A["pool_scale"].shape[0] > 0 else np.zeros((1, D), f32)
    shared = {
        "ident": hc["ident"], "posT": hc["posT"], "tri": hc["tri"],
        "wmod": A["w_mod"], "bmod": A["b_mod"], "normw": A["norm_w"].reshape(c.depth * 2, D), "fnw": A["final_norm_w"].reshape(1, D),
        "ffg": A["ffn_w_gate"], "ffu": A["ffn_w_up"], "ffd": A["ffn_w_down"],
        "inproj": A["ssd_in_proj"], "convw": A["ssd_conv_w"], "convb": A["ssd_conv_b"][:, None, :],
        "dtb": A["ssd_dt_bias"].reshape(c.NSSD, 2 * NH, 1), "alog": A["ssd_a_log"].reshape(c.NSSD, 2 * NH, 1),
        "dskip": A["ssd_d"], "ssdnw": A["ssd_norm_w"][:, None, :], "outproj": A["ssd_out_proj"],
        "fouw": fw, "poolw": pw2, "pscale": ps[:, None, :],
        "dftc": hc["dftc"], "dftL": hc["dftL"], "dftLc": hc["dftLc"],
    }
    shared = {k: np.ascontiguousarray(v) for k, v in shared.items()}
    maps = []
    for r in range(NCORES):
        m = dict(shared)
        m["x"] = np.ascontiguousarray(A["x"][r])
        m["ctx"] = np.ascontiguousarray(A["ctx"][r])
        m["c2"] = np.ascontiguousarray(np.stack([A["c"][r], A["c_ctx"]], axis=0).astype(f32))
        maps.append(m)
    return maps


_NC_CACHE = {}


def run(cfg, inp):
    key = (cfg.D, cfg.L, cfg.Lc, cfg.depth)
    if key not in _NC_CACHE:
        _NC_CACHE[key] = build(cfg)
    nc = _NC_CACHE[key]
    maps = make_in_maps(cfg, inp)
    res = run_bass_kernel_spmd(nc, maps, core_ids=list(range(NCORES)))
    return np.stack([res.results[r]["out"] for r in range(NCORES)], axis=0)


def kernel(**inputs):
    cfg = Cfg(2048, 4096, 256, 4)
    return run(cfg, inputs).astype(np.float32)
```

```python
import contextlib
import numpy as np
import ml_dtypes
import concourse.bass as bass
import concourse.mybir as mybir
from concourse.bass_utils import run_bass_kernel_spmd

F32 = mybir.dt.float32
BF16 = mybir.dt.bfloat16
ALU = mybir.AluOpType
AF = mybir.ActivationFunctionType
AX = mybir.AxisListType
NCORES = 8
POOL_WINDOWS = (2, 4, 8, 16)


class Cfg:
    def __init__(s, D=2048, L=4096, Lc=256, depth=4):
        s.D, s.L, s.Lc, s.T, s.depth = D, L, Lc, L + Lc, depth
        s.KD = D // 128
        s.DI = 2 * D
        s.NH = s.DI // 64
        s.G = 8
        s.E = s.NH // 8
        s.EP = s.E * 64
        s.CONV = s.DI + 2 * s.G * 128
        s.INP = s.DI + s.CONV + 2 * s.NH
        s.FH = ((8 * D // 3 + 255) // 256) * 256
        s.KH = s.FH // 128
        s.GD = D // 4
        s.GK = s.GD // 128
        s.MC = 6 * D // 8
        s.CB = s.MC // 3
        s.NSSD = (depth + 2) // 3
        s.NFOU = (depth + 1) // 3
        s.NPOOL = depth // 3
        s.NCH = s.T // 128
        s.blocks = [(0, Lc)] + [(Lc + 512 * i, 512) for i in range(L // 512)]


class Res:
    __slots__ = ("w", "readers", "wprev")

    def __init__(self):
        self.w = None
        self.readers = []
        self.wprev = []


class FW:
    NDMA = 24

    def __init__(self, nc):
        self.nc = nc
        self.es = contextlib.ExitStack()
        self.eng = {"pe": nc.tensor, "act": nc.scalar, "dve": nc.vector, "pool": nc.gpsimd, "sp": nc.sync}
        self.sem, self.cnt = {}, {}
        for e in ("pe", "act", "dve", "pool"):
            self.sem[e] = self.es.enter_context(nc.semaphore("s_" + e))
            self.cnt[e] = 0
        self.dsem, self.dcnt, self.dnext = {}, {}, {}
        for q in ("sp", "pool"):
            self.dsem[q] = [self.es.enter_context(nc.semaphore(f"d_{q}{i}")) for i in range(self.NDMA)]
            self.dcnt[q] = [0] * self.NDMA
            self.dnext[q] = 0
        self.known = {}
        self.uid = 0
        self.cc_toks = []

    def sb(self, stack, shape, dtype, name="t"):
        self.uid += 1
        return stack.enter_context(self.nc.sbuf_tensor(f"{name}_{self.uid}", list(shape), dtype))

    def ps(self, stack, shape, dtype, name="p"):
        self.uid += 1
        return stack.enter_context(self.nc.psum_tensor(f"{name}_{self.uid}", list(shape), dtype))

    def _wait(self, e, tok):
        if tok is None:
            return
        sem, val = tok
        key = (e, id(sem))
        if self.known.get(key, 0) >= val:
            return
        self.known[key] = val
        self.eng[e].wait_ge(sem, val)

    def _deps(self, e, reads, writes, extra=()):
        for t in extra:
            self._wait(e, t)
        for r in reads:
            self._wait(e, r.w)
            for t in r.wprev:
                self._wait(e, t)
        for w in writes:
            self._wait(e, w.w)
            for t in w.wprev:
                self._wait(e, t)
            for t in w.readers:
                self._wait(e, t)

    def _post(self, tok, reads, writes):
        for r in reads:
            r.readers.append(tok)
        for w in writes:
            if w.w is not None and w.w[0] is not tok[0]:
                w.wprev.append(w.w)
                if len(w.wprev) > 64:
                    w.wprev = w.wprev[-64:]
            w.w = tok
            w.readers = []

    def _sig(self, e, ins, reads, writes):
        self.cnt[e] += 1
        ins.then_inc(self.sem[e], 1)
        tok = (self.sem[e], self.cnt[e])
        self._post(tok, reads, writes)
        return tok

    def op(self, e, fn, reads=(), writes=()):
        self._deps(e, reads, writes)
        return self._sig(e, fn(self.eng[e]), reads, writes)

    def mm(self, out_ap, pairs, reads=(), writes=(), start=True, stop=True):
        self._deps("pe", reads, writes)
        n = len(pairs)
        ins = None
        for i, (l, r) in enumerate(pairs):
            ins = self.nc.tensor.matmul(out_ap, l, r, start=(start and i == 0), stop=(stop and i == n - 1))
        return self._sig("pe", ins, reads, writes)

    def mms(self, groups, reads=(), writes=()):
        self._deps("pe", reads, writes)
        ins = None
        for out_ap, pairs in groups:
            n = len(pairs)
            for i, (l, r) in enumerate(pairs):
                ins = self.nc.tensor.matmul(out_ap, l, r, start=(i == 0), stop=(i == n - 1))
        return self._sig("pe", ins, reads, writes)

    def trs(self, items, ident, reads=(), writes=()):
        self._deps("pe", reads, writes)
        ins = None
        for o, i in items:
            ins = self.nc.tensor.transpose(o, i, ident)
        return self._sig("pe", ins, reads, writes)

    def dma(self, q, out, in_, reads=(), writes=(), extra=()):
        try:
            so, si = tuple(out.shape), tuple(in_.shape)
        except Exception:
            so, si = None, None
        if so is not None and so == si and len(so) >= 3:
            ndesc = so[0] * int(np.prod(so[1:-1]))
            if ndesc > 1024 and so[1] > 1:
                per = max(1, so[1] * 1024 // ndesc)
                tok = None
                for a in range(0, so[1], per):
                    b = min(so[1], a + per)
                    tok = self._dma1(q, out[:, a:b], in_[:, a:b], reads, writes, extra)
                return tok
        return self._dma1(q, out, in_, reads, writes, extra)

    def _dma1(self, q, out, in_, reads=(), writes=(), extra=()):
        i = self.dnext[q]
        self.dnext[q] = (i + 1) % self.NDMA
        sem = self.dsem[q][i]
        if self.dcnt[q][i] > 0:
            self._wait(q, (sem, self.dcnt[q][i]))
        self._deps(q, reads, writes, extra)
        ins = self.eng[q].dma_start(out=out, in_=in_)
        self.dcnt[q][i] += 16
        ins.then_inc(sem, 16)
        tok = (sem, self.dcnt[q][i])
        self._post(tok, reads, writes)
        return tok

    def barrier(self):
        toks = [(self.sem[e], self.cnt[e]) for e in self.sem if self.cnt[e] > 0]
        for q in self.dsem:
            for i, s in enumerate(self.dsem[q]):
                if self.dcnt[q][i] > 0:
                    toks.append((s, self.dcnt[q][i]))
        for e in ("pe", "act", "dve", "pool", "sp"):
            for t in toks:
                self._wait(e, t)


def bc(ap, steps):
    return bass.AP(ap.tensor, ap.offset, [list(ap.ap[0])] + [list(x) for x in steps])


def dap(t, offset, dims):
    a = t if isinstance(t, bass.AP) else t.ap()
    return bass.AP(a.tensor, a.offset + offset, [list(x) for x in dims])


def build(cfg):
    c = cfg
    D, L, Lc, T, KD, DI, NH, G, E, EP = c.D, c.L, c.Lc, c.T, c.KD, c.DI, c.NH, c.G, c.E, c.EP
    CONV, INP, FH, KH, GD, GK, MC, CB, depth = c.CONV, c.INP, c.FH, c.KH, c.GD, c.GK, c.MC, c.CB, c.depth
    NCH = c.NCH
    nc = bass.Bass("TRN2", target_bir_lowering=False)
    f = FW(nc)

    def din(name, shape, dt=F32):
        return nc.dram_tensor(name, list(shape), dt, kind="ExternalInput").ap()

    x_in = din("x", [L, D])
    ctx_in = din("ctx", [Lc, D])
    c2_in = din("c2", [2, D])
    ident_in = din("ident", [128, 128])
    pos_in = din("posT", [D // 2, 64])
    tri_in = din("tri", [128, 4, 128])
    wmod_in = din("wmod", [depth, D, 6 * D])
    bmod_in = din("bmod", [depth, 6 * D])
    normw_in = din("normw", [depth * 2, D])
    fnw_in = din("fnw", [1, D])
    ffg_in = din("ffg", [depth, D, FH])
    ffu_in = din("ffu", [depth, D, FH])
    ffd_in = din("ffd", [depth, FH, D])
    inp_in = din("inproj", [c.NSSD, D, INP])
    convw_in = din("convw", [c.NSSD, 5, CONV])
    convb_in = din("convb", [c.NSSD, 1, CONV])
    dtb_in = din("dtb", [c.NSSD, 2 * NH, 1])
    alog_in = din("alog", [c.NSSD, 2 * NH, 1])
    dsk_in = din("dskip", [c.NSSD, 2, NH])
    ssdnw_in = din("ssdnw", [c.NSSD, 1, DI])
    outp_in = din("outproj", [c.NSSD, DI, D])
    fou_in = din("fouw", [max(c.NFOU, 1), D, D])
    poolw_in = din("poolw", [max(c.NPOOL, 1), 4 * GD, GD])
    pscale_in = din("pscale", [max(c.NPOOL, 1), 1, D])
    dftc_in = din("dftc", [2, GD, GD], BF16)
    dftL_in = din("dftL", [2, L, L], BF16)
    dftLc_in = din("dftLc", [2, Lc, Lc], BF16)
    out_ext = nc.dram_tensor("out", [L, D], F32, kind="ExternalOutput").ap()

    def dscr(name, shape, dt):
        return nc.dram_tensor(name, list(shape), dt)

    _shared = {}

    def dscr_shared(name, shape, dt):
        if name not in _shared:
            _shared[name] = nc.dram_tensor(name, list(shape), dt)
        return _shared[name]

    XT = dscr("XT", [D, T], F32).ap()
    S_big = dscr("S_big", [max(DI, D), T], BF16).ap()

    top = f.es
    ident_f = f.sb(top, [128, 128], F32, "identf")
    ident_b = f.sb(top, [128, 128], BF16, "identb")
    ones_b = f.sb(top, [128, 128], BF16, "onesb")
    tri_f = f.sb(top, [128, 4, 128], F32, "tri")
    modT = f.sb(top, [128, depth, 6 * KD, 2], F32, "modT")
    A12 = f.sb(top, [128, depth, 2, KD, 2], F32, "A12")
    normT = f.sb(top, [128, KD, depth * 2], F32, "normT")
    fnwT = f.sb(top, [128, KD, 1], F32, "fnwT")
    rconst = Res()
    banks = [f.ps(top, [128, 512], F32, f"bank{i}") for i in range(8)]
    rbank = [Res() for _ in range(8)]

    f.dma("sp", ident_f[:], ident_in[:, :], writes=[rconst])
    f.dma("sp", tri_f[:], tri_in[:, :, :], writes=[rconst])
    f.op("dve", lambda e: e.tensor_copy(out=ident_b[:], in_=ident_f[:]), reads=[rconst], writes=[rconst])
    f.op("dve", lambda e: e.memset(ones_b[:], 1.0), writes=[rconst])

    gathered = {}

    WCB = 512

    def prep_weight(key, src, rows, cols):
        nkt = rows // 128
        ncb = (cols + WCB - 1) // WCB
        full = dscr(f"wg_{key}", [ncb * 128 * nkt * WCB], BF16)
        toks = []
        nfull = cols // WCB
        tail = cols - nfull * WCB
        for k in range(nkt):
            if nfull > 0:
                dst = dap(full, k * WCB, [(nkt * WCB, 128), (128 * nkt * WCB, nfull), (1, WCB)])
                sap = dap(src, k * 128 * cols, [(cols, 128), (WCB, nfull), (1, WCB)])
                toks.append(f.dma("pool", dst, sap))
            if tail > 0:
                dst = dap(full, nfull * 128 * nkt * WCB + k * WCB, [(nkt * WCB, 128), (1, tail)])
                sap = dap(src, k * 128 * cols + nfull * WCB, [(cols, 128), (1, tail)])
                toks.append(f.dma("pool", dst, sap))
        gathered[key] = (full, toks, nkt)

    def prep_layer_weights(i):
        kind, j = i % 3, i // 3
        if kind == 0:
            prep_weight(f"inp{j}", inp_in[j], D, INP)
            prep_weight(f"outp{j}", outp_in[j], DI, D)
        elif kind == 1:
            gathered["dftL0"] = (dftL_in[0], [], 0)
            gathered["dftL1"] = (dftL_in[1], [], 0)
            prep_weight(f"fou{j}", fou_in[j], D, D)
        else:
            prep_weight(f"pool{j}", poolw_in[j], 4 * GD, GD)
        prep_weight(f"ffg{i}", ffg_in[i], D, FH)
        prep_weight(f"ffu{i}", ffu_in[i], D, FH)
        prep_weight(f"ffd{i}", ffd_in[i], FH, D)

    prep_layer_weights(0)

    def load_T(rows_ap, n, W, out_tile_ap, rows_tile=None, rows_res=None):
        with contextlib.ExitStack() as st:
            if rows_tile is None:
                rows = f.sb(st, [n, W], F32, "rows")
                rr = Res()
                f.dma("sp", rows[:], rows_ap, writes=[rr])
            else:
                rows, rr = rows_tile, rows_res
            nk = W // 128
            per = 512 // n
            for k0 in range(0, nk, per):
                k1 = min(nk, k0 + per)
                b = 0
                groups = []
                for k in range(k0, k1):
                    groups.append((banks[b][:, (k - k0) * n:(k - k0 + 1) * n],
                                   [(rows[:, k * 128:(k + 1) * 128], ident_f[0:n, 0:n])]))
                f.mms(groups, reads=[rr, rconst], writes=[rbank[b]])
                src = banks[b][:, 0:(k1 - k0) * n].rearrange("p (k n) -> p k n", n=n)
                f.op("dve", lambda e: e.tensor_copy(out=out_tile_ap[:, k0:k1, :], in_=src), reads=[rbank[b]], writes=[rconst])
            f.barrier()

    with contextlib.ExitStack() as st:
        scT = f.sb(st, [128, KD, 2], F32, "scT")
        rsc = Res()
        load_T(c2_in[:, :], 2, D, scT)
        f.op("act", lambda e: e.activation(out=scT[:], in_=scT[:], func=AF.Silu), reads=[rconst], writes=[rsc])
        rowsM = f.sb(st, [2, 6 * D], F32, "rowsM")
        bm = f.sb(st, [2, 6 * D], F32, "bm")
        rrows = Res()
        rbm = Res()
        wm = [f.sb(st, [128, KD, 512], F32, "wm") for _ in range(3)]
        rwm = [Res(), Res(), Res()]
        MCB = min(512, 6 * D)
        it = 0
        for i in range(depth):
            f.dma("sp", bm[:], dap(bmod_in, i * 6 * D, [(0, 2), (1, 6 * D)]), writes=[rbm])
            for cb in range(6 * D // MCB):
                s = it % 3
                it += 1
                f.dma("sp", wm[s][:, :, 0:MCB], dap(wmod_in, i * D * 6 * D + cb * MCB, [(6 * D, 128), (128 * 6 * D, KD), (1, MCB)]), writes=[rwm[s]])
                b = 1 + s
                f.mm(banks[b][0:2, 0:MCB], [(scT[:, k, :], wm[s][:, k, 0:MCB]) for k in range(KD)],
                     reads=[rsc, rwm[s]], writes=[rbank[b]])
                f.op("dve", lambda e: e.tensor_tensor(out=rowsM[:, cb * MCB:(cb + 1) * MCB], in0=banks[b][0:2, 0:MCB],
                                                      in1=bm[:, cb * MCB:(cb + 1) * MCB], op=ALU.add),
                     reads=[rbank[b], rbm], writes=[rrows])
            load_T(None, 2, 6 * D, modT[:, i, :, :], rows_tile=rowsM, rows_res=rrows)
        f.barrier()
    load_T(normw_in[:, :], depth * 2, D, normT)
    load_T(fnw_in[:, :], 1, D, fnwT)
    for i in range(depth):
        for j in range(2):
            sc = modT[:, i, (3 * j + 1) * KD:(3 * j + 2) * KD, :]
            nw = bc(normT[:, :, 2 * i + j], [(depth * 2, KD), (0, 2)])
            f.op("dve", lambda e: e.scalar_tensor_tensor(out=A12[:, i, j, :, :], in0=sc, scalar=1.0, in1=nw,
                                                         op0=ALU.add, op1=ALU.mult), reads=[rconst], writes=[rconst])
    f.barrier()

    def MOD(i, s, k, v):
        return modT[:, i, s * KD + k, v:v + 1]

    def vsplit(t0, n):
        return [(0, n, 1 if t0 < Lc else 0)]

    with contextlib.ExitStack() as st:
        posT = f.sb(st, [128, KD // 2, 64], F32, "posT")
        rpos = Res()
        f.dma("sp", posT[:], dap(pos_in, 0, [(64, 128), (128 * 64, KD // 2), (1, 64)]), writes=[rpos])
        xin = [f.sb(st, [128, 4, D], F32, "xin") for _ in range(2)]
        rxin = [Res(), Res()]
        xo = [f.sb(st, [128, KD, 512], F32, "xo") for _ in range(2)]
        rxo = [Res(), Res()]
        for bi, (t0, nb) in enumerate(c.blocks):
            s = bi % 2
            nt = nb // 128
            if t0 < Lc:
                src = ctx_in[t0:t0 + nb, :]
            else:
                src = x_in[t0 - Lc:t0 - Lc + nb, :]
            f.dma("sp", xin[s][:, 0:nt, :], src.rearrange("(a p) d -> p a d", p=128), writes=[rxin[s]])
            for k in range(KD):
                b = k % 8
                f.trs([(banks[b][:, a * 128:(a + 1) * 128], xin[s][:, a, k * 128:(k + 1) * 128]) for a in range(nt)],
                      ident_f[:], reads=[rxin[s], rconst], writes=[rbank[b]])
                if t0 < Lc:
                    f.op("act", lambda e: e.copy(out=xo[s][:, k, 0:nb], in_=banks[b][:, 0:nb]), reads=[rbank[b]], writes=[rxo[s]])
                else:
                    r0 = (t0 - Lc) // 64
                    nr = nb // 64
                    if k < KD // 2:
                        in1 = bc(posT[:, k, r0:r0 + nr], [(1, nr), (0, 64)])
                    else:
                        in1 = bc(posT[:, k - KD // 2, 0:64], [(0, nr), (1, 64)])
                    f.op("dve", lambda e: e.tensor_tensor(out=xo[s][:, k, 0:nb].rearrange("p (r c) -> p r c", c=64),
                                                          in0=banks[b][:, 0:nb].rearrange("p (r c) -> p r c", c=64),
                                                          in1=in1, op=ALU.add), reads=[rbank[b], rpos], writes=[rxo[s]])
            f.dma("pool", dap(XT, t0, [(T, 128), (128 * T, KD), (1, nb)]), xo[s][:, :, 0:nb], reads=[rxo[s]])
        f.barrier()

    def load_norm_block(st_tiles, i, j, t0, nb, want_x=True):
        xt, rxt, hn, rhn, sq, rsq, rstd, rrstd = st_tiles[:8]
        v = 1 if t0 < Lc else 0
        f.dma("sp", xt[:, :, 0:nb], dap(XT, t0, [(T, 128), (128 * T, KD), (1, nb)]), writes=[rxt])
        for k in range(KD):
            f.op("act", lambda e: e.activation(out=sq[:, k % 2, 0:nb], in_=xt[:, k, 0:nb], func=AF.Square),
                 reads=[rxt], writes=[rsq[k % 2]])
            f.mm(banks[0][:, 0:nb], [(ones_b[:], sq[:, k % 2, 0:nb])], reads=[rsq[k % 2], rconst],
                 writes=[rbank[0]], start=(k == 0), stop=(k == KD - 1))
        eps = 1e-6
        f.op("act", lambda e: e.activation(out=rstd[:, 0:nb], in_=banks[0][:, 0:nb], func=AF.Sqrt, bias=eps, scale=1.0 / D),
             reads=[rbank[0]], writes=[rrstd])
        f.op("dve", lambda e: e.reciprocal(out=rstd[:, 0:nb], in_=rstd[:, 0:nb]), reads=[rrstd], writes=[rrstd])
        if hn is None:
            return
        tmp = st_tiles[8]
        rtmp = st_tiles[9]
        for k in range(KD):
            s = k % 2
            f.op("dve", lambda e: e.scalar_tensor_tensor(out=tmp[:, s, 0:nb], in0=xt[:, k, 0:nb], scalar=A12[:, i, j, k, v:v + 1],
                                                         in1=rstd[:, 0:nb], op0=ALU.mult, op1=ALU.mult),
                 reads=[rxt, rrstd, rconst], writes=[rtmp[s]])
            f.op("act", lambda e: e.activation(out=hn[:, k, 0:nb], in_=tmp[:, s, 0:nb], func=AF.Identity,
                                               bias=MOD(i, 3 * j, k, v), scale=1.0),
                 reads=[rtmp[s], rconst], writes=[rhn])

    def alloc_norm_tiles(st, with_hn=True, hn_dtype=BF16):
        xt = f.sb(st, [128, KD, 512], F32, "xt")
        hn = f.sb(st, [128, KD, 512], hn_dtype, "hn") if with_hn else None
        sq = f.sb(st, [128, 2, 512], BF16, "sq")
        rstd = f.sb(st, [128, 512], F32, "rstd")
        tmp = f.sb(st, [128, 2, 512], F32, "tmpn")
        return [xt, Res(), hn, Res(), sq, [Res(), Res()], rstd, Res(), tmp, [Res(), Res()]]

    WSLOT = 3

    def alloc_ring(st, nk_max):
        ring = [f.sb(st, [128, nk_max, 512], BF16, "wring") for _ in range(WSLOT)]
        return {"t": ring, "r": [Res() for _ in range(WSLOT)], "n": 0}

    def wload(ring, wkey, row0, nk, col0, ncol):
        s = ring["n"] % WSLOT
        ring["n"] += 1
        full, tok, nkt = gathered[wkey]
        cb, off = col0 // WCB, col0 % WCB
        assert off + ncol <= WCB and row0 % 128 == 0
        f.dma("sp", ring["t"][s][:, 0:nk, 0:ncol],
              dap(full, cb * 128 * nkt * WCB + (row0 // 128) * WCB + off, [(nkt * WCB, 128), (WCB, nk), (1, ncol)]),
              writes=[ring["r"][s]], extra=list(tok))
        return ring["t"][s], ring["r"][s]

    def proj_residual(i, gslot, wkey, K_chunks, src, kmap=None, extra_scale=None, skip_ctx=False):
        with contextlib.ExitStack() as st:
            ring = alloc_ring(st, max(K_chunks, 1))
            sin = f.sb(st, [128, K_chunks, 512], BF16, "sin")
            rsin = Res()
            xt = f.sb(st, [128, KD, 512], F32, "xtp")
            rxt = Res()
            gs = f.sb(st, [128, KD, 2], F32, "gs")
            rgs = Res()
            if extra_scale is not None:
                f.op("dve", lambda e: e.tensor_tensor(out=gs[:], in0=modT[:, i, gslot * KD:(gslot + 1) * KD, :],
                                                      in1=bc(extra_scale[:, :, 0], [(1, KD), (0, 2)]), op=ALU.mult),
                     reads=[rconst], writes=[rgs])
            else:
                f.op("dve", lambda e: e.tensor_copy(out=gs[:], in_=modT[:, i, gslot * KD:(gslot + 1) * KD, :]),
                     reads=[rconst], writes=[rgs])
            for (t0, nb) in c.blocks:
                if skip_ctx and t0 < Lc:
                    continue
                v = 1 if t0 < Lc else 0
                f.dma("sp", sin[:, :, 0:nb], dap(src, t0, [(T, 128), (128 * T, K_chunks), (1, nb)]), writes=[rsin])
                f.dma("sp", xt[:, :, 0:nb], dap(XT, t0, [(T, 128), (128 * T, KD), (1, nb)]), writes=[rxt])
                mcols = min(512, D)
                for mb in range(D // mcols):
                    if kmap is None:
                        wt, rw = wload(ring, wkey, 0, K_chunks, mb * mcols, mcols)
                    for mm_ in range(mcols // 128):
                        m = mb * (mcols // 128) + mm_
                        b = m % 4
                        if kmap is None:
                            pairs = [(wt[:, k, mm_ * 128:(mm_ + 1) * 128], sin[:, k, 0:nb]) for k in range(K_chunks)]
                            f.mm(banks[b][:, 0:nb], pairs, reads=[rw, rsin], writes=[rbank[b]])
                        else:
                            ks, wrow0, wcol0 = kmap(m)
                            wt, rw = wload(ring, wkey, wrow0, len(ks), wcol0, 128)
                            pairs = [(wt[:, kk, 0:128], sin[:, k, 0:nb]) for kk, k in enumerate(ks)]
                            f.mm(banks[b][:, 0:nb], pairs, reads=[rw, rsin], writes=[rbank[b]])
                        f.op("dve", lambda e: e.scalar_tensor_tensor(out=xt[:, m, 0:nb], in0=banks[b][:, 0:nb], scalar=gs[:, m, v:v + 1],
                                                                     in1=xt[:, m, 0:nb], op0=ALU.mult, op1=ALU.add),
                             reads=[rbank[b], rgs], writes=[rxt])
                f.dma("pool", dap(XT, t0, [(T, 128), (128 * T, KD), (1, nb)]), xt[:, :, 0:nb], reads=[rxt])
            f.barrier()

    def ffn_phase(i, skip_ctx):
        HB = 512 if FH % 512 == 0 else 256
        with contextlib.ExitStack() as st:
            nt0 = alloc_norm_tiles(st)
            xt1 = f.sb(st, [128, KD, 512], F32, "xt1")
            nt1 = list(nt0)
            nt1[0], nt1[1] = xt1, Res()
            nts = [nt0, nt1]
            hn, rhn = nt0[2], nt0[3]
            ND = 3
            kd = (KH + ND - 1) // ND
            ring = alloc_ring(st, max(KD, kd))
            act = f.sb(st, [128, KH, 512], BF16, "act")
            ract = Res()
            sg = f.sb(st, [128, 2, 512], F32, "sg")
            rsg = [Res(), Res()]
            blks = [(t0, nb) for (t0, nb) in c.blocks if not (skip_ctx and t0 < Lc)]
            load_norm_block(nts[0], i, 1, blks[0][0], blks[0][1])
            for bi, (t0, nb) in enumerate(blks):
                nt = nts[bi % 2]
                xt, rxt = nt[0], nt[1]
                v = 1 if t0 < Lc else 0
                it = 0
                for jb in range(FH // HB):
                    wg, rwg = wload(ring, f"ffg{i}", 0, KD, jb * HB, HB)
                    wu, rwu = wload(ring, f"ffu{i}", 0, KD, jb * HB, HB)
                    for cc in range(HB // 128):
                        j = jb * (HB // 128) + cc
                        s = it % 2
                        it += 1
                        bg, bu = 4 + 2 * s, 5 + 2 * s
                        f.mm(banks[bg][:, 0:nb], [(wg[:, k, cc * 128:(cc + 1) * 128], hn[:, k, 0:nb]) for k in range(KD)],
                             reads=[rwg, rhn], writes=[rbank[bg]])
                        f.mm(banks[bu][:, 0:nb], [(wu[:, k, cc * 128:(cc + 1) * 128], hn[:, k, 0:nb]) for k in range(KD)],
                             reads=[rwu, rhn], writes=[rbank[bu]])
                        f.op("act", lambda e: e.activation(out=sg[:, s, 0:nb], in_=banks[bg][:, 0:nb], func=AF.Silu),
                             reads=[rbank[bg]], writes=[rsg[s]])
                        f.op("dve", lambda e: e.tensor_tensor(out=act[:, j, 0:nb], in0=sg[:, s, 0:nb], in1=banks[bu][:, 0:nb], op=ALU.mult),
                             reads=[rsg[s], rbank[bu]], writes=[ract])
                if bi + 1 < len(blks):
                    load_norm_block(nts[(bi + 1) % 2], i, 1, blks[bi + 1][0], blks[bi + 1][1])
                mcols = min(512, D)
                splits = [(a, min(KH, a + kd)) for a in range(0, KH, kd)]
                for mb in range(D // mcols):
                    nm = mcols // 128
                    for hi, (k0, k1) in enumerate(splits):
                        wd, rwd = wload(ring, f"ffd{i}", k0 * 128, k1 - k0, mb * mcols, mcols)
                        for mm_ in range(nm):
                            b = mm_ + 4 * (mb % 2)
                            f.mm(banks[b][:, 0:nb], [(wd[:, k - k0, mm_ * 128:(mm_ + 1) * 128], act[:, k, 0:nb]) for k in range(k0, k1)],
                                 reads=[rwd, ract], writes=[rbank[b]], start=(hi == 0), stop=(hi == len(splits) - 1))
                    for mm_ in range(nm):
                        m = mb * nm + mm_
                        b = mm_ + 4 * (mb % 2)
                        f.op("dve", lambda e: e.scalar_tensor_tensor(out=xt[:, m, 0:nb], in0=banks[b][:, 0:nb], scalar=MOD(i, 5, m, v),
                                                                     in1=xt[:, m, 0:nb], op0=ALU.mult, op1=ALU.add),
                             reads=[rbank[b], rconst], writes=[rxt])
                f.dma("pool", dap(XT, t0, [(T, 128), (128 * T, KD), (1, nb)]), xt[:, :, 0:nb], reads=[rxt])
            f.barrier()

    def final_phase():
        with contextlib.ExitStack() as st:
            nt = alloc_norm_tiles(st, with_hn=False)
            xt, rxt, rstd, rrstd = nt[0], nt[1], nt[6], nt[7]
            yt = f.sb(st, [128, KD, 512], F32, "yt")
            ryt = Res()
            ot = f.sb(st, [128, 4, D], F32, "ot")
            rot = Res()
            for (t0, nb) in c.blocks:
                if t0 < Lc:
                    continue
                load_norm_block(nt, 0, 0, t0, nb)
                for k in range(KD):
                    f.op("dve", lambda e: e.scalar_tensor_tensor(out=yt[:, k, 0:nb], in0=xt[:, k, 0:nb], scalar=fnwT[:, k, 0:1],
                                                                 in1=rstd[:, 0:nb], op0=ALU.mult, op1=ALU.mult),
                         reads=[rxt, rrstd, rconst], writes=[ryt])
                na = nb // 128
                for a in range(na):
                    for k0 in range(0, KD, 4):
                        b = (k0 // 4) % 8
                        f.trs([(banks[b][:, (k - k0) * 128:(k - k0 + 1) * 128], yt[:, k, a * 128:(a + 1) * 128]) for k in range(k0, min(KD, k0 + 4))],
                              ident_f[:], reads=[ryt, rconst], writes=[rbank[b]])
                        w = min(KD, k0 + 4) - k0
                        if (k0 // 4) % 2 == 0:
                            f.op("act", lambda e: e.copy(out=ot[:, a, k0 * 128:(k0 + w) * 128], in_=banks[b][:, 0:w * 128]),
                                 reads=[rbank[b]], writes=[rot])
                        else:
                            f.op("dve", lambda e: e.tensor_copy(out=ot[:, a, k0 * 128:(k0 + w) * 128], in_=banks[b][:, 0:w * 128]),
                                 reads=[rbank[b]], writes=[rot])
                f.dma("pool", out_ext[t0 - Lc:t0 - Lc + nb, :].rearrange("(a p) d -> p a d", p=128), ot[:, 0:na, :], reads=[rot])
            f.barrier()

    def pool_phase(i, j, need_ctx):
        with contextlib.ExitStack() as st0:
          psT = f.sb(st0, [128, KD, 1], F32, "psT")
          load_T(pscale_in[j], 1, D, psT)
          with contextlib.ExitStack() as st:
            rstd_all = f.sb(st, [128, T], F32, "rstdall")
            rra = Res()
            with contextlib.ExitStack() as st2:
                nt = alloc_norm_tiles(st2, with_hn=False)
                for (t0, nb) in c.blocks:
                    if (not need_ctx) and t0 < Lc:
                        continue
                    load_norm_block(nt, i, 0, t0, nb)
                    f.op("act", lambda e: e.copy(out=rstd_all[:, t0:t0 + nb], in_=nt[6][:, 0:nb]), reads=[nt[7]], writes=[rra])
                f.barrier()
            PADW = 8
            seqs = ([(0, Lc, 1)] if need_ctx else []) + [(Lc, L, 0)]
            xc = f.sb(st, [128, L], F32, "xc")
            rxc = Res()
            hp = f.sb(st, [128, L + 2 * PADW], F32, "hp")
            rhp = Res()
            wa = f.sb(st, [128, L + 2 * PADW], F32, "wa")
            wb = f.sb(st, [128, L + 2 * PADW], F32, "wb")
            rwa, rwb = Res(), Res()
            inv = f.sb(st, [128, L], F32, "inv")
            rinv = Res()
            po = f.sb(st, [128, L], BF16, "po")
            rpo = Res()
            for (s0, S, v) in seqs:
                for g, win in enumerate(POOL_WINDOWS):
                    f.op("dve", lambda e: e.memset(inv[:, 0:S], 1.0 / win), writes=[rinv])
                    for pos in list(range(win // 2)) + list(range(S - win // 2 + 1, S)):
                        lo, hi = max(pos - win // 2, 0), min(pos + win - win // 2, S)
                        val = 1.0 / (hi - lo)
                        f.op("dve", lambda e: e.memset(inv[:, pos:pos + 1], val), writes=[rinv])
                    for kk in range(GK):
                        k = g * GK + kk
                        f.dma("sp", xc[:, 0:S], dap(XT, k * 128 * T + s0, [(T, 128), (1, S)]), writes=[rxc])
                        f.op("dve", lambda e: e.memset(hp[:, 0:PADW], 0.0), writes=[rhp])
                        f.op("dve", lambda e: e.memset(hp[:, PADW + S:PADW + S + PADW], 0.0), writes=[rhp])
                        f.op("dve", lambda e: e.scalar_tensor_tensor(out=xc[:, 0:S], in0=xc[:, 0:S], scalar=A12[:, i, 0, k, v:v + 1],
                                                                     in1=rstd_all[:, s0:s0 + S], op0=ALU.mult, op1=ALU.mult),
                             reads=[rra, rconst], writes=[rxc])
                        f.op("act", lambda e: e.activation(out=hp[:, PADW:PADW + S], in_=xc[:, 0:S], func=AF.Identity,
                                                           bias=MOD(i, 0, k, v), scale=1.0), reads=[rxc, rconst], writes=[rhp])
                        cur, rcur = hp, rhp
                        width = S + 2 * PADW
                        step = 1
                        bufs = [(wa, rwa), (wb, rwb)]
                        bi = 0
                        while step < win:
                            nxt, rn = bufs[bi % 2]
                            bi += 1
                            nw = width - step
                            f.op("dve", lambda e: e.tensor_tensor(out=nxt[:, 0:nw], in0=cur[:, 0:nw], in1=cur[:, step:step + nw], op=ALU.add),
                                 reads=[rcur], writes=[rn])
                            cur, rcur, width = nxt, rn, nw
                            step *= 2
                        off = PADW - win // 2
                        f.op("dve", lambda e: e.tensor_tensor(out=xc[:, 0:S], in0=cur[:, off:off + S], in1=inv[:, 0:S], op=ALU.mult),
                             reads=[rcur, rinv], writes=[rxc])
                        f.op("dve", lambda e: e.tensor_tensor(out=po[:, 0:S], in0=xc[:, 0:S], in1=hp[:, PADW:PADW + S], op=ALU.subtract),
                             reads=[rxc, rhp], writes=[rpo])
                        f.dma("pool", dap(S_big, k * 128 * T + s0, [(T, 128), (1, S)]), po[:, 0:S], reads=[rpo])
            f.barrier()

          def kmap(m):
              g = m // GK
              return [g * GK + kk for kk in range(GK)], g * GD, (m % GK) * 128
          proj_residual(i, 2, f"pool{j}", KD, S_big, kmap=kmap, extra_scale=psT, skip_ctx=not need_ctx)

    def fourier_phase(i, j, need_ctx):
        ABd = dscr(f"ABd{i}", [D // 128, T, 2, 128], BF16).ap()
        with contextlib.ExitStack() as st:
            nt = alloc_norm_tiles(st)
            hn, rhn = nt[2], nt[3]
            cs = f.sb(st, [128, 2, GK, GD], BF16, "cs")
            rcs = Res()
            for a in range(2):
                f.dma("sp", cs[:, a, :, :], dap(dftc_in, a * GD * GD, [(GD, 128), (128 * GD, GK), (1, GD)]), writes=[rcs])
            ab = f.sb(st, [128, KD, 2, 128], BF16, "ab")
            rab = Res()
            for (t0, nb) in c.blocks:
                if (not need_ctx) and t0 < Lc:
                    continue
                load_norm_block(nt, i, 0, t0, nb)
                for a in range(nb // 128):
                    it = 0
                    for g in range(4):
                        for part in range(2):
                            b = 4 + it % 4
                            it += 1
                            f.mm(banks[b][:, 0:GD], [(hn[:, g * GK + kk, a * 128:(a + 1) * 128], cs[:, part, kk, :]) for kk in range(GK)],
                                 reads=[rhn, rcs], writes=[rbank[b]])
                            dst = ab[:, g * GK:(g + 1) * GK, part, :]
                            srcp = banks[b][:, 0:GD].rearrange("p (k n) -> p k n", n=128)
                            if part == 0:
                                f.op("act", lambda e: e.copy(out=dst, in_=srcp), reads=[rbank[b]], writes=[rab])
                            else:
                                f.op("dve", lambda e: e.tensor_copy(out=dst, in_=srcp), reads=[rbank[b]], writes=[rab])
                    tt = t0 + a * 128
                    f.dma("pool", dap(ABd, tt * 256, [(256, 128), (T * 256, KD), (1, 256)]), ab[:].rearrange("p k a n -> p k (a n)"), reads=[rab])
            f.barrier()
        with contextlib.ExitStack() as st:
            seqs = ([(0, Lc, dftLc_in, None)] if need_ctx else []) + [(Lc, L, None, None)]
            yt = f.sb(st, [128, 2, 512], BF16, "ytf")
            ryt = [Res(), Res()]
            for (s0, S, small_src, _) in seqs:
                nl = S // 128
                kb = min(512, S)
                scale = 1.0 / float(np.sqrt(S * GD))
                with contextlib.ExitStack() as st2:
                    slab = f.sb(st2, [128, 2, nl, kb], BF16, "slab")
                    rslab = Res()
                    abl = [f.sb(st2, [128, nl, 256], BF16, "abl") for _ in range(2)]
                    rabl = [Res(), Res()]
                    for k1b in range(S // kb):
                        for part in range(2):
                            if small_src is not None:
                                f.dma("sp", slab[:, part, :, :], dap(small_src, part * S * S + k1b * kb, [(S, 128), (128 * S, nl), (1, kb)]), writes=[rslab])
                            else:
                                g_ap, g_tok, _ = gathered[f"dftL{part}"]
                                f.dma("sp", slab[:, part, :, :], dap(g_ap, k1b * kb, [(S, 128), (128 * S, nl), (1, kb)]), writes=[rslab], extra=list(g_tok))
                        for q in range(KD):
                            s = q % 2
                            f.dma("sp", abl[s][:], dap(ABd, q * T * 256 + s0 * 256, [(256, 128), (128 * 256, nl), (1, 256)]), writes=[rabl[s]])
                            b = 4 + q % 4
                            pairs = []
                            for part in range(2):
                                for l in range(nl):
                                    pairs.append((abl[s][:, l, part * 128:(part + 1) * 128], slab[:, part, l, :]))
                            f.mm(banks[b][:, 0:kb], pairs, reads=[rabl[s], rslab], writes=[rbank[b]])
                            f.op("act", lambda e: e.activation(out=yt[:, s, 0:kb], in_=banks[b][:, 0:kb], func=AF.Copy, scale=scale),
                                 reads=[rbank[b]], writes=[ryt[s]])
                            f.dma("pool", dap(S_big, q * 128 * T + s0 + k1b * kb, [(T, 128), (1, kb)]), yt[:, s, 0:kb], reads=[ryt[s]])
                    f.barrier()
        proj_residual(i, 2, f"fou{j}", KD, S_big, skip_ctx=not need_ctx)

    def ssd_phase(i, j, with_ctx):
        KC = CONV // 128
        KI = DI // 128
        Ztok = dscr_shared("Ztok", [T, DI], BF16).ap()
        XBC = dscr_shared("XBC", [CONV, T], F32).ap()
        XStok = dscr_shared("XStok", [T, DI], BF16).ap()
        BCT = dscr_shared("BCT", [2 * G * 128, T], BF16).ap()
        Btok = dscr_shared("Btok", [T, G * 128], BF16).ap()
        CUMd = dscr_shared("CUMd", [2, NH, T], F32).ap()
        DTd = dscr_shared("DTd", [2, NH, T], F32).ap()
        Hd = dscr_shared("Hd", [2, NCH, 128, DI], BF16).ap()
        wkey = f"inp{j}"
        with contextlib.ExitStack() as stA:
            cwT = f.sb(stA, [128, KC, 5], F32, "cwT")
            cbT = f.sb(stA, [128, KC, 1], F32, "cbT")
            snT = f.sb(stA, [128, KI, 1], F32, "snT")
            load_T(convw_in[j], 5, CONV, cwT)
            load_T(convb_in[j], 1, CONV, cbT)
            load_T(ssdnw_in[j], 1, DI, snT)
            dtp = f.sb(stA, [NH, 2, 3], F32, "dtp")
            rdtp = Res()
            for d in range(2):
                f.dma("sp", dtp[:, d, 0:1], dtb_in[j, d * NH:(d + 1) * NH, :], writes=[rdtp])
                f.dma("sp", dtp[:, d, 1:2], alog_in[j, d * NH:(d + 1) * NH, :], writes=[rdtp])
            f.op("act", lambda e: e.activation(out=dtp[:, :, 1:2], in_=dtp[:, :, 1:2], func=AF.Exp), reads=[rdtp], writes=[rdtp])
            f.op("dve", lambda e: e.tensor_scalar(out=dtp[:, :, 1:2], in0=dtp[:, :, 1:2], scalar1=-1.0, scalar2=None, op0=ALU.mult), reads=[rdtp], writes=[rdtp])
            skb = f.sb(stA, [128, 3, NH], F32, "skb")
            rskb = Res()
            f.dma("sp", skb[:, 0:2, :], dap(dsk_in, j * 2 * NH, [(0, 128), (NH, 2), (1, NH)]), writes=[rskb])
            f.op("dve", lambda e: e.tensor_tensor(out=skb[:, 2, :], in0=skb[:, 0, :], in1=skb[:, 1, :], op=ALU.add), reads=[rskb], writes=[rskb])
            f.barrier()

            st_dt = contextlib.ExitStack()
            dtT = f.sb(st_dt, [NH, 2, T], F32, "dtT")
            rdt = Res()
            with contextlib.ExitStack() as st:
                nt = alloc_norm_tiles(st)
                hn, rhn = nt[2], nt[3]
                ring = alloc_ring(st, KD)
                zt = f.sb(st, [128, 2, 512], BF16, "zt")
                rzt = [Res(), Res()]
                xe = f.sb(st, [128, 2, 512], F32, "xe")
                rxe = [Res(), Res()]
                it = 0
                for (t0, nb) in c.blocks:
                    load_norm_block(nt, i, 0, t0, nb)
                    for cb0 in range(0, DI, 512):
                        ncb = min(512, DI - cb0)
                        wt, rw = wload(ring, wkey, 0, KD, cb0, ncb)
                        for a in range(nb // 128):
                            s = it % 2
                            it += 1
                            b = 4 + s
                            f.mm(banks[b][:, 0:ncb], [(hn[:, k, a * 128:(a + 1) * 128], wt[:, k, 0:ncb]) for k in range(KD)],
                                 reads=[rhn, rw], writes=[rbank[b]])
                            f.op("act", lambda e: e.copy(out=zt[:, s, 0:ncb], in_=banks[b][:, 0:ncb]), reads=[rbank[b]], writes=[rzt[s]])
                            f.dma("pool", Ztok[t0 + a * 128:t0 + (a + 1) * 128, cb0:cb0 + ncb], zt[:, s, 0:ncb], reads=[rzt[s]])
                    for cb0 in range(0, CONV, 512):
                        ncb = min(512, CONV - cb0)
                        wt, rw = wload(ring, wkey, 0, KD, DI + cb0, ncb)
                        for cc in range(ncb // 128):
                            s = it % 2
                            it += 1
                            b = 6 + s
                            f.mm(banks[b][:, 0:nb], [(wt[:, k, cc * 128:(cc + 1) * 128], hn[:, k, 0:nb]) for k in range(KD)],
                                 reads=[rhn, rw], writes=[rbank[b]])
                            f.op("dve", lambda e: e.tensor_copy(out=xe[:, s, 0:nb], in_=banks[b][:, 0:nb]), reads=[rbank[b]], writes=[rxe[s]])
                            f.dma("pool", XBC[cb0 + cc * 128:cb0 + (cc + 1) * 128, t0:t0 + nb], xe[:, s, 0:nb], reads=[rxe[s]])
                    wt, rw = wload(ring, wkey, 0, KD, DI + CONV, 2 * NH)
                    for d in range(2):
                        b = d
                        f.mm(banks[b][0:NH, 0:nb], [(wt[:, k, d * NH:(d + 1) * NH], hn[:, k, 0:nb]) for k in range(KD)],
                             reads=[rhn, rw], writes=[rbank[b]])
                        f.op("act", lambda e: e.activation(out=dtT[:, d, t0:t0 + nb], in_=banks[b][0:NH, 0:nb], func=AF.Exp,
                                                           bias=dtp[:, d, 0:1], scale=1.0), reads=[rbank[b], rdtp], writes=[rdt])
                f.op("act", lambda e: e.activation(out=dtT[:], in_=dtT[:], func=AF.Ln, bias=1.0, scale=1.0), reads=[rdt], writes=[rdt])
                f.dma("pool", dap(DTd, 0, [(T, NH), (NH * T, 2), (1, T)]), dtT[:], reads=[rdt])
                f.barrier()
            with contextlib.ExitStack() as st:
                ca = f.sb(st, [NH, 2, T], F32, "ca")
                cbuf = f.sb(st, [NH, 2, T], F32, "cbuf")
                rca, rcb = Res(), Res()
                for d in range(2):
                    f.op("dve", lambda e: e.tensor_scalar(out=ca[:, d, :], in0=dtT[:, d, :], scalar1=dtp[:, d, 1:2], scalar2=None, op0=ALU.mult),
                         reads=[rdt, rdtp], writes=[rca])
                cur, rcur, nxt, rnxt = ca, rca, cbuf, rcb
                step = 1
                while step < 128:
                    for d in range(2):
                        cv = cur[:, d, :].rearrange("p (c t) -> p c t", t=128)
                        nv = nxt[:, d, :].rearrange("p (c t) -> p c t", t=128)
                        if d == 0:
                            f.op("dve", lambda e: e.tensor_tensor(out=nv[:, :, step:128], in0=cv[:, :, step:128], in1=cv[:, :, 0:128 - step], op=ALU.add),
                                 reads=[rcur], writes=[rnxt])
                            f.op("dve", lambda e: e.tensor_copy(out=nv[:, :, 0:step], in_=cv[:, :, 0:step]), reads=[rcur], writes=[rnxt])
                        else:
                            f.op("dve", lambda e: e.tensor_tensor(out=nv[:, :, 0:128 - step], in0=cv[:, :, 0:128 - step], in1=cv[:, :, step:128], op=ALU.add),
                                 reads=[rcur], writes=[rnxt])
                            f.op("dve", lambda e: e.tensor_copy(out=nv[:, :, 128 - step:128], in_=cv[:, :, 128 - step:128]), reads=[rcur], writes=[rnxt])
                    cur, rcur, nxt, rnxt = nxt, rnxt, cur, rcur
                    step *= 2
                f.dma("pool", dap(CUMd, 0, [(T, NH), (NH * T, 2), (1, T)]), cur[:], reads=[rcur])
                f.barrier()
            st_dt.close()

            with contextlib.ExitStack() as st:
                GRP = 4
                raw = f.sb(st, [128, GRP, 516], F32, "raw")
                rraw = Res()
                acc = f.sb(st, [128, GRP, 512], F32, "cacc")
                racc = [Res() for _ in range(GRP)]
                so = f.sb(st, [128, GRP, 512], BF16, "so")
                rso = Res()
                tk = f.sb(st, [128, 4, GRP * 128], BF16, "tk")
                rtk = Res()
                for (t0, nb) in c.blocks:
                    s0, S = (0, Lc) if t0 < Lc else (Lc, L)
                    lo = 2 if t0 > s0 else 0
                    hi = 2 if t0 + nb < s0 + S else 0
                    for g0 in range(0, KC, GRP):
                        if lo == 0:
                            f.op("dve", lambda e: e.memset(raw[:, :, 0:2], 0.0), writes=[rraw])
                        if hi == 0:
                            f.op("dve", lambda e: e.memset(raw[:, :, 2 + nb:4 + nb], 0.0), writes=[rraw])
                        f.dma("sp", raw[:, :, 2 - lo:2 + nb + hi], dap(XBC, g0 * 128 * T + t0 - lo, [(T, 128), (128 * T, GRP), (1, nb + lo + hi)]), writes=[rraw])
                        for gg in range(GRP):
                            ch = g0 + gg
                            f.op("act", lambda e: e.activation(out=acc[:, gg, 0:nb], in_=raw[:, gg, 0:nb], func=AF.Identity,
                                                               bias=cbT[:, ch, 0:1], scale=cwT[:, ch, 0:1]), reads=[rraw, rconst], writes=[racc[gg]])
                        for w in range(1, 5):
                            for gg in range(GRP):
                                ch = g0 + gg
                                f.op("dve", lambda e: e.scalar_tensor_tensor(out=acc[:, gg, 0:nb], in0=raw[:, gg, w:w + nb], scalar=cwT[:, ch, w:w + 1],
                                                                             in1=acc[:, gg, 0:nb], op0=ALU.mult, op1=ALU.add),
                                     reads=[rraw, rconst], writes=[racc[gg]])
                        for gg in range(GRP):
                            f.op("act", lambda e: e.activation(out=so[:, gg, 0:nb], in_=acc[:, gg, 0:nb], func=AF.Silu), reads=[racc[gg]], writes=[rso])
                        is_x = g0 < KI
                        is_b = (g0 >= KI) and (g0 < KI + G)
                        if not is_x:
                            f.dma("pool", dap(BCT, (g0 - KI) * 128 * T + t0, [(T, 128), (128 * T, GRP), (1, nb)]), so[:, :, 0:nb], reads=[rso])
                        if is_x or is_b:
                            na = nb // 128
                            for a in range(na):
                                b = 4 + a % 4
                                pb = banks[b][:].bitcast(BF16)
                                f.trs([(pb[:, gg * 128:(gg + 1) * 128], so[:, gg, a * 128:(a + 1) * 128]) for gg in range(GRP)],
                                      ident_b[:], reads=[rso, rconst], writes=[rbank[b]])
                                if a % 2 == 0:
                                    f.op("act", lambda e: e.copy(out=tk[:, a, :], in_=pb[:, 0:GRP * 128]), reads=[rbank[b]], writes=[rtk])
                                else:
                                    f.op("dve", lambda e: e.tensor_copy(out=tk[:, a, :], in_=pb[:, 0:GRP * 128]), reads=[rbank[b]], writes=[rtk])
                            if is_x:
                                dst = dap(XStok, t0 * DI + g0 * 128, [(DI, 128), (128 * DI, na), (1, GRP * 128)])
                            else:
                                dst = dap(Btok, t0 * G * 128 + (g0 - KI) * 128, [(G * 128, 128), (128 * G * 128, na), (1, GRP * 128)])
                            f.dma("pool", dst, tk[:, 0:na, :], reads=[rtk])
                f.barrier()

            chunk_prep_cache = {}

            def chunk_small(st_t, cidx, d, want, b=0):
                sm, rsm = st_t["sm"], st_t["rsm"]
                cdt = st_t["cdt"]
                rcdt = st_t["rcdt"]
                t0 = cidx * 128
                f.dma("sp", cdt[:, 0, :], dap(CUMd, d * NH * T + t0, [(T, NH), (1, 128)]), writes=[rcdt])
                f.dma("sp", cdt[:, 1, :], dap(DTd, d * NH * T + t0, [(T, NH), (1, 128)]), writes=[rcdt])
                f.mms([(banks[b][:, 0:NH], [(cdt[:, 0, :], ident_f[0:NH, 0:NH])]),
                       (banks[b][:, NH:2 * NH], [(cdt[:, 1, :], ident_f[0:NH, 0:NH])])], reads=[rcdt, rconst], writes=[rbank[b]])
                f.op("dve", lambda e: e.tensor_copy(out=sm[:, 0:2, :], in_=banks[b][:, 0:2 * NH].rearrange("p (a h) -> p a h", h=NH)),
                     reads=[rbank[b]], writes=[rsm])
                f.op("dve", lambda e: e.tensor_scalar(out=sm[:, 2, :], in0=sm[:, 0, :], scalar1=-1.0, scalar2=None, op0=ALU.mult), reads=[rsm], writes=[rsm])
                f.op("act", lambda e: e.activation(out=sm[:, 3, :], in_=sm[:, 0, :], func=AF.Exp), reads=[rsm], writes=[rsm])
                if want == "state":
                    selm = tri_f[:, 2 + d, :]
                    f.mm(banks[b][:, 2 * NH:3 * NH], [(selm, sm[:, 0, :])], reads=[rsm, rconst], writes=[rbank[b]])
                    f.op("dve", lambda e: e.tensor_copy(out=sm[:, 4, :], in_=banks[b][:, 2 * NH:3 * NH]), reads=[rbank[b]], writes=[rsm])
                    f.op("dve", lambda e: e.tensor_tensor(out=sm[:, 5, :], in0=sm[:, 4, :], in1=sm[:, 0, :], op=ALU.subtract), reads=[rsm], writes=[rsm])
                    f.op("act", lambda e: e.activation(out=sm[:, 5, :], in_=sm[:, 5, :], func=AF.Exp), reads=[rsm], writes=[rsm])
                    f.op("act", lambda e: e.activation(out=sm[:, 6, :], in_=sm[:, 4, :], func=AF.Exp), reads=[rsm], writes=[rsm])
                    f.op("dve", lambda e: e.tensor_tensor(out=sm[:, 7, :], in0=sm[:, 5, :], in1=sm[:, 1, :], op=ALU.mult), reads=[rsm], writes=[rsm])

            with contextlib.ExitStack() as st:
                tls = [{"sm": f.sb(st, [128, 8, NH], F32, "sm"), "rsm": Res(), "cdt": f.sb(st, [NH, 2, 128], F32, "cdt"), "rcdt": Res()} for _ in range(2)]
                xsD = [f.sb(st, [128, DI], BF16, "xs") for _ in range(2)]
                rxsD = [Res(), Res()]
                btD = [f.sb(st, [128, G * 128], BF16, "bt") for _ in range(2)]
                rbtD = [Res(), Res()]
                wxD = [f.sb(st, [128, DI], BF16, "wx") for _ in range(2)]
                rwxD = [Res(), Res()]
                H = [f.sb(st, [128, DI], F32, "H") for _ in range(2)]
                rH = [Res(), Res()]
                HbD = [f.sb(st, [128, DI], BF16, "Hb") for _ in range(2)]
                rHbD = [Res(), Res()]
                orders = [list(range(NCH)), [1, 0] + list(range(NCH - 1, 1, -1))]
                for d in range(2):
                    f.op("dve", lambda e: e.memset(H[d][:], 0.0), writes=[rH[d]])
                for step_i in range(NCH):
                    for d in range(2):
                        tl = tls[d]
                        sm, rsm = tl["sm"], tl["rsm"]
                        xs, rxs, bt, rbt, wx, rwx, Hb, rHb = xsD[d], rxsD[d], btD[d], rbtD[d], wxD[d], rwxD[d], HbD[d], rHbD[d]
                        cidx = orders[d][step_i]
                        t0 = cidx * 128
                        f.op("act", lambda e: e.copy(out=Hb[:], in_=H[d][:]), reads=[rH[d]], writes=[rHb])
                        f.dma("pool", Hd[d, cidx, :, :], Hb[:], reads=[rHb])
                        if step_i == NCH - 1:
                            continue
                        chunk_small(tl, cidx, d, "state", b=d)
                        f.dma("sp", xs[:], XStok[t0:t0 + 128, :], writes=[rxs])
                        f.dma("sp", bt[:], Btok[t0:t0 + 128, :], writes=[rbt])
                        f.op("dve", lambda e: e.tensor_tensor(out=wx[:].rearrange("p (h q) -> p h q", q=64), in0=xs[:].rearrange("p (h q) -> p h q", q=64),
                                                              in1=bc(sm[:, 7, :], [(1, NH), (0, 64)]), op=ALU.mult), reads=[rxs, rsm], writes=[rwx])
                        ngb = max(1, 512 // EP)
                        for g0 in range(0, G, ngb):
                            b = 2 + ((g0 // ngb) + 3 * d) % 6
                            groups = [(banks[b][:, (g - g0) * EP:(g - g0 + 1) * EP], [(bt[:, g * 128:(g + 1) * 128], wx[:, g * EP:(g + 1) * EP])])
                                      for g in range(g0, g0 + ngb)]
                            f.mms(groups, reads=[rbt, rwx], writes=[rbank[b]])
                            wdt = ngb * EP
                            hv = H[d][:, g0 * EP:g0 * EP + wdt]
                            h0 = g0 * E
                            nh = ngb * E
                            f.op("dve", lambda e: e.tensor_tensor(out=hv.rearrange("p (h q) -> p h q", q=64), in0=hv.rearrange("p (h q) -> p h q", q=64),
                                                                  in1=bc(sm[:, 6, h0:h0 + nh], [(1, nh), (0, 64)]), op=ALU.mult), reads=[rsm], writes=[rH[d]])
                            f.op("dve", lambda e: e.tensor_tensor(out=hv, in0=hv, in1=banks[b][:, 0:wdt], op=ALU.add), reads=[rbank[b]], writes=[rH[d]])
                f.barrier()

            with contextlib.ExitStack() as st:
                tls = [{"sm": f.sb(st, [128, 8, NH], F32, "sm"), "rsm": Res(), "cdt": f.sb(st, [NH, 2, 128], F32, "cdt"), "rcdt": Res()} for _ in range(2)]
                xsP = [f.sb(st, [128, DI], BF16, "xs") for _ in range(2)]
                rxsP = [Res(), Res()]
                zzP = [f.sb(st, [128, DI], BF16, "zz") for _ in range(2)]
                rzzP = [Res(), Res()]
                bctP = [f.sb(st, [128, 2 * G, 128], BF16, "bct") for _ in range(2)]
                rbctP = [Res(), Res()]
                GmP = [f.sb(st, [128, 2, G, 128], F32, "Gm") for _ in range(2)]
                rGmP = [Res(), Res()]
                sz = f.sb(st, [128, DI], F32, "sz")
                rsz = Res()
                HsD = [f.sb(st, [128, DI], BF16, "Hs") for _ in range(2)]
                rHsD = [Res(), Res()]
                HPC = min(16, NH)
                cumb = [f.sb(st, [128, HPC, 128], F32, "cumb") for _ in range(2)]
                rcumb = [Res(), Res()]
                xdtD = [f.sb(st, [128, DI], BF16, "xdt") for _ in range(2)]
                rxdtD = [Res(), Res()]
                yacc = f.sb(st, [128, DI], F32, "yacc")
                ryacc = Res()
                Eb = [f.sb(st, [128, E, 128], F32, "Eb") for _ in range(3)]
                rEb = [Res(), Res(), Res()]
                Mb = [f.sb(st, [128, E, 128], BF16, "Mb") for _ in range(3)]
                rMb = [Res(), Res(), Res()]
                tmpyP = [f.sb(st, [128, EP], F32, "tmpy") for _ in range(3)]
                rtmpyP = [Res(), Res(), Res()]
                PAIRS = [(2, 3), (4, 5), (6, 7)]
                cs_cur = 0
                ssq = f.sb(st, [128, 2, G], F32, "ssq")
                rssq = Res()
                gn = f.sb(st, [128, DI], BF16, "gn")
                rgn = Res()
                gT = f.sb(st, [128, KI, 128], BF16, "gT")
                rgT = Res()
                junk = f.sb(st, [128, DI // G], F32, "junk")
                rjunk = Res()
                gi = 0
                ci = 0
                pc = 0
                for cidx in range(NCH):
                    t0 = cidx * 128
                    if t0 < Lc and not with_ctx:
                        continue
                    par = ci % 2
                    ci += 1
                    xs, rxs, zz, rzz, bct, rbct, Gm, rGm = xsP[par], rxsP[par], zzP[par], rzzP[par], bctP[par], rbctP[par], GmP[par], rGmP[par]
                    f.dma("sp", xs[:], XStok[t0:t0 + 128, :], writes=[rxs])
                    f.dma("sp", zz[:], Ztok[t0:t0 + 128, :], writes=[rzz])
                    f.dma("sp", bct[:], dap(BCT, t0, [(T, 128), (128 * T, 2 * G), (1, 128)]), writes=[rbct])
                    for d in range(2):
                        f.dma("sp", HsD[d][:], Hd[d, cidx, :, :], writes=[rHsD[d]])
                    groups = [(banks[0][:, g * 128:(g + 1) * 128] if g < 4 else banks[1][:, (g - 4) * 128:(g - 3) * 128],
                               [(bct[:, g, :], bct[:, G + g, :])]) for g in range(G)]
                    f.mms(groups, reads=[rbct], writes=[rbank[0], rbank[1]])
                    for d in range(2):
                        for hb in range(2):
                            f.op("dve", lambda e: e.tensor_tensor(out=Gm[:, d, hb * 4:(hb + 1) * 4, :], in0=banks[hb][:, :].rearrange("p (g l) -> p g l", l=128),
                                                                  in1=bc(tri_f[:, d, :], [(0, 4), (1, 128)]), op=ALU.mult),
                                 reads=[rbank[hb], rconst], writes=[rGm])
                    for d in range(2):
                        chunk_small(tls[d], cidx, d, "out", b=d)
                        smd = tls[d]["sm"]
                        f.op("dve", lambda e: e.tensor_tensor(out=xdtD[d][:].rearrange("p (h q) -> p h q", q=64), in0=xs[:].rearrange("p (h q) -> p h q", q=64),
                                                              in1=bc(smd[:, 1, :], [(1, NH), (0, 64)]), op=ALU.mult), reads=[rxs, tls[d]["rsm"]], writes=[rxdtD[d]])
                    def taskA(d, g, s):
                        nonlocal pc, cs_cur
                        sm, rsm = tls[d]["sm"], tls[d]["rsm"]
                        xdt, rxdt, Hs, rHs = xdtD[d], rxdtD[d], HsD[d], rHsD[d]
                        h0 = g * E
                        if h0 % HPC == 0:
                            cs_cur = pc % 2
                            pc += 1
                            f.dma("sp", cumb[cs_cur][:], dap(CUMd, (d * NH + h0) * T + t0, [(0, 128), (T, HPC), (1, 128)]), writes=[rcumb[cs_cur]])
                        cs_ = cs_cur
                        for e_ in range(E):
                            h = h0 + e_
                            f.op("act", lambda e: e.activation(out=Eb[s][:, e_, :], in_=cumb[cs_][:, h % HPC, :], func=AF.Exp,
                                                               bias=sm[:, 2, h:h + 1], scale=1.0), reads=[rcumb[cs_], rsm], writes=[rEb[s]])
                        f.op("dve", lambda e: e.scalar_tensor_tensor(out=Mb[s][:], in0=Eb[s][:], scalar=1.0, in1=bc(Gm[:, d, g, :], [(0, E), (1, 128)]),
                                                                     op0=ALU.min, op1=ALU.mult), reads=[rEb[s], rGm], writes=[rMb[s]])
                        bi_, bx_ = PAIRS[s]
                        groups = [(banks[bi_][:, e_ * 64:(e_ + 1) * 64], [(Mb[s][:, e_, :], xdt[:, (h0 + e_) * 64:(h0 + e_ + 1) * 64])]) for e_ in range(E)]
                        f.mms(groups, reads=[rMb[s], rxdt], writes=[rbank[bi_]])
                        f.mm(banks[bx_][:, 0:EP], [(bct[:, G + g, :], Hs[:, g * EP:(g + 1) * EP])], reads=[rbct, rHs], writes=[rbank[bx_]])

                    def taskB(d, g, s):
                        sm, rsm = tls[d]["sm"], tls[d]["rsm"]
                        h0 = g * E
                        bi_, bx_ = PAIRS[s]
                        tmpy, rtmpy = tmpyP[s], rtmpyP[s]
                        yv = yacc[:, g * EP:(g + 1) * EP]
                        f.op("dve", lambda e: e.tensor_tensor(out=tmpy[:].rearrange("p (h q) -> p h q", q=64), in0=banks[bx_][:, 0:EP].rearrange("p (h q) -> p h q", q=64),
                                                              in1=bc(sm[:, 3, h0:h0 + E], [(1, E), (0, 64)]), op=ALU.mult), reads=[rbank[bx_], rsm], writes=[rtmpy])
                        if d == 0:
                            f.op("dve", lambda e: e.tensor_tensor(out=yv, in0=tmpy[:], in1=banks[bi_][:, 0:EP], op=ALU.add), reads=[rtmpy, rbank[bi_]], writes=[ryacc])
                        else:
                            f.op("dve", lambda e: e.tensor_tensor(out=tmpy[:], in0=tmpy[:], in1=banks[bi_][:, 0:EP], op=ALU.add), reads=[rtmpy, rbank[bi_]], writes=[rtmpy])
                            f.op("dve", lambda e: e.tensor_tensor(out=yv, in0=yv, in1=tmpy[:], op=ALU.add), reads=[rtmpy], writes=[ryacc])

                    pend = None
                    for d in range(2):
                        for g in range(G):
                            s = gi % 3
                            gi += 1
                            taskA(d, g, s)
                            if pend is not None:
                                taskB(*pend)
                            pend = (d, g, s)
                    taskB(*pend)
                    f.op("dve", lambda e: e.tensor_tensor(out=sz[:].rearrange("p (h q) -> p h q", q=64), in0=xs[:].rearrange("p (h q) -> p h q", q=64),
                                                          in1=bc(skb[:, 2, :], [(1, NH), (0, 64)]), op=ALU.mult), reads=[rxs, rskb], writes=[rsz])
                    f.op("dve", lambda e: e.tensor_tensor(out=yacc[:], in0=yacc[:], in1=sz[:], op=ALU.add), reads=[rsz], writes=[ryacc])
                    f.op("act", lambda e: e.activation(out=sz[:], in_=zz[:], func=AF.Silu), reads=[rzz], writes=[rsz])
                    f.op("dve", lambda e: e.tensor_tensor(out=yacc[:], in0=yacc[:], in1=sz[:], op=ALU.mult), reads=[rsz], writes=[ryacc])
                    gw = DI // G
                    f.op("dve", lambda e: e.memset(ssq[:], 0.0), writes=[rssq])
                    for g in range(G):
                        f.op("act", lambda e: e.activation(out=junk[:], in_=yacc[:, g * gw:(g + 1) * gw], func=AF.Square, accum_out=ssq[:, 0, g:g + 1]),
                             reads=[ryacc], writes=[rjunk, rssq])
                    f.op("act", lambda e: e.activation(out=ssq[:, 1, :], in_=ssq[:, 0, :], func=AF.Sqrt, bias=1e-5, scale=1.0 / gw), reads=[rssq], writes=[rssq])
                    f.op("dve", lambda e: e.reciprocal(out=ssq[:, 1, :], in_=ssq[:, 1, :]), reads=[rssq], writes=[rssq])
                    f.op("dve", lambda e: e.tensor_tensor(out=gn[:].rearrange("p (g q) -> p g q", q=gw), in0=yacc[:].rearrange("p (g q) -> p g q", q=gw),
                                                          in1=bc(ssq[:, 1, :], [(1, G), (0, gw)]), op=ALU.mult), reads=[ryacc, rssq], writes=[rgn])
                    for k0 in range(0, KI, 4):
                        b = (k0 // 4) % 2
                        pb = banks[b][:].bitcast(BF16)
                        f.trs([(pb[:, (k - k0) * 128:(k - k0 + 1) * 128], gn[:, k * 128:(k + 1) * 128]) for k in range(k0, k0 + 4)],
                              ident_b[:], reads=[rgn, rconst], writes=[rbank[b]])
                        f.op("dve", lambda e: e.tensor_tensor(out=gT[:, k0:k0 + 4, :], in0=pb[:, 0:512].rearrange("p (k t) -> p k t", t=128),
                                                              in1=bc(snT[:, k0:k0 + 4, 0], [(1, 4), (0, 128)]), op=ALU.mult), reads=[rbank[b], rconst], writes=[rgT])
                    f.dma("pool", dap(S_big, t0, [(T, 128), (128 * T, KI), (1, 128)]), gT[:], reads=[rgT])
                f.barrier()
        proj_residual(i, 2, f"outp{j}", KI, S_big, skip_ctx=not with_ctx)

    for i in range(depth):
        kind, j = i % 3, i // 3
        last = (i == depth - 1)
        if i + 1 < depth:
            prep_layer_weights(i + 1)
        if kind == 0:
            ssd_phase(i, j, not last)
        elif kind == 1:
            fourier_phase(i, j, not last)
        else:
            pool_phase(i, j, not last)
        ffn_phase(i, skip_ctx=last)
    final_phase()
    f.barrier()
    f.es.close()
    return nc


def host_consts(cfg):
    c = cfg
    D, L, Lc, GD = c.D, c.L, c.Lc, c.GD
    quarter = D // 4
    omega = (1.0 / (np.float32(10000.0) ** (np.arange(quarter, dtype=np.float32) / np.float32(quarter)))).astype(np.float32)
    kk = np.arange(64, dtype=np.float32)[:, None]
    Tt = np.concatenate([np.sin(kk * omega), np.cos(kk * omega)], axis=-1).astype(np.float32)
    posT = np.ascontiguousarray(Tt.T)
    s = np.arange(128)[:, None]
    l = np.arange(128)[None, :]
    tri = np.zeros((128, 4, 128), np.float32)
    tri[:, 0, :] = (s <= l)
    tri[:, 1, :] = (s >= l)
    tri[127, 2, :] = 1.0
    tri[0, 3, :] = 1.0

    def dft(n):
        idx = np.arange(n, dtype=np.int64)
        ang = 2.0 * np.pi * ((idx[:, None] * idx[None, :]) % n).astype(np.float64) / n
        return np.cos(ang), np.sin(ang)
    cc, sc = dft(GD)
    cL, sL = dft(L)
    cLc, sLc = dft(Lc)
    bf = ml_dtypes.bfloat16
    return {
        "posT": posT, "tri": tri, "ident": np.eye(128, dtype=np.float32),
        "dftc": np.stack([cc, sc]).astype(np.float32).astype(bf),
        "dftL": np.stack([cL, -sL]).astype(np.float32).astype(bf),
        "dftLc": np.stack([cLc, -sLc]).astype(np.float32).astype(bf),
    }


def make_in_maps(cfg, inp):
    c = cfg
    D, NH = c.D, c.NH
    hc = host_consts(c)
    f32 = np.float32
    A = {k: np.asarray(v) for k, v in inp.items()}

    pw = A["pool_w"]
    pw2 = pw.reshape(pw.shape[0], 4 * c.GD, c.GD) if pw.shape[0] > 0 else np.zeros((1, 4 * c.GD, c.GD), f32)
    fw = A["fourier_w_out"] if A["fourier_w_out"].shape[0] > 0 else np.zeros((1, D, D), f32)
    ps = A["pool_scale"] if A["pool_scale"].shape[0] > 0 else np.zeros((1, D), f32)
    shared = {
        "ident": hc["ident"], "posT": hc["posT"], "tri": hc["tri"],
        "wmod": A["w_mod"], "bmod": A["b_mod"], "normw": A["norm_w"].reshape(c.depth * 2, D), "fnw": A["final_norm_w"].reshape(1, D),
        "ffg": A["ffn_w_gate"], "ffu": A["ffn_w_up"], "ffd": A["ffn_w_down"],
        "inproj": A["ssd_in_proj"], "convw": A["ssd_conv_w"], "convb": A["ssd_conv_b"][:, None, :],
        "dtb": A["ssd_dt_bias"].reshape(c.NSSD, 2 * NH, 1), "alog": A["ssd_a_log"].reshape(c.NSSD, 2 * NH, 1),
        "dskip": A["ssd_d"], "ssdnw": A["ssd_norm_w"][:, None, :], "outproj": A["ssd_out_proj"],
        "fouw": fw, "poolw": pw2, "pscale": ps[:, None, :],
        "dftc": hc["dftc"], "dftL": hc["dftL"], "dftLc": hc["dftLc"],
    }
    shared = {k: np.ascontiguousarray(v) for k, v in shared.items()}
    maps = []
    for r in range(NCORES):
        m = dict(shared)
        m["x"] = np.ascontiguousarray(A["x"][r])
        m["ctx"] = np.ascontiguousarray(A["ctx"][r])
        m["c2"] = np.ascontiguousarray(np.stack([A["c"][r], A["c_ctx"]], axis=0).astype(f32))
        maps.append(m)
    return maps


_NC_CACHE = {}


def run(cfg, inp):
    key = (cfg.D, cfg.L, cfg.Lc, cfg.depth)
    if key not in _NC_CACHE:
        _NC_CACHE[key] = build(cfg)
    nc = _NC_CACHE[key]
    maps = make_in_maps(cfg, inp)
    res = run_bass_kernel_spmd(nc, maps, core_ids=list(range(NCORES)))
    return np.stack([res.results[r]["out"] for r in range(NCORES)], axis=0)


def kernel(**inputs):
    cfg = Cfg(2048, 4096, 256, 4)
    return run(cfg, inputs).astype(np.float32)
```

```python
import contextlib
import numpy as np
import ml_dtypes
import concourse.bass as bass
import concourse.mybir as mybir
from concourse.bass_utils import run_bass_kernel_spmd

F32 = mybir.dt.float32
BF16 = mybir.dt.bfloat16
ALU = mybir.AluOpType
AF = mybir.ActivationFunctionType
AX = mybir.AxisListType
NCORES = 8
POOL_WINDOWS = (2, 4, 8, 16)


class Cfg:
    def __init__(s, D=2048, L=4096, Lc=256, depth=4):
        s.D, s.L, s.Lc, s.T, s.depth = D, L, Lc, L + Lc, depth
        s.KD = D // 128
        s.DI = 2 * D
        s.NH = s.DI // 64
        s.G = 8
        s.E = s.NH // 8
        s.EP = s.E * 64
        s.CONV = s.DI + 2 * s.G * 128
        s.INP = s.DI + s.CONV + 2 * s.NH
        s.FH = ((8 * D // 3 + 255) // 256) * 256
        s.KH = s.FH // 128
        s.GD = D // 4
        s.GK = s.GD // 128
        s.MC = 6 * D // 8
        s.CB = s.MC // 3
        s.NSSD = (depth + 2) // 3
        s.NFOU = (depth + 1) // 3
        s.NPOOL = depth // 3
        s.NCH = s.T // 128
        s.blocks = [(0, Lc)] + [(Lc + 512 * i, 512) for i in range(L // 512)]


class Res:
    __slots__ = ("w", "readers", "wprev")

    def __init__(self):
        self.w = None
        self.readers = []
        self.wprev = []


class FW:
    NDMA = 24

    def __init__(self, nc):
        self.nc = nc
        self.es = contextlib.ExitStack()
        self.eng = {"pe": nc.tensor, "act": nc.scalar, "dve": nc.vector, "pool": nc.gpsimd, "sp": nc.sync}
        self.sem, self.cnt = {}, {}
        for e in ("pe", "act", "dve", "pool"):
            self.sem[e] = self.es.enter_context(nc.semaphore("s_" + e))
            self.cnt[e] = 0
        self.dsem, self.dcnt, self.dnext = {}, {}, {}
        for q in ("sp", "pool"):
            self.dsem[q] = [self.es.enter_context(nc.semaphore(f"d_{q}{i}")) for i in range(self.NDMA)]
            self.dcnt[q] = [0] * self.NDMA
            self.dnext[q] = 0
        self.known = {}
        self.uid = 0
        self.cc_toks = []

    def sb(self, stack, shape, dtype, name="t"):
        self.uid += 1
        return stack.enter_context(self.nc.sbuf_tensor(f"{name}_{self.uid}", list(shape), dtype))

    def ps(self, stack, shape, dtype, name="p"):
        self.uid += 1
        return stack.enter_context(self.nc.psum_tensor(f"{name}_{self.uid}", list(shape), dtype))

    def _wait(self, e, tok):
        if tok is None:
            return
        sem, val = tok
        key = (e, id(sem))
        if self.known.get(key, 0) >= val:
            return
        self.known[key] = val
        self.eng[e].wait_ge(sem, val)

    def _deps(self, e, reads, writes, extra=(), nowaw=False):
        for t in extra:
            self._wait(e, t)
        for r in reads:
            self._wait(e, r.w)
            for t in r.wprev:
                self._wait(e, t)
        own = self.sem.get(e)
        for w in writes:
            if not (nowaw and w.w is not None and w.w[0] is own):
                self._wait(e, w.w)
            for t in w.wprev:
                self._wait(e, t)
            for t in w.readers:
                self._wait(e, t)

    def _post(self, tok, reads, writes):
        for r in reads:
            r.readers.append(tok)
        for w in writes:
            if w.w is not None and w.w[0] is not tok[0]:
                w.wprev.append(w.w)
                if len(w.wprev) > 64:
                    w.wprev = w.wprev[-64:]
            w.w = tok
            w.readers = []

    def _sig(self, e, ins, reads, writes):
        self.cnt[e] += 1
        ins.then_inc(self.sem[e], 1)
        tok = (self.sem[e], self.cnt[e])
        self._post(tok, reads, writes)
        return tok

    def op(self, e, fn, reads=(), writes=(), nowaw=False):
        self._deps(e, reads, writes, nowaw=nowaw)
        return self._sig(e, fn(self.eng[e]), reads, writes)

    def mm(self, out_ap, pairs, reads=(), writes=(), start=True, stop=True, nowaw=False):
        self._deps("pe", reads, writes, nowaw=nowaw)
        n = len(pairs)
        ins = None
        for i, (l, r) in enumerate(pairs):
            ins = self.nc.tensor.matmul(out_ap, l, r, start=(start and i == 0), stop=(stop and i == n - 1))
        return self._sig("pe", ins, reads, writes)

    def mms(self, groups, reads=(), writes=()):
        self._deps("pe", reads, writes)
        ins = None
        for out_ap, pairs in groups:
            n = len(pairs)
            for i, (l, r) in enumerate(pairs):
                ins = self.nc.tensor.matmul(out_ap, l, r, start=(i == 0), stop=(i == n - 1))
        return self._sig("pe", ins, reads, writes)

    def trs(self, items, ident, reads=(), writes=()):
        self._deps("pe", reads, writes)
        ins = None
        for o, i in items:
            ins = self.nc.tensor.transpose(o, i, ident)
        return self._sig("pe", ins, reads, writes)

    def dma(self, q, out, in_, reads=(), writes=(), extra=()):
        try:
            so, si = tuple(out.shape), tuple(in_.shape)
        except Exception:
            so, si = None, None
        if so is not None and so == si and len(so) >= 3:
            ndesc = so[0] * int(np.prod(so[1:-1]))
            if ndesc > 1024 and so[1] > 1:
                per = max(1, so[1] * 1024 // ndesc)
                tok = None
                for a in range(0, so[1], per):
                    b = min(so[1], a + per)
                    tok = self._dma1(q, out[:, a:b], in_[:, a:b], reads, writes, extra)
                return tok
        return self._dma1(q, out, in_, reads, writes, extra)

    def _dma1(self, q, out, in_, reads=(), writes=(), extra=()):
        i = self.dnext[q]
        self.dnext[q] = (i + 1) % self.NDMA
        sem = self.dsem[q][i]
        if self.dcnt[q][i] > 0:
            self._wait(q, (sem, self.dcnt[q][i]))
        self._deps(q, reads, writes, extra)
        ins = self.eng[q].dma_start(out=out, in_=in_)
        self.dcnt[q][i] += 16
        ins.then_inc(sem, 16)
        tok = (sem, self.dcnt[q][i])
        self._post(tok, reads, writes)
        return tok

    def barrier(self):
        toks = [(self.sem[e], self.cnt[e]) for e in self.sem if self.cnt[e] > 0]
        for q in self.dsem:
            for i, s in enumerate(self.dsem[q]):
                if self.dcnt[q][i] > 0:
                    toks.append((s, self.dcnt[q][i]))
        for e in ("pe", "act", "dve", "pool", "sp"):
            for t in toks:
                self._wait(e, t)


def bc(ap, steps):
    return bass.AP(ap.tensor, ap.offset, [list(ap.ap[0])] + [list(x) for x in steps])


def dap(t, offset, dims):
    a = t if isinstance(t, bass.AP) else t.ap()
    return bass.AP(a.tensor, a.offset + offset, [list(x) for x in dims])


def build(cfg):
    c = cfg
    D, L, Lc, T, KD, DI, NH, G, E, EP = c.D, c.L, c.Lc, c.T, c.KD, c.DI, c.NH, c.G, c.E, c.EP
    CONV, INP, FH, KH, GD, GK, MC, CB, depth = c.CONV, c.INP, c.FH, c.KH, c.GD, c.GK, c.MC, c.CB, c.depth
    NCH = c.NCH
    nc = bass.Bass("TRN2", target_bir_lowering=False)
    f = FW(nc)

    def din(name, shape, dt=F32):
        return nc.dram_tensor(name, list(shape), dt, kind="ExternalInput").ap()

    x_in = din("x", [L, D])
    ctx_in = din("ctx", [Lc, D])
    c2_in = din("c2", [2, D])
    ident_in = din("ident", [128, 128])
    pos_in = din("posT", [D // 2, 64])
    tri_in = din("tri", [128, 4, 128])
    wmod_in = din("wmod", [depth, D, 6 * D])
    bmod_in = din("bmod", [depth, 6 * D])
    normw_in = din("normw", [depth * 2, D])
    fnw_in = din("fnw", [1, D])
    ffg_in = din("ffg", [depth, D, FH])
    ffu_in = din("ffu", [depth, D, FH])
    ffd_in = din("ffd", [depth, FH, D])
    inp_in = din("inproj", [c.NSSD, D, INP])
    convw_in = din("convw", [c.NSSD, 5, CONV])
    convb_in = din("convb", [c.NSSD, 1, CONV])
    dtb_in = din("dtb", [c.NSSD, 2 * NH, 1])
    alog_in = din("alog", [c.NSSD, 2 * NH, 1])
    dsk_in = din("dskip", [c.NSSD, 2, NH])
    ssdnw_in = din("ssdnw", [c.NSSD, 1, DI])
    outp_in = din("outproj", [c.NSSD, DI, D])
    fou_in = din("fouw", [max(c.NFOU, 1), D, D])
    poolw_in = din("poolw", [max(c.NPOOL, 1), 4 * GD, GD])
    pscale_in = din("pscale", [max(c.NPOOL, 1), 1, D])
    dftc_in = din("dftc", [2, GD, GD], BF16)
    dftL_in = din("dftL", [2, L, L], BF16)
    dftLc_in = din("dftLc", [2, Lc, Lc], BF16)
    out_ext = nc.dram_tensor("out", [L, D], F32, kind="ExternalOutput").ap()

    def dscr(name, shape, dt):
        return nc.dram_tensor(name, list(shape), dt)

    _shared = {}

    def dscr_shared(name, shape, dt):
        if name not in _shared:
            _shared[name] = nc.dram_tensor(name, list(shape), dt)
        return _shared[name]

    XT = dscr("XT", [D, T], F32).ap()
    S_big = dscr("S_big", [max(DI, D), T], BF16).ap()

    top = f.es
    ident_f = f.sb(top, [128, 128], F32, "identf")
    ident_b = f.sb(top, [128, 128], BF16, "identb")
    ones_b = f.sb(top, [128, 128], BF16, "onesb")
    tri_f = f.sb(top, [128, 4, 128], F32, "tri")
    modT = f.sb(top, [128, depth, 6 * KD, 2], F32, "modT")
    A12 = f.sb(top, [128, depth, 2, KD, 2], F32, "A12")
    normT = f.sb(top, [128, KD, depth * 2], F32, "normT")
    fnwT = f.sb(top, [128, KD, 1], F32, "fnwT")
    rconst = Res()
    banks = [f.ps(top, [128, 512], F32, f"bank{i}") for i in range(8)]
    rbank = [Res() for _ in range(8)]

    f.dma("sp", ident_f[:], ident_in[:, :], writes=[rconst])
    f.dma("sp", tri_f[:], tri_in[:, :, :], writes=[rconst])
    f.op("dve", lambda e: e.tensor_copy(out=ident_b[:], in_=ident_f[:]), reads=[rconst], writes=[rconst])
    f.op("dve", lambda e: e.memset(ones_b[:], 1.0), writes=[rconst])

    gathered = {}

    WCB = 512

    def prep_weight(key, src, rows, cols):
        nkt = rows // 128
        ncb = (cols + WCB - 1) // WCB
        full = dscr(f"wg_{key}", [ncb * 128 * nkt * WCB], BF16)
        toks = []
        nfull = cols // WCB
        tail = cols - nfull * WCB
        for k in range(nkt):
            if nfull > 0:
                dst = dap(full, k * WCB, [(nkt * WCB, 128), (128 * nkt * WCB, nfull), (1, WCB)])
                sap = dap(src, k * 128 * cols, [(cols, 128), (WCB, nfull), (1, WCB)])
                toks.append(f.dma("pool", dst, sap))
            if tail > 0:
                dst = dap(full, nfull * 128 * nkt * WCB + k * WCB, [(nkt * WCB, 128), (1, tail)])
                sap = dap(src, k * 128 * cols + nfull * WCB, [(cols, 128), (1, tail)])
                toks.append(f.dma("pool", dst, sap))
        gathered[key] = (full, toks, nkt)

    def prep_layer_weights(i):
        kind, j = i % 3, i // 3
        if kind == 0:
            prep_weight(f"inp{j}", inp_in[j], D, INP)
            prep_weight(f"outp{j}", outp_in[j], DI, D)
        elif kind == 1:
            gathered["dftL0"] = (dftL_in[0], [], 0)
            gathered["dftL1"] = (dftL_in[1], [], 0)
            prep_weight(f"fou{j}", fou_in[j], D, D)
        else:
            prep_weight(f"pool{j}", poolw_in[j], 4 * GD, GD)
        prep_weight(f"ffg{i}", ffg_in[i], D, FH)
        prep_weight(f"ffu{i}", ffu_in[i], D, FH)
        prep_weight(f"ffd{i}", ffd_in[i], FH, D)

    prep_layer_weights(0)

    def load_T(rows_ap, n, W, out_tile_ap, rows_tile=None, rows_res=None):
        with contextlib.ExitStack() as st:
            if rows_tile is None:
                rows = f.sb(st, [n, W], F32, "rows")
                rr = Res()
                f.dma("sp", rows[:], rows_ap, writes=[rr])
            else:
                rows, rr = rows_tile, rows_res
            nk = W // 128
            per = 512 // n
            for k0 in range(0, nk, per):
                k1 = min(nk, k0 + per)
                b = 0
                groups = []
                for k in range(k0, k1):
                    groups.append((banks[b][:, (k - k0) * n:(k - k0 + 1) * n],
                                   [(rows[:, k * 128:(k + 1) * 128], ident_f[0:n, 0:n])]))
                f.mms(groups, reads=[rr, rconst], writes=[rbank[b]])
                src = banks[b][:, 0:(k1 - k0) * n].rearrange("p (k n) -> p k n", n=n)
                f.op("dve", lambda e: e.tensor_copy(out=out_tile_ap[:, k0:k1, :], in_=src), reads=[rbank[b]], writes=[rconst])
            f.barrier()

    with contextlib.ExitStack() as st:
        scT = f.sb(st, [128, KD, 2], F32, "scT")
        rsc = Res()
        load_T(c2_in[:, :], 2, D, scT)
        f.op("act", lambda e: e.activation(out=scT[:], in_=scT[:], func=AF.Silu), reads=[rconst], writes=[rsc])
        rowsM = f.sb(st, [2, 6 * D], F32, "rowsM")
        bm = f.sb(st, [2, 6 * D], F32, "bm")
        rrows = Res()
        rbm = Res()
        wm = [f.sb(st, [128, KD, 512], F32, "wm") for _ in range(3)]
        rwm = [Res(), Res(), Res()]
        MCB = min(512, 6 * D)
        it = 0
        for i in range(depth):
            f.dma("sp", bm[:], dap(bmod_in, i * 6 * D, [(0, 2), (1, 6 * D)]), writes=[rbm])
            for cb in range(6 * D // MCB):
                s = it % 3
                it += 1
                f.dma("sp", wm[s][:, :, 0:MCB], dap(wmod_in, i * D * 6 * D + cb * MCB, [(6 * D, 128), (128 * 6 * D, KD), (1, MCB)]), writes=[rwm[s]])
                b = 1 + s
                f.mm(banks[b][0:2, 0:MCB], [(scT[:, k, :], wm[s][:, k, 0:MCB]) for k in range(KD)],
                     reads=[rsc, rwm[s]], writes=[rbank[b]])
                f.op("dve", lambda e: e.tensor_tensor(out=rowsM[:, cb * MCB:(cb + 1) * MCB], in0=banks[b][0:2, 0:MCB],
                                                      in1=bm[:, cb * MCB:(cb + 1) * MCB], op=ALU.add),
                     reads=[rbank[b], rbm], writes=[rrows])
            load_T(None, 2, 6 * D, modT[:, i, :, :], rows_tile=rowsM, rows_res=rrows)
        f.barrier()
    load_T(normw_in[:, :], depth * 2, D, normT)
    load_T(fnw_in[:, :], 1, D, fnwT)
    for i in range(depth):
        for j in range(2):
            sc = modT[:, i, (3 * j + 1) * KD:(3 * j + 2) * KD, :]
            nw = bc(normT[:, :, 2 * i + j], [(depth * 2, KD), (0, 2)])
            f.op("dve", lambda e: e.scalar_tensor_tensor(out=A12[:, i, j, :, :], in0=sc, scalar=1.0, in1=nw,
                                                         op0=ALU.add, op1=ALU.mult), reads=[rconst], writes=[rconst])
    f.barrier()

    def MOD(i, s, k, v):
        return modT[:, i, s * KD + k, v:v + 1]

    def vsplit(t0, n):
        return [(0, n, 1 if t0 < Lc else 0)]

    with contextlib.ExitStack() as st:
        posT = f.sb(st, [128, KD // 2, 64], F32, "posT")
        rpos = Res()
        f.dma("sp", posT[:], dap(pos_in, 0, [(64, 128), (128 * 64, KD // 2), (1, 64)]), writes=[rpos])
        xin = [f.sb(st, [128, 4, D], F32, "xin") for _ in range(2)]
        rxin = [Res(), Res()]
        xo = [f.sb(st, [128, KD, 512], F32, "xo") for _ in range(2)]
        rxo = [Res(), Res()]
        for bi, (t0, nb) in enumerate(c.blocks):
            s = bi % 2
            nt = nb // 128
            if t0 < Lc:
                src = ctx_in[t0:t0 + nb, :]
            else:
                src = x_in[t0 - Lc:t0 - Lc + nb, :]
            f.dma("sp", xin[s][:, 0:nt, :], src.rearrange("(a p) d -> p a d", p=128), writes=[rxin[s]])
            for k in range(KD):
                b = k % 8
                f.trs([(banks[b][:, a * 128:(a + 1) * 128], xin[s][:, a, k * 128:(k + 1) * 128]) for a in range(nt)],
                      ident_f[:], reads=[rxin[s], rconst], writes=[rbank[b]])
                if t0 < Lc:
                    f.op("act", lambda e: e.copy(out=xo[s][:, k, 0:nb], in_=banks[b][:, 0:nb]), reads=[rbank[b]], writes=[rxo[s]], nowaw=True)
                else:
                    r0 = (t0 - Lc) // 64
                    nr = nb // 64
                    if k < KD // 2:
                        in1 = bc(posT[:, k, r0:r0 + nr], [(1, nr), (0, 64)])
                    else:
                        in1 = bc(posT[:, k - KD // 2, 0:64], [(0, nr), (1, 64)])
                    f.op("dve", lambda e: e.tensor_tensor(out=xo[s][:, k, 0:nb].rearrange("p (r c) -> p r c", c=64),
                                                          in0=banks[b][:, 0:nb].rearrange("p (r c) -> p r c", c=64),
                                                          in1=in1, op=ALU.add), reads=[rbank[b], rpos], writes=[rxo[s]], nowaw=True)
            f.dma("pool", dap(XT, t0, [(T, 128), (128 * T, KD), (1, nb)]), xo[s][:, :, 0:nb], reads=[rxo[s]])
        f.barrier()

    def load_norm_block(st_tiles, i, j, t0, nb, want_x=True):
        xt, rxt, hn, rhn, sq, rsq, rstd, rrstd = st_tiles[:8]
        v = 1 if t0 < Lc else 0
        f.dma("sp", xt[:, :, 0:nb], dap(XT, t0, [(T, 128), (128 * T, KD), (1, nb)]), writes=[rxt])
        for k in range(KD):
            f.op("act", lambda e: e.activation(out=sq[:, k % 2, 0:nb], in_=xt[:, k, 0:nb], func=AF.Square),
                 reads=[rxt], writes=[rsq[k % 2]])
            f.mm(banks[0][:, 0:nb], [(ones_b[:], sq[:, k % 2, 0:nb])], reads=[rsq[k % 2], rconst],
                 writes=[rbank[0]], start=(k == 0), stop=(k == KD - 1), nowaw=(k > 0))
        eps = 1e-6
        f.op("act", lambda e: e.activation(out=rstd[:, 0:nb], in_=banks[0][:, 0:nb], func=AF.Sqrt, bias=eps, scale=1.0 / D),
             reads=[rbank[0]], writes=[rrstd])
        f.op("dve", lambda e: e.reciprocal(out=rstd[:, 0:nb], in_=rstd[:, 0:nb]), reads=[rrstd], writes=[rrstd])
        if hn is None:
            return
        tmp = st_tiles[8]
        rtmp = st_tiles[9]
        for k in range(KD):
            s = k % 2
            f.op("dve", lambda e: e.scalar_tensor_tensor(out=tmp[:, s, 0:nb], in0=xt[:, k, 0:nb], scalar=A12[:, i, j, k, v:v + 1],
                                                         in1=rstd[:, 0:nb], op0=ALU.mult, op1=ALU.mult),
                 reads=[rxt, rrstd, rconst], writes=[rtmp[s]])
            f.op("act", lambda e: e.activation(out=hn[:, k, 0:nb], in_=tmp[:, s, 0:nb], func=AF.Identity,
                                               bias=MOD(i, 3 * j, k, v), scale=1.0),
                 reads=[rtmp[s], rconst], writes=[rhn], nowaw=(k > 0))

    def alloc_norm_tiles(st, with_hn=True, hn_dtype=BF16):
        xt = f.sb(st, [128, KD, 512], F32, "xt")
        hn = f.sb(st, [128, KD, 512], hn_dtype, "hn") if with_hn else None
        sq = f.sb(st, [128, 2, 512], BF16, "sq")
        rstd = f.sb(st, [128, 512], F32, "rstd")
        tmp = f.sb(st, [128, 2, 512], F32, "tmpn")
        return [xt, Res(), hn, Res(), sq, [Res(), Res()], rstd, Res(), tmp, [Res(), Res()]]

    WSLOT = 3

    def alloc_ring(st, nk_max):
        ring = [f.sb(st, [128, nk_max, 512], BF16, "wring") for _ in range(WSLOT)]
        return {"t": ring, "r": [Res() for _ in range(WSLOT)], "n": 0}

    def wload(ring, wkey, row0, nk, col0, ncol):
        s = ring["n"] % WSLOT
        ring["n"] += 1
        full, tok, nkt = gathered[wkey]
        cb, off = col0 // WCB, col0 % WCB
        assert off + ncol <= WCB and row0 % 128 == 0
        f.dma("sp", ring["t"][s][:, 0:nk, 0:ncol],
              dap(full, cb * 128 * nkt * WCB + (row0 // 128) * WCB + off, [(nkt * WCB, 128), (WCB, nk), (1, ncol)]),
              writes=[ring["r"][s]], extra=list(tok))
        return ring["t"][s], ring["r"][s]

    def proj_residual(i, gslot, wkey, K_chunks, src, kmap=None, extra_scale=None, skip_ctx=False):
        with contextlib.ExitStack() as st:
            ring = alloc_ring(st, max(K_chunks, 1))
            sin = f.sb(st, [128, K_chunks, 512], BF16, "sin")
            rsin = Res()
            xt = f.sb(st, [128, KD, 512], F32, "xtp")
            rxt = Res()
            gs = f.sb(st, [128, KD, 2], F32, "gs")
            rgs = Res()
            if extra_scale is not None:
                f.op("dve", lambda e: e.tensor_tensor(out=gs[:], in0=modT[:, i, gslot * KD:(gslot + 1) * KD, :],
                                                      in1=bc(extra_scale[:, :, 0], [(1, KD), (0, 2)]), op=ALU.mult),
                     reads=[rconst], writes=[rgs])
            else:
                f.op("dve", lambda e: e.tensor_copy(out=gs[:], in_=modT[:, i, gslot * KD:(gslot + 1) * KD, :]),
                     reads=[rconst], writes=[rgs])
            for (t0, nb) in c.blocks:
                if skip_ctx and t0 < Lc:
                    continue
                v = 1 if t0 < Lc else 0
                f.dma("sp", sin[:, :, 0:nb], dap(src, t0, [(T, 128), (128 * T, K_chunks), (1, nb)]), writes=[rsin])
                f.dma("sp", xt[:, :, 0:nb], dap(XT, t0, [(T, 128), (128 * T, KD), (1, nb)]), writes=[rxt])
                mcols = min(512, D)
                for mb in range(D // mcols):
                    if kmap is None:
                        wt, rw = wload(ring, wkey, 0, K_chunks, mb * mcols, mcols)
                    for mm_ in range(mcols // 128):
                        m = mb * (mcols // 128) + mm_
                        b = m % 4
                        if kmap is None:
                            pairs = [(wt[:, k, mm_ * 128:(mm_ + 1) * 128], sin[:, k, 0:nb]) for k in range(K_chunks)]
                            f.mm(banks[b][:, 0:nb], pairs, reads=[rw, rsin], writes=[rbank[b]])
                        else:
                            ks, wrow0, wcol0 = kmap(m)
                            wt, rw = wload(ring, wkey, wrow0, len(ks), wcol0, 128)
                            pairs = [(wt[:, kk, 0:128], sin[:, k, 0:nb]) for kk, k in enumerate(ks)]
                            f.mm(banks[b][:, 0:nb], pairs, reads=[rw, rsin], writes=[rbank[b]])
                        f.op("dve", lambda e: e.scalar_tensor_tensor(out=xt[:, m, 0:nb], in0=banks[b][:, 0:nb], scalar=gs[:, m, v:v + 1],
                                                                     in1=xt[:, m, 0:nb], op0=ALU.mult, op1=ALU.add),
                             reads=[rbank[b], rgs], writes=[rxt], nowaw=True)
                f.dma("pool", dap(XT, t0, [(T, 128), (128 * T, KD), (1, nb)]), xt[:, :, 0:nb], reads=[rxt])
            f.barrier()

    def ffn_phase(i, skip_ctx):
        HB = 512 if FH % 512 == 0 else 256
        with contextlib.ExitStack() as st:
            nt0 = alloc_norm_tiles(st)
            xt1 = f.sb(st, [128, KD, 512], F32, "xt1")
            nt1 = list(nt0)
            nt1[0], nt1[1] = xt1, Res()
            nts = [nt0, nt1]
            hn, rhn = nt0[2], nt0[3]
            ND = 3
            kd = (KH + ND - 1) // ND
            ring = alloc_ring(st, max(KD, kd))
            act = f.sb(st, [128, KH, 512], BF16, "act")
            ract = Res()
            sg = f.sb(st, [128, 2, 512], F32, "sg")
            rsg = [Res(), Res()]
            blks = [(t0, nb) for (t0, nb) in c.blocks if not (skip_ctx and t0 < Lc)]
            load_norm_block(nts[0], i, 1, blks[0][0], blks[0][1])
            for bi, (t0, nb) in enumerate(blks):
                nt = nts[bi % 2]
                xt, rxt = nt[0], nt[1]
                v = 1 if t0 < Lc else 0
                it = 0
                for jb in range(FH // HB):
                    wg, rwg = wload(ring, f"ffg{i}", 0, KD, jb * HB, HB)
                    wu, rwu = wload(ring, f"ffu{i}", 0, KD, jb * HB, HB)
                    for cc in range(HB // 128):
                        j = jb * (HB // 128) + cc
                        s = it % 2
                        it += 1
                        bg, bu = 4 + 2 * s, 5 + 2 * s
                        f.mm(banks[bg][:, 0:nb], [(wg[:, k, cc * 128:(cc + 1) * 128], hn[:, k, 0:nb]) for k in range(KD)],
                             reads=[rwg, rhn], writes=[rbank[bg]])
                        f.mm(banks[bu][:, 0:nb], [(wu[:, k, cc * 128:(cc + 1) * 128], hn[:, k, 0:nb]) for k in range(KD)],
                             reads=[rwu, rhn], writes=[rbank[bu]])
                        f.op("act", lambda e: e.activation(out=sg[:, s, 0:nb], in_=banks[bg][:, 0:nb], func=AF.Silu),
                             reads=[rbank[bg]], writes=[rsg[s]])
                        f.op("dve", lambda e: e.tensor_tensor(out=act[:, j, 0:nb], in0=sg[:, s, 0:nb], in1=banks[bu][:, 0:nb], op=ALU.mult),
                             reads=[rsg[s], rbank[bu]], writes=[ract], nowaw=True)
                if bi + 1 < len(blks):
                    load_norm_block(nts[(bi + 1) % 2], i, 1, blks[bi + 1][0], blks[bi + 1][1])
                mcols = min(512, D)
                splits = [(a, min(KH, a + kd)) for a in range(0, KH, kd)]
                for mb in range(D // mcols):
                    nm = mcols // 128
                    for hi, (k0, k1) in enumerate(splits):
                        wd, rwd = wload(ring, f"ffd{i}", k0 * 128, k1 - k0, mb * mcols, mcols)
                        for mm_ in range(nm):
                            b = mm_ + 4 * (mb % 2)
                            f.mm(banks[b][:, 0:nb], [(wd[:, k - k0, mm_ * 128:(mm_ + 1) * 128], act[:, k, 0:nb]) for k in range(k0, k1)],
                                 reads=[rwd, ract], writes=[rbank[b]], start=(hi == 0), stop=(hi == len(splits) - 1))
                    for mm_ in range(nm):
                        m = mb * nm + mm_
                        b = mm_ + 4 * (mb % 2)
                        f.op("dve", lambda e: e.scalar_tensor_tensor(out=xt[:, m, 0:nb], in0=banks[b][:, 0:nb], scalar=MOD(i, 5, m, v),
                                                                     in1=xt[:, m, 0:nb], op0=ALU.mult, op1=ALU.add),
                             reads=[rbank[b], rconst], writes=[rxt], nowaw=True)
                f.dma("pool", dap(XT, t0, [(T, 128), (128 * T, KD), (1, nb)]), xt[:, :, 0:nb], reads=[rxt])
            f.barrier()

    def final_phase():
        with contextlib.ExitStack() as st:
            nt = alloc_norm_tiles(st, with_hn=False)
            xt, rxt, rstd, rrstd = nt[0], nt[1], nt[6], nt[7]
            yt = f.sb(st, [128, KD, 512], F32, "yt")
            ryt = Res()
            ot = f.sb(st, [128, 4, D], F32, "ot")
            rot = Res()
            for (t0, nb) in c.blocks:
                if t0 < Lc:
                    continue
                load_norm_block(nt, 0, 0, t0, nb)
                for k in range(KD):
                    f.op("dve", lambda e: e.scalar_tensor_tensor(out=yt[:, k, 0:nb], in0=xt[:, k, 0:nb], scalar=fnwT[:, k, 0:1],
                                                                 in1=rstd[:, 0:nb], op0=ALU.mult, op1=ALU.mult),
                         reads=[rxt, rrstd, rconst], writes=[ryt], nowaw=True)
                na = nb // 128
                for a in range(na):
                    for k0 in range(0, KD, 4):
                        b = (k0 // 4) % 8
                        f.trs([(banks[b][:, (k - k0) * 128:(k - k0 + 1) * 128], yt[:, k, a * 128:(a + 1) * 128]) for k in range(k0, min(KD, k0 + 4))],
                              ident_f[:], reads=[ryt, rconst], writes=[rbank[b]])
                        w = min(KD, k0 + 4) - k0
                        if (k0 // 4) % 2 == 0:
                            f.op("act", lambda e: e.copy(out=ot[:, a, k0 * 128:(k0 + w) * 128], in_=banks[b][:, 0:w * 128]),
                                 reads=[rbank[b]], writes=[rot])
                        else:
                            f.op("dve", lambda e: e.tensor_copy(out=ot[:, a, k0 * 128:(k0 + w) * 128], in_=banks[b][:, 0:w * 128]),
                                 reads=[rbank[b]], writes=[rot])
                f.dma("pool", out_ext[t0 - Lc:t0 - Lc + nb, :].rearrange("(a p) d -> p a d", p=128), ot[:, 0:na, :], reads=[rot])
            f.barrier()

    def pool_phase(i, j, need_ctx):
        with contextlib.ExitStack() as st0:
          psT = f.sb(st0, [128, KD, 1], F32, "psT")
          load_T(pscale_in[j], 1, D, psT)
          with contextlib.ExitStack() as st:
            rstd_all = f.sb(st, [128, T], F32, "rstdall")
            rra = Res()
            with contextlib.ExitStack() as st2:
                nt = alloc_norm_tiles(st2, with_hn=False)
                for (t0, nb) in c.blocks:
                    if (not need_ctx) and t0 < Lc:
                        continue
                    load_norm_block(nt, i, 0, t0, nb)
                    f.op("act", lambda e: e.copy(out=rstd_all[:, t0:t0 + nb], in_=nt[6][:, 0:nb]), reads=[nt[7]], writes=[rra])
                f.barrier()
            PADW = 8
            seqs = ([(0, Lc, 1)] if need_ctx else []) + [(Lc, L, 0)]
            xc = f.sb(st, [128, L], F32, "xc")
            rxc = Res()
            hp = f.sb(st, [128, L + 2 * PADW], F32, "hp")
            rhp = Res()
            wa = f.sb(st, [128, L + 2 * PADW], F32, "wa")
            wb = f.sb(st, [128, L + 2 * PADW], F32, "wb")
            rwa, rwb = Res(), Res()
            inv = f.sb(st, [128, L], F32, "inv")
            rinv = Res()
            po = f.sb(st, [128, L], BF16, "po")
            rpo = Res()
            for (s0, S, v) in seqs:
                for g, win in enumerate(POOL_WINDOWS):
                    f.op("dve", lambda e: e.memset(inv[:, 0:S], 1.0 / win), writes=[rinv])
                    for pos in list(range(win // 2)) + list(range(S - win // 2 + 1, S)):
                        lo, hi = max(pos - win // 2, 0), min(pos + win - win // 2, S)
                        val = 1.0 / (hi - lo)
                        f.op("dve", lambda e: e.memset(inv[:, pos:pos + 1], val), writes=[rinv])
                    for kk in range(GK):
                        k = g * GK + kk
                        f.dma("sp", xc[:, 0:S], dap(XT, k * 128 * T + s0, [(T, 128), (1, S)]), writes=[rxc])
                        f.op("dve", lambda e: e.memset(hp[:, 0:PADW], 0.0), writes=[rhp])
                        f.op("dve", lambda e: e.memset(hp[:, PADW + S:PADW + S + PADW], 0.0), writes=[rhp])
                        f.op("dve", lambda e: e.scalar_tensor_tensor(out=xc[:, 0:S], in0=xc[:, 0:S], scalar=A12[:, i, 0, k, v:v + 1],
                                                                     in1=rstd_all[:, s0:s0 + S], op0=ALU.mult, op1=ALU.mult),
                             reads=[rra, rconst], writes=[rxc])
                        f.op("act", lambda e: e.activation(out=hp[:, PADW:PADW + S], in_=xc[:, 0:S], func=AF.Identity,
                                                           bias=MOD(i, 0, k, v), scale=1.0), reads=[rxc, rconst], writes=[rhp])
                        cur, rcur = hp, rhp
                        width = S + 2 * PADW
                        step = 1
                        bufs = [(wa, rwa), (wb, rwb)]
                        bi = 0
                        while step < win:
                            nxt, rn = bufs[bi % 2]
                            bi += 1
                            nw = width - step
                            f.op("dve", lambda e: e.tensor_tensor(out=nxt[:, 0:nw], in0=cur[:, 0:nw], in1=cur[:, step:step + nw], op=ALU.add),
                                 reads=[rcur], writes=[rn])
                            cur, rcur, width = nxt, rn, nw
                            step *= 2
                        off = PADW - win // 2
                        f.op("dve", lambda e: e.tensor_tensor(out=xc[:, 0:S], in0=cur[:, off:off + S], in1=inv[:, 0:S], op=ALU.mult),
                             reads=[rcur, rinv], writes=[rxc])
                        f.op("dve", lambda e: e.tensor_tensor(out=po[:, 0:S], in0=xc[:, 0:S], in1=hp[:, PADW:PADW + S], op=ALU.subtract),
                             reads=[rxc, rhp], writes=[rpo])
                        f.dma("pool", dap(S_big, k * 128 * T + s0, [(T, 128), (1, S)]), po[:, 0:S], reads=[rpo])
            f.barrier()

          def kmap(m):
              g = m // GK
              return [g * GK + kk for kk in range(GK)], g * GD, (m % GK) * 128
          proj_residual(i, 2, f"pool{j}", KD, S_big, kmap=kmap, extra_scale=psT, skip_ctx=not need_ctx)

    def fourier_phase(i, j, need_ctx):
        ABd = dscr(f"ABd{i}", [D // 128, T, 2, 128], BF16).ap()
        with contextlib.ExitStack() as st:
            nt = alloc_norm_tiles(st)
            hn, rhn = nt[2], nt[3]
            cs = f.sb(st, [128, 2, GK, GD], BF16, "cs")
            rcs = Res()
            for a in range(2):
                f.dma("sp", cs[:, a, :, :], dap(dftc_in, a * GD * GD, [(GD, 128), (128 * GD, GK), (1, GD)]), writes=[rcs])
            ab = f.sb(st, [128, KD, 2, 128], BF16, "ab")
            rab = Res()
            for (t0, nb) in c.blocks:
                if (not need_ctx) and t0 < Lc:
                    continue
                load_norm_block(nt, i, 0, t0, nb)
                for a in range(nb // 128):
                    it = 0
                    for g in range(4):
                        for part in range(2):
                            b = 4 + it % 4
                            it += 1
                            f.mm(banks[b][:, 0:GD], [(hn[:, g * GK + kk, a * 128:(a + 1) * 128], cs[:, part, kk, :]) for kk in range(GK)],
                                 reads=[rhn, rcs], writes=[rbank[b]])
                            dst = ab[:, g * GK:(g + 1) * GK, part, :]
                            srcp = banks[b][:, 0:GD].rearrange("p (k n) -> p k n", n=128)
                            if part == 0:
                                f.op("act", lambda e: e.copy(out=dst, in_=srcp), reads=[rbank[b]], writes=[rab], nowaw=True)
                            else:
                                f.op("dve", lambda e: e.tensor_copy(out=dst, in_=srcp), reads=[rbank[b]], writes=[rab], nowaw=True)
                    tt = t0 + a * 128
                    f.dma("pool", dap(ABd, tt * 256, [(256, 128), (T * 256, KD), (1, 256)]), ab[:].rearrange("p k a n -> p k (a n)"), reads=[rab])
            f.barrier()
        with contextlib.ExitStack() as st:
            seqs = ([(0, Lc, dftLc_in, None)] if need_ctx else []) + [(Lc, L, None, None)]
            yt = f.sb(st, [128, 2, 512], BF16, "ytf")
            ryt = [Res(), Res()]
            for (s0, S, small_src, _) in seqs:
                nl = S // 128
                kb = min(512, S)
                scale = 1.0 / float(np.sqrt(S * GD))
                with contextlib.ExitStack() as st2:
                    slab = f.sb(st2, [128, 2, nl, kb], BF16, "slab")
                    rslab = Res()
                    abl = [f.sb(st2, [128, nl, 256], BF16, "abl") for _ in range(2)]
                    rabl = [Res(), Res()]
                    for k1b in range(S // kb):
                        for part in range(2):
                            if small_src is not None:
                                f.dma("sp", slab[:, part, :, :], dap(small_src, part * S * S + k1b * kb, [(S, 128), (128 * S, nl), (1, kb)]), writes=[rslab])
                            else:
                                g_ap, g_tok, _ = gathered[f"dftL{part}"]
                                f.dma("sp", slab[:, part, :, :], dap(g_ap, k1b * kb, [(S, 128), (128 * S, nl), (1, kb)]), writes=[rslab], extra=list(g_tok))
                        for q in range(KD):
                            s = q % 2
                            f.dma("sp", abl[s][:], dap(ABd, q * T * 256 + s0 * 256, [(256, 128), (128 * 256, nl), (1, 256)]), writes=[rabl[s]])
                            b = 4 + q % 4
                            pairs = []
                            for part in range(2):
                                for l in range(nl):
                                    pairs.append((abl[s][:, l, part * 128:(part + 1) * 128], slab[:, part, l, :]))
                            f.mm(banks[b][:, 0:kb], pairs, reads=[rabl[s], rslab], writes=[rbank[b]])
                            f.op("act", lambda e: e.activation(out=yt[:, s, 0:kb], in_=banks[b][:, 0:kb], func=AF.Copy, scale=scale),
                                 reads=[rbank[b]], writes=[ryt[s]])
                            f.dma("pool", dap(S_big, q * 128 * T + s0 + k1b * kb, [(T, 128), (1, kb)]), yt[:, s, 0:kb], reads=[ryt[s]])
                    f.barrier()
        proj_residual(i, 2, f"fou{j}", KD, S_big, skip_ctx=not need_ctx)

    def ssd_phase(i, j, with_ctx):
        KC = CONV // 128
        KI = DI // 128
        Ztok = dscr_shared("Ztok", [T, DI], BF16).ap()
        XBC = dscr_shared("XBC", [CONV, T], F32).ap()
        XStok = dscr_shared("XStok", [T, DI], BF16).ap()
        BCT = dscr_shared("BCT", [2 * G * 128, T], BF16).ap()
        Btok = dscr_shared("Btok", [T, G * 128], BF16).ap()
        CUMd = dscr_shared("CUMd", [2, NH, T], F32).ap()
        DTd = dscr_shared("DTd", [2, NH, T], F32).ap()
        Hd = dscr_shared("Hd", [2, NCH, 128, DI], BF16).ap()
        wkey = f"inp{j}"
        with contextlib.ExitStack() as stA:
            cwT = f.sb(stA, [128, KC, 5], F32, "cwT")
            cbT = f.sb(stA, [128, KC, 1], F32, "cbT")
            snT = f.sb(stA, [128, KI, 1], F32, "snT")
            load_T(convw_in[j], 5, CONV, cwT)
            load_T(convb_in[j], 1, CONV, cbT)
            load_T(ssdnw_in[j], 1, DI, snT)
            dtp = f.sb(stA, [NH, 2, 3], F32, "dtp")
            rdtp = Res()
            for d in range(2):
                f.dma("sp", dtp[:, d, 0:1], dtb_in[j, d * NH:(d + 1) * NH, :], writes=[rdtp])
                f.dma("sp", dtp[:, d, 1:2], alog_in[j, d * NH:(d + 1) * NH, :], writes=[rdtp])
            f.op("act", lambda e: e.activation(out=dtp[:, :, 1:2], in_=dtp[:, :, 1:2], func=AF.Exp), reads=[rdtp], writes=[rdtp])
            f.op("dve", lambda e: e.tensor_scalar(out=dtp[:, :, 1:2], in0=dtp[:, :, 1:2], scalar1=-1.0, scalar2=None, op0=ALU.mult), reads=[rdtp], writes=[rdtp])
            skb = f.sb(stA, [128, 3, NH], F32, "skb")
            rskb = Res()
            f.dma("sp", skb[:, 0:2, :], dap(dsk_in, j * 2 * NH, [(0, 128), (NH, 2), (1, NH)]), writes=[rskb])
            f.op("dve", lambda e: e.tensor_tensor(out=skb[:, 2, :], in0=skb[:, 0, :], in1=skb[:, 1, :], op=ALU.add), reads=[rskb], writes=[rskb])
            f.barrier()

            st_dt = contextlib.ExitStack()
            dtT = f.sb(st_dt, [NH, 2, T], F32, "dtT")
            rdt = Res()
            with contextlib.ExitStack() as st:
                nt = alloc_norm_tiles(st)
                hn, rhn = nt[2], nt[3]
                ring = alloc_ring(st, KD)
                zt = f.sb(st, [128, 2, 512], BF16, "zt")
                rzt = [Res(), Res()]
                xe = f.sb(st, [128, 2, 512], F32, "xe")
                rxe = [Res(), Res()]
                it = 0
                for (t0, nb) in c.blocks:
                    load_norm_block(nt, i, 0, t0, nb)
                    for cb0 in range(0, DI, 512):
                        ncb = min(512, DI - cb0)
                        wt, rw = wload(ring, wkey, 0, KD, cb0, ncb)
                        for a in range(nb // 128):
                            s = it % 2
                            it += 1
                            b = 4 + s
                            f.mm(banks[b][:, 0:ncb], [(hn[:, k, a * 128:(a + 1) * 128], wt[:, k, 0:ncb]) for k in range(KD)],
                                 reads=[rhn, rw], writes=[rbank[b]])
                            f.op("act", lambda e: e.copy(out=zt[:, s, 0:ncb], in_=banks[b][:, 0:ncb]), reads=[rbank[b]], writes=[rzt[s]])
                            f.dma("pool", Ztok[t0 + a * 128:t0 + (a + 1) * 128, cb0:cb0 + ncb], zt[:, s, 0:ncb], reads=[rzt[s]])
                    for cb0 in range(0, CONV, 512):
                        ncb = min(512, CONV - cb0)
                        wt, rw = wload(ring, wkey, 0, KD, DI + cb0, ncb)
                        for cc in range(ncb // 128):
                            s = it % 2
                            it += 1
                            b = 6 + s
                            f.mm(banks[b][:, 0:nb], [(wt[:, k, cc * 128:(cc + 1) * 128], hn[:, k, 0:nb]) for k in range(KD)],
                                 reads=[rhn, rw], writes=[rbank[b]])
                            f.op("dve", lambda e: e.tensor_copy(out=xe[:, s, 0:nb], in_=banks[b][:, 0:nb]), reads=[rbank[b]], writes=[rxe[s]])
                            f.dma("pool", XBC[cb0 + cc * 128:cb0 + (cc + 1) * 128, t0:t0 + nb], xe[:, s, 0:nb], reads=[rxe[s]])
                    wt, rw = wload(ring, wkey, 0, KD, DI + CONV, 2 * NH)
                    for d in range(2):
                        b = d
                        f.mm(banks[b][0:NH, 0:nb], [(wt[:, k, d * NH:(d + 1) * NH], hn[:, k, 0:nb]) for k in range(KD)],
                             reads=[rhn, rw], writes=[rbank[b]])
                        f.op("act", lambda e: e.activation(out=dtT[:, d, t0:t0 + nb], in_=banks[b][0:NH, 0:nb], func=AF.Exp,
                                                           bias=dtp[:, d, 0:1], scale=1.0), reads=[rbank[b], rdtp], writes=[rdt])
                f.op("act", lambda e: e.activation(out=dtT[:], in_=dtT[:], func=AF.Ln, bias=1.0, scale=1.0), reads=[rdt], writes=[rdt])
                f.dma("pool", dap(DTd, 0, [(T, NH), (NH * T, 2), (1, T)]), dtT[:], reads=[rdt])
                f.barrier()
            with contextlib.ExitStack() as st:
                ca = f.sb(st, [NH, 2, T], F32, "ca")
                cbuf = f.sb(st, [NH, 2, T], F32, "cbuf")
                rca, rcb = Res(), Res()
                for d in range(2):
                    f.op("dve", lambda e: e.tensor_scalar(out=ca[:, d, :], in0=dtT[:, d, :], scalar1=dtp[:, d, 1:2], scalar2=None, op0=ALU.mult),
                         reads=[rdt, rdtp], writes=[rca])
                cur, rcur, nxt, rnxt = ca, rca, cbuf, rcb
                step = 1
                while step < 128:
                    for d in range(2):
                        cv = cur[:, d, :].rearrange("p (c t) -> p c t", t=128)
                        nv = nxt[:, d, :].rearrange("p (c t) -> p c t", t=128)
                        if d == 0:
                            f.op("dve", lambda e: e.tensor_tensor(out=nv[:, :, step:128], in0=cv[:, :, step:128], in1=cv[:, :, 0:128 - step], op=ALU.add),
                                 reads=[rcur], writes=[rnxt])
                            f.op("dve", lambda e: e.tensor_copy(out=nv[:, :, 0:step], in_=cv[:, :, 0:step]), reads=[rcur], writes=[rnxt])
                        else:
                            f.op("dve", lambda e: e.tensor_tensor(out=nv[:, :, 0:128 - step], in0=cv[:, :, 0:128 - step], in1=cv[:, :, step:128], op=ALU.add),
                                 reads=[rcur], writes=[rnxt])
                            f.op("dve", lambda e: e.tensor_copy(out=nv[:, :, 128 - step:128], in_=cv[:, :, 128 - step:128]), reads=[rcur], writes=[rnxt])
                    cur, rcur, nxt, rnxt = nxt, rnxt, cur, rcur
                    step *= 2
                f.dma("pool", dap(CUMd, 0, [(T, NH), (NH * T, 2), (1, T)]), cur[:], reads=[rcur])
                f.barrier()
            st_dt.close()

            with contextlib.ExitStack() as st:
                GRP = 4
                raw = f.sb(st, [128, GRP, 516], F32, "raw")
                rraw = Res()
                acc = f.sb(st, [128, GRP, 512], F32, "cacc")
                racc = [Res() for _ in range(GRP)]
                so = f.sb(st, [128, GRP, 512], BF16, "so")
                rso = Res()
                tk = f.sb(st, [128, 4, GRP * 128], BF16, "tk")
                rtk = Res()
                for (t0, nb) in c.blocks:
                    s0, S = (0, Lc) if t0 < Lc else (Lc, L)
                    lo = 2 if t0 > s0 else 0
                    hi = 2 if t0 + nb < s0 + S else 0
                    for g0 in range(0, KC, GRP):
                        if lo == 0:
                            f.op("dve", lambda e: e.memset(raw[:, :, 0:2], 0.0), writes=[rraw])
                        if hi == 0:
                            f.op("dve", lambda e: e.memset(raw[:, :, 2 + nb:4 + nb], 0.0), writes=[rraw])
                        f.dma("sp", raw[:, :, 2 - lo:2 + nb + hi], dap(XBC, g0 * 128 * T + t0 - lo, [(T, 128), (128 * T, GRP), (1, nb + lo + hi)]), writes=[rraw])
                        for gg in range(GRP):
                            ch = g0 + gg
                            f.op("act", lambda e: e.activation(out=acc[:, gg, 0:nb], in_=raw[:, gg, 0:nb], func=AF.Identity,
                                                               bias=cbT[:, ch, 0:1], scale=cwT[:, ch, 0:1]), reads=[rraw, rconst], writes=[racc[gg]])
                        for w in range(1, 5):
                            for gg in range(GRP):
                                ch = g0 + gg
                                f.op("dve", lambda e: e.scalar_tensor_tensor(out=acc[:, gg, 0:nb], in0=raw[:, gg, w:w + nb], scalar=cwT[:, ch, w:w + 1],
                                                                             in1=acc[:, gg, 0:nb], op0=ALU.mult, op1=ALU.add),
                                     reads=[rraw, rconst], writes=[racc[gg]])
                        for gg in range(GRP):
                            f.op("act", lambda e: e.activation(out=so[:, gg, 0:nb], in_=acc[:, gg, 0:nb], func=AF.Silu), reads=[racc[gg]], writes=[rso], nowaw=True)
                        is_x = g0 < KI
                        is_b = (g0 >= KI) and (g0 < KI + G)
                        if not is_x:
                            f.dma("pool", dap(BCT, (g0 - KI) * 128 * T + t0, [(T, 128), (128 * T, GRP), (1, nb)]), so[:, :, 0:nb], reads=[rso])
                        if is_x or is_b:
                            na = nb // 128
                            for a in range(na):
                                b = 4 + a % 4
                                pb = banks[b][:].bitcast(BF16)
                                f.trs([(pb[:, gg * 128:(gg + 1) * 128], so[:, gg, a * 128:(a + 1) * 128]) for gg in range(GRP)],
                                      ident_b[:], reads=[rso, rconst], writes=[rbank[b]])
                                if a % 2 == 0:
                                    f.op("act", lambda e: e.copy(out=tk[:, a, :], in_=pb[:, 0:GRP * 128]), reads=[rbank[b]], writes=[rtk])
                                else:
                                    f.op("dve", lambda e: e.tensor_copy(out=tk[:, a, :], in_=pb[:, 0:GRP * 128]), reads=[rbank[b]], writes=[rtk])
                            if is_x:
                                dst = dap(XStok, t0 * DI + g0 * 128, [(DI, 128), (128 * DI, na), (1, GRP * 128)])
                            else:
                                dst = dap(Btok, t0 * G * 128 + (g0 - KI) * 128, [(G * 128, 128), (128 * G * 128, na), (1, GRP * 128)])
                            f.dma("pool", dst, tk[:, 0:na, :], reads=[rtk])
                f.barrier()

            chunk_prep_cache = {}

            def chunk_small(st_t, cidx, d, want, b=0):
                sm, rsm = st_t["sm"], st_t["rsm"]
                cdt = st_t["cdt"]
                rcdt = st_t["rcdt"]
                t0 = cidx * 128
                f.dma("sp", cdt[:, 0, :], dap(CUMd, d * NH * T + t0, [(T, NH), (1, 128)]), writes=[rcdt])
                f.dma("sp", cdt[:, 1, :], dap(DTd, d * NH * T + t0, [(T, NH), (1, 128)]), writes=[rcdt])
                f.mms([(banks[b][:, 0:NH], [(cdt[:, 0, :], ident_f[0:NH, 0:NH])]),
                       (banks[b][:, NH:2 * NH], [(cdt[:, 1, :], ident_f[0:NH, 0:NH])])], reads=[rcdt, rconst], writes=[rbank[b]])
                f.op("dve", lambda e: e.tensor_copy(out=sm[:, 0:2, :], in_=banks[b][:, 0:2 * NH].rearrange("p (a h) -> p a h", h=NH)),
                     reads=[rbank[b]], writes=[rsm])
                f.op("dve", lambda e: e.tensor_scalar(out=sm[:, 2, :], in0=sm[:, 0, :], scalar1=-1.0, scalar2=None, op0=ALU.mult), reads=[rsm], writes=[rsm])
                f.op("act", lambda e: e.activation(out=sm[:, 3, :], in_=sm[:, 0, :], func=AF.Exp), reads=[rsm], writes=[rsm])
                if want == "state":
                    selm = tri_f[:, 2 + d, :]
                    f.mm(banks[b][:, 2 * NH:3 * NH], [(selm, sm[:, 0, :])], reads=[rsm, rconst], writes=[rbank[b]])
                    f.op("dve", lambda e: e.tensor_copy(out=sm[:, 4, :], in_=banks[b][:, 2 * NH:3 * NH]), reads=[rbank[b]], writes=[rsm])
                    f.op("dve", lambda e: e.tensor_tensor(out=sm[:, 5, :], in0=sm[:, 4, :], in1=sm[:, 0, :], op=ALU.subtract), reads=[rsm], writes=[rsm])
                    f.op("act", lambda e: e.activation(out=sm[:, 5, :], in_=sm[:, 5, :], func=AF.Exp), reads=[rsm], writes=[rsm])
                    f.op("act", lambda e: e.activation(out=sm[:, 6, :], in_=sm[:, 4, :], func=AF.Exp), reads=[rsm], writes=[rsm])
                    f.op("dve", lambda e: e.tensor_tensor(out=sm[:, 7, :], in0=sm[:, 5, :], in1=sm[:, 1, :], op=ALU.mult), reads=[rsm], writes=[rsm])

            with contextlib.ExitStack() as st:
                tls = [{"sm": f.sb(st, [128, 8, NH], F32, "sm"), "rsm": Res(), "cdt": f.sb(st, [NH, 2, 128], F32, "cdt"), "rcdt": Res()} for _ in range(2)]
                xsD = [f.sb(st, [128, DI], BF16, "xs") for _ in range(2)]
                rxsD = [Res(), Res()]
                btD = [f.sb(st, [128, G * 128], BF16, "bt") for _ in range(2)]
                rbtD = [Res(), Res()]
                wxD = [f.sb(st, [128, DI], BF16, "wx") for _ in range(2)]
                rwxD = [Res(), Res()]
                H = [f.sb(st, [128, DI], F32, "H") for _ in range(2)]
                rH = [Res(), Res()]
                HbD = [f.sb(st, [128, DI], BF16, "Hb") for _ in range(2)]
                rHbD = [Res(), Res()]
                orders = [list(range(NCH)), [1, 0] + list(range(NCH - 1, 1, -1))]
                for d in range(2):
                    f.op("dve", lambda e: e.memset(H[d][:], 0.0), writes=[rH[d]])
                for step_i in range(NCH):
                    for d in range(2):
                        tl = tls[d]
                        sm, rsm = tl["sm"], tl["rsm"]
                        xs, rxs, bt, rbt, wx, rwx, Hb, rHb = xsD[d], rxsD[d], btD[d], rbtD[d], wxD[d], rwxD[d], HbD[d], rHbD[d]
                        cidx = orders[d][step_i]
                        t0 = cidx * 128
                        f.op("act", lambda e: e.copy(out=Hb[:], in_=H[d][:]), reads=[rH[d]], writes=[rHb])
                        f.dma("pool", Hd[d, cidx, :, :], Hb[:], reads=[rHb])
                        if step_i == NCH - 1:
                            continue
                        chunk_small(tl, cidx, d, "state", b=d)
                        f.dma("sp", xs[:], XStok[t0:t0 + 128, :], writes=[rxs])
                        f.dma("sp", bt[:], Btok[t0:t0 + 128, :], writes=[rbt])
                        f.op("dve", lambda e: e.tensor_tensor(out=wx[:].rearrange("p (h q) -> p h q", q=64), in0=xs[:].rearrange("p (h q) -> p h q", q=64),
                                                              in1=bc(sm[:, 7, :], [(1, NH), (0, 64)]), op=ALU.mult), reads=[rxs, rsm], writes=[rwx])
                        ngb = max(1, 512 // EP)
                        for g0 in range(0, G, ngb):
                            b = 2 + ((g0 // ngb) + 3 * d) % 6
                            groups = [(banks[b][:, (g - g0) * EP:(g - g0 + 1) * EP], [(bt[:, g * 128:(g + 1) * 128], wx[:, g * EP:(g + 1) * EP])])
                                      for g in range(g0, g0 + ngb)]
                            f.mms(groups, reads=[rbt, rwx], writes=[rbank[b]])
                            wdt = ngb * EP
                            hv = H[d][:, g0 * EP:g0 * EP + wdt]
                            h0 = g0 * E
                            nh = ngb * E
                            f.op("dve", lambda e: e.tensor_tensor(out=hv.rearrange("p (h q) -> p h q", q=64), in0=hv.rearrange("p (h q) -> p h q", q=64),
                                                                  in1=bc(sm[:, 6, h0:h0 + nh], [(1, nh), (0, 64)]), op=ALU.mult), reads=[rsm], writes=[rH[d]])
                            f.op("dve", lambda e: e.tensor_tensor(out=hv, in0=hv, in1=banks[b][:, 0:wdt], op=ALU.add), reads=[rbank[b]], writes=[rH[d]])
                f.barrier()

            with contextlib.ExitStack() as st:
                tls = [{"sm": f.sb(st, [128, 8, NH], F32, "sm"), "rsm": Res(), "cdt": f.sb(st, [NH, 2, 128], F32, "cdt"), "rcdt": Res()} for _ in range(2)]
                xsP = [f.sb(st, [128, DI], BF16, "xs") for _ in range(2)]
                rxsP = [Res(), Res()]
                zzP = [f.sb(st, [128, DI], BF16, "zz") for _ in range(2)]
                rzzP = [Res(), Res()]
                bctP = [f.sb(st, [128, 2 * G, 128], BF16, "bct") for _ in range(2)]
                rbctP = [Res(), Res()]
                GmP = [f.sb(st, [128, 2, G, 128], F32, "Gm") for _ in range(2)]
                rGmP = [Res(), Res()]
                sz = f.sb(st, [128, DI], F32, "sz")
                rsz = Res()
                HsD = [f.sb(st, [128, DI], BF16, "Hs") for _ in range(2)]
                rHsD = [Res(), Res()]
                HPC = min(16, NH)
                cumb = [f.sb(st, [128, HPC, 128], F32, "cumb") for _ in range(2)]
                rcumb = [Res(), Res()]
                xdtD = [f.sb(st, [128, DI], BF16, "xdt") for _ in range(2)]
                rxdtD = [Res(), Res()]
                yacc = f.sb(st, [128, DI], F32, "yacc")
                ryacc = Res()
                Eb = [f.sb(st, [128, E, 128], F32, "Eb") for _ in range(3)]
                rEb = [Res(), Res(), Res()]
                Mb = [f.sb(st, [128, E, 128], BF16, "Mb") for _ in range(3)]
                rMb = [Res(), Res(), Res()]
                tmpyP = [f.sb(st, [128, EP], F32, "tmpy") for _ in range(3)]
                rtmpyP = [Res(), Res(), Res()]
                PAIRS = [(2, 3), (4, 5), (6, 7)]
                cs_cur = 0
                ssq = f.sb(st, [128, 2, G], F32, "ssq")
                rssq = Res()
                gn = f.sb(st, [128, DI], BF16, "gn")
                rgn = Res()
                gT = f.sb(st, [128, KI, 128], BF16, "gT")
                rgT = Res()
                junk = f.sb(st, [128, DI // G], F32, "junk")
                rjunk = Res()
                gi = 0
                ci = 0
                pc = 0
                for cidx in range(NCH):
                    t0 = cidx * 128
                    if t0 < Lc and not with_ctx:
                        continue
                    par = ci % 2
                    ci += 1
                    xs, rxs, zz, rzz, bct, rbct, Gm, rGm = xsP[par], rxsP[par], zzP[par], rzzP[par], bctP[par], rbctP[par], GmP[par], rGmP[par]
                    f.dma("sp", xs[:], XStok[t0:t0 + 128, :], writes=[rxs])
                    f.dma("sp", zz[:], Ztok[t0:t0 + 128, :], writes=[rzz])
                    f.dma("sp", bct[:], dap(BCT, t0, [(T, 128), (128 * T, 2 * G), (1, 128)]), writes=[rbct])
                    for d in range(2):
                        f.dma("sp", HsD[d][:], Hd[d, cidx, :, :], writes=[rHsD[d]])
                    groups = [(banks[0][:, g * 128:(g + 1) * 128] if g < 4 else banks[1][:, (g - 4) * 128:(g - 3) * 128],
                               [(bct[:, g, :], bct[:, G + g, :])]) for g in range(G)]
                    f.mms(groups, reads=[rbct], writes=[rbank[0], rbank[1]])
                    for d in range(2):
                        for hb in range(2):
                            f.op("dve", lambda e: e.tensor_tensor(out=Gm[:, d, hb * 4:(hb + 1) * 4, :], in0=banks[hb][:, :].rearrange("p (g l) -> p g l", l=128),
                                                                  in1=bc(tri_f[:, d, :], [(0, 4), (1, 128)]), op=ALU.mult),
                                 reads=[rbank[hb], rconst], writes=[rGm], nowaw=True)
                    for d in range(2):
                        chunk_small(tls[d], cidx, d, "out", b=d)
                        smd = tls[d]["sm"]
                        f.op("dve", lambda e: e.tensor_tensor(out=xdtD[d][:].rearrange("p (h q) -> p h q", q=64), in0=xs[:].rearrange("p (h q) -> p h q", q=64),
                                                              in1=bc(smd[:, 1, :], [(1, NH), (0, 64)]), op=ALU.mult), reads=[rxs, tls[d]["rsm"]], writes=[rxdtD[d]])
                    def taskA(d, g, s):
                        nonlocal pc, cs_cur
                        sm, rsm = tls[d]["sm"], tls[d]["rsm"]
                        xdt, rxdt, Hs, rHs = xdtD[d], rxdtD[d], HsD[d], rHsD[d]
                        h0 = g * E
                        if h0 % HPC == 0:
                            cs_cur = pc % 2
                            pc += 1
                            f.dma("sp", cumb[cs_cur][:], dap(CUMd, (d * NH + h0) * T + t0, [(0, 128), (T, HPC), (1, 128)]), writes=[rcumb[cs_cur]])
                        cs_ = cs_cur
                        for e_ in range(E):
                            h = h0 + e_
                            f.op("act", lambda e: e.activation(out=Eb[s][:, e_, :], in_=cumb[cs_][:, h % HPC, :], func=AF.Exp,
                                                               bias=sm[:, 2, h:h + 1], scale=1.0), reads=[rcumb[cs_], rsm], writes=[rEb[s]], nowaw=True)
                        f.op("dve", lambda e: e.scalar_tensor_tensor(out=Mb[s][:], in0=Eb[s][:], scalar=1.0, in1=bc(Gm[:, d, g, :], [(0, E), (1, 128)]),
                                                                     op0=ALU.min, op1=ALU.mult), reads=[rEb[s], rGm], writes=[rMb[s]])
                        bi_, bx_ = PAIRS[s]
                        groups = [(banks[bi_][:, e_ * 64:(e_ + 1) * 64], [(Mb[s][:, e_, :], xdt[:, (h0 + e_) * 64:(h0 + e_ + 1) * 64])]) for e_ in range(E)]
                        f.mms(groups, reads=[rMb[s], rxdt], writes=[rbank[bi_]])
                        f.mm(banks[bx_][:, 0:EP], [(bct[:, G + g, :], Hs[:, g * EP:(g + 1) * EP])], reads=[rbct, rHs], writes=[rbank[bx_]])

                    def taskB(d, g, s):
                        sm, rsm = tls[d]["sm"], tls[d]["rsm"]
                        h0 = g * E
                        bi_, bx_ = PAIRS[s]
                        tmpy, rtmpy = tmpyP[s], rtmpyP[s]
                        yv = yacc[:, g * EP:(g + 1) * EP]
                        f.op("dve", lambda e: e.tensor_tensor(out=tmpy[:].rearrange("p (h q) -> p h q", q=64), in0=banks[bx_][:, 0:EP].rearrange("p (h q) -> p h q", q=64),
                                                              in1=bc(sm[:, 3, h0:h0 + E], [(1, E), (0, 64)]), op=ALU.mult), reads=[rbank[bx_], rsm], writes=[rtmpy])
                        if d == 0:
                            f.op("dve", lambda e: e.tensor_tensor(out=yv, in0=tmpy[:], in1=banks[bi_][:, 0:EP], op=ALU.add), reads=[rtmpy, rbank[bi_]], writes=[ryacc], nowaw=True)
                        else:
                            f.op("dve", lambda e: e.tensor_tensor(out=tmpy[:], in0=tmpy[:], in1=banks[bi_][:, 0:EP], op=ALU.add), reads=[rtmpy, rbank[bi_]], writes=[rtmpy])
                            f.op("dve", lambda e: e.tensor_tensor(out=yv, in0=yv, in1=tmpy[:], op=ALU.add), reads=[rtmpy], writes=[ryacc])

                    pend = None
                    for d in range(2):
                        for g in range(G):
                            s = gi % 3
                            gi += 1
                            taskA(d, g, s)
                            if pend is not None:
                                taskB(*pend)
                            pend = (d, g, s)
                    taskB(*pend)
                    f.op("dve", lambda e: e.tensor_tensor(out=sz[:].rearrange("p (h q) -> p h q", q=64), in0=xs[:].rearrange("p (h q) -> p h q", q=64),
                                                          in1=bc(skb[:, 2, :], [(1, NH), (0, 64)]), op=ALU.mult), reads=[rxs, rskb], writes=[rsz])
                    f.op("dve", lambda e: e.tensor_tensor(out=yacc[:], in0=yacc[:], in1=sz[:], op=ALU.add), reads=[rsz], writes=[ryacc])
                    f.op("act", lambda e: e.activation(out=sz[:], in_=zz[:], func=AF.Silu), reads=[rzz], writes=[rsz])
                    f.op("dve", lambda e: e.tensor_tensor(out=yacc[:], in0=yacc[:], in1=sz[:], op=ALU.mult), reads=[rsz], writes=[ryacc])
                    gw = DI // G
                    f.op("dve", lambda e: e.memset(ssq[:], 0.0), writes=[rssq])
                    for g in range(G):
                        f.op("act", lambda e: e.activation(out=junk[:], in_=yacc[:, g * gw:(g + 1) * gw], func=AF.Square, accum_out=ssq[:, 0, g:g + 1]),
                             reads=[ryacc], writes=[rjunk, rssq])
                    f.op("act", lambda e: e.activation(out=ssq[:, 1, :], in_=ssq[:, 0, :], func=AF.Sqrt, bias=1e-5, scale=1.0 / gw), reads=[rssq], writes=[rssq])
                    f.op("dve", lambda e: e.reciprocal(out=ssq[:, 1, :], in_=ssq[:, 1, :]), reads=[rssq], writes=[rssq])
                    f.op("dve", lambda e: e.tensor_tensor(out=gn[:].rearrange("p (g q) -> p g q", q=gw), in0=yacc[:].rearrange("p (g q) -> p g q", q=gw),
                                                          in1=bc(ssq[:, 1, :], [(1, G), (0, gw)]), op=ALU.mult), reads=[ryacc, rssq], writes=[rgn])
                    for k0 in range(0, KI, 4):
                        b = (k0 // 4) % 2
                        pb = banks[b][:].bitcast(BF16)
                        f.trs([(pb[:, (k - k0) * 128:(k - k0 + 1) * 128], gn[:, k * 128:(k + 1) * 128]) for k in range(k0, k0 + 4)],
                              ident_b[:], reads=[rgn, rconst], writes=[rbank[b]])
                        f.op("dve", lambda e: e.tensor_tensor(out=gT[:, k0:k0 + 4, :], in0=pb[:, 0:512].rearrange("p (k t) -> p k t", t=128),
                                                              in1=bc(snT[:, k0:k0 + 4, 0], [(1, 4), (0, 128)]), op=ALU.mult), reads=[rbank[b], rconst], writes=[rgT], nowaw=True)
                    f.dma("pool", dap(S_big, t0, [(T, 128), (128 * T, KI), (1, 128)]), gT[:], reads=[rgT])
                f.barrier()
        proj_residual(i, 2, f"outp{j}", KI, S_big, skip_ctx=not with_ctx)

    for i in range(depth):
        kind, j = i % 3, i // 3
        last = (i == depth - 1)
        if i + 1 < depth:
            prep_layer_weights(i + 1)
        if kind == 0:
            ssd_phase(i, j, not last)
        elif kind == 1:
            fourier_phase(i, j, not last)
        else:
            pool_phase(i, j, not last)
        ffn_phase(i, skip_ctx=last)
    final_phase()
    f.barrier()
    f.es.close()
    return nc


def host_consts(cfg):
    c = cfg
    D, L, Lc, GD = c.D, c.L, c.Lc, c.GD
    quarter = D // 4
    omega = (1.0 / (np.float32(10000.0) ** (np.arange(quarter, dtype=np.float32) / np.float32(quarter)))).astype(np.float32)
    kk = np.arange(64, dtype=np.float32)[:, None]
    Tt = np.concatenate([np.sin(kk * omega), np.cos(kk * omega)], axis=-1).astype(np.float32)
    posT = np.ascontiguousarray(Tt.T)
    s = np.arange(128)[:, None]
    l = np.arange(128)[None, :]
    tri = np.zeros((128, 4, 128), np.float32)
    tri[:, 0, :] = (s <= l)
    tri[:, 1, :] = (s >= l)
    tri[127, 2, :] = 1.0
    tri[0, 3, :] = 1.0

    def dft(n):
        idx = np.arange(n, dtype=np.int64)
        ang = 2.0 * np.pi * ((idx[:, None] * idx[None, :]) % n).astype(np.float64) / n
        return np.cos(ang), np.sin(ang)
    cc, sc = dft(GD)
    cL, sL = dft(L)
    cLc, sLc = dft(Lc)
    bf = ml_dtypes.bfloat16
    return {
        "posT": posT, "tri": tri, "ident": np.eye(128, dtype=np.float32),
        "dftc": np.stack([cc, sc]).astype(np.float32).astype(bf),
        "dftL": np.stack([cL, -sL]).astype(np.float32).astype(bf),
        "dftLc": np.stack([cLc, -sLc]).astype(np.float32).astype(bf),
    }


def make_in_maps(cfg, inp):
    c = cfg
    D, NH = c.D, c.NH
    hc = host_consts(c)
    f32 = np.float32
    A = {k: np.asarray(v) for k, v in inp.items()}

    pw = A["pool_w"]
    pw2 = pw.reshape(pw.shape[0], 4 * c.GD, c.GD) if pw.shape[0] > 0 else np.zeros((1, 4 * c.GD, c.GD), f32)
    fw = A["fourier_w_out"] if A["fourier_w_out"].shape[0] > 0 else np.zeros((1, D, D), f32)
    ps = A["pool_scale"] if A["pool_scale"].shape[0] > 0 else np.zeros((1, D), f32)
    shared = {
        "ident": hc["ident"], "posT": hc["posT"], "tri": hc["tri"],
        "wmod": A["w_mod"], "bmod": A["b_mod"], "normw": A["norm_w"].reshape(c.depth * 2, D), "fnw": A["final_norm_w"].reshape(1, D),
        "ffg": A["ffn_w_gate"], "ffu": A["ffn_w_up"], "ffd": A["ffn_w_down"],
        "inproj": A["ssd_in_proj"], "convw": A["ssd_conv_w"], "convb": A["ssd_conv_b"][:, None, :],
        "dtb": A["ssd_dt_bias"].reshape(c.NSSD, 2 * NH, 1), "alog": A["ssd_a_log"].reshape(c.NSSD, 2 * NH, 1),
        "dskip": A["ssd_d"], "ssdnw": A["ssd_norm_w"][:, None, :], "outproj": A["ssd_out_proj"],
        "fouw": fw, "poolw": pw2, "pscale": ps[:, None, :],
        "dftc": hc["dftc"], "dftL": hc["dftL"], "dftLc": hc["dftLc"],
    }
    shared = {k: np.ascontiguousarray(v) for k, v in shared.items()}
    maps = []
    for r in range(NCORES):
        m = dict(shared)
        m["x"] = np.ascontiguousarray(A["x"][r])
        m["ctx"] = np.ascontiguousarray(A["ctx"][r])
        m["c2"] = np.ascontiguousarray(np.stack([A["c"][r], A["c_ctx"]], axis=0).astype(f32))
        maps.append(m)
    return maps


_NC_CACHE = {}


def run(cfg, inp):
    key = (cfg.D, cfg.L, cfg.Lc, cfg.depth)
    if key not in _NC_CACHE:
        _NC_CACHE[key] = build(cfg)
    nc = _NC_CACHE[key]
    maps = make_in_maps(cfg, inp)
    res = run_bass_kernel_spmd(nc, maps, core_ids=list(range(NCORES)))
    return np.stack([res.results[r]["out"] for r in range(NCORES)], axis=0)


def kernel(**inputs):
    cfg = Cfg(2048, 4096, 256, 4)
    return run(cfg, inputs).astype(np.float32)
```
